# Optimizing a Trainium2 kernel written in Bass

```python
import jax
import jax.numpy as jnp
from jax import lax
import numpy as np

D_MODEL = 2048
BATCH = 2
SEQ = 8192
DEPTH = 2

GRID_W = 64
CTX_LEN = 256
HEAD_DIM = 128
ROPE_BASE = 10000.0
EPS = 1e-6
Q_BLOCK = 128
NEG = -1e30

A_HEADS = 8
A_KV_HEADS = 2
B_HEADS = 8
B_Q_RANK = 512
B_KV_RANK = 512
B_NOPE = 128
B_ROPE = 64
B_V = 128
C_HEADS = 8
C_KV_HEADS = 2
C_WINDOW = 128
D_HEADS = 8
NA_KH = 8
NA_KW = 16

A_IN = (A_HEADS + 2 * A_KV_HEADS) * HEAD_DIM
B_IN = B_Q_RANK + B_KV_RANK + B_ROPE
AB_IN = A_IN + B_IN
AB_MIX = A_HEADS * HEAD_DIM + B_HEADS * B_V
C_IN = (C_HEADS + 2 * C_KV_HEADS) * HEAD_DIM
D_IN = 3 * D_HEADS * HEAD_DIM
CD_IN = C_IN + D_IN
CD_MIX = (C_HEADS + D_HEADS) * HEAD_DIM
D_FF = -(-8 * D_MODEL // (3 * 256)) * 256
N_AB = (DEPTH + 1) // 2
N_CD = DEPTH // 2

kernel_name = 'hybrid_dit_prefix_block'


def rmsnorm(x, g):
    xf = x.astype(jnp.float32)
    xf = xf * lax.rsqrt(jnp.mean(xf * xf, axis=-1, keepdims=True) + EPS)
    return (xf * g.astype(jnp.float32)).astype(x.dtype)


def modulate(n, shift, scale):
    return n * (1 + scale) + shift


def swiglu(u, wg, wu, wd):
    return (jax.nn.silu(u @ wg) * (u @ wu)) @ wd


def rope_2d(n_tok, dim, dtype):
    t = jnp.arange(n_tok, dtype=jnp.int32)
    row = (t // GRID_W).astype(jnp.float32)
    col = (t % GRID_W).astype(jnp.float32)
    half = dim // 2
    inv_freq = ROPE_BASE ** (-jnp.arange(0, half, 2, dtype=jnp.float32) / half)
    ar = row[:, None] * inv_freq[None, :]
    ac = col[:, None] * inv_freq[None, :]
    ang = jnp.concatenate([ar, ar, ac, ac], axis=-1)
    return jnp.cos(ang).astype(dtype), jnp.sin(ang).astype(dtype)


def apply_rope(x, cos, sin):
    x1, x2, x3, x4 = jnp.split(x, 4, axis=-1)
    rot = jnp.concatenate([-x2, x1, -x4, x3], axis=-1)
    return x * cos[:, None, :] + rot * sin[:, None, :]


def softmax_attend(q, k, v, scale):
    s = jnp.einsum('bqgrd,bkgd->bgrqk', q, k, preferred_element_type=jnp.float32) * scale
    p = jax.nn.softmax(s, axis=-1).astype(v.dtype)
    return jnp.einsum('bgrqk,bkgd->bqgrd', p, v)


def dense_blocked(q, k, v, scale):
    B, L = q.shape[:2]
    nb = L // Q_BLOCK
    qb = jnp.moveaxis(q.reshape((B, nb, Q_BLOCK) + q.shape[2:]), 1, 0)
    ob = lax.map(lambda qi: softmax_attend(qi, k, v, scale), qb)
    return jnp.moveaxis(ob, 0, 1).reshape((B, L) + ob.shape[3:])


def qkv_heads(p, n_q, n_kv, qn, kn):
    B, L = p.shape[:2]
    q, k, v = jnp.split(p, [n_q * HEAD_DIM, (n_q + n_kv) * HEAD_DIM], axis=-1)
    q = rmsnorm(q.reshape(B, L, n_q, HEAD_DIM), qn)
    k = rmsnorm(k.reshape(B, L, n_kv, HEAD_DIM), kn)
    return q, k, v.reshape(B, L, n_kv, HEAD_DIM)


def mixer_a(p_lat, p_ctx, qn, kn, cos, sin, need_ctx):
    B, L = p_lat.shape[:2]
    n_ctx = p_ctx.shape[1]
    R = A_HEADS // A_KV_HEADS
    scale = HEAD_DIM ** -0.5
    qc, kc, vc = qkv_heads(p_ctx, A_HEADS, A_KV_HEADS, qn, kn)
    ql, kl, vl = qkv_heads(p_lat, A_HEADS, A_KV_HEADS, qn, kn)
    ql = apply_rope(ql, cos, sin)
    kl = apply_rope(kl, cos, sin)
    k_all = jnp.concatenate([kc, kl], axis=1)
    v_all = jnp.concatenate([vc, vl], axis=1)
    o_lat = dense_blocked(ql.reshape(B, L, A_KV_HEADS, R, HEAD_DIM), k_all, v_all, scale).reshape(B, L, A_HEADS * HEAD_DIM)
    o_ctx = None
    if need_ctx:
        o_ctx = softmax_attend(qc.reshape(B, n_ctx, A_KV_HEADS, R, HEAD_DIM), kc, vc, scale).reshape(B, n_ctx, A_HEADS * HEAD_DIM)
    return o_lat, o_ctx


def mla_heads(p, qa_n, kva_n, w_uq, w_ukv, rope):
    B, L = p.shape[:2]
    cq, ckv, kr = jnp.split(p, [B_Q_RANK, B_Q_RANK + B_KV_RANK], axis=-1)
    q = (rmsnorm(cq, qa_n) @ w_uq).reshape(B, L, B_HEADS, B_NOPE + B_ROPE)
    kv = (rmsnorm(ckv, kva_n) @ w_ukv).reshape(B, L, B_HEADS, B_NOPE + B_V)
    q_nope, q_rope = jnp.split(q, [B_NOPE], axis=-1)
    k_nope, v = jnp.split(kv, [B_NOPE], axis=-1)
    kr = kr[:, :, None, :]
    if rope is not None:
        q_rope = apply_rope(q_rope, rope[0], rope[1])
        kr = apply_rope(kr, rope[0], rope[1])
    q = jnp.concatenate([q_nope, q_rope], axis=-1)[:, :, :, None, :]
    k = jnp.concatenate([k_nope, jnp.broadcast_to(kr, (B, L, B_HEADS, B_ROPE))], axis=-1)
    return q, k, v


def mixer_b(p_lat, p_ctx, qa_n, kva_n, w_uq, w_ukv, cos, sin, need_ctx):
    B, L = p_lat.shape[:2]
    n_ctx = p_ctx.shape[1]
    scale = (B_NOPE + B_ROPE) ** -0.5
    qc, kc, vc = mla_heads(p_ctx, qa_n, kva_n, w_uq, w_ukv, None)
    ql, kl, vl = mla_heads(p_lat, qa_n, kva_n, w_uq, w_ukv, (cos, sin))
    k_all = jnp.concatenate([kc, kl], axis=1)
    v_all = jnp.concatenate([vc, vl], axis=1)
    o_lat = dense_blocked(ql, k_all, v_all, scale).reshape(B, L, B_HEADS * B_V)
    o_ctx = None
    if need_ctx:
        o_ctx = softmax_attend(qc, kc, vc, scale).reshape(B, n_ctx, B_HEADS * B_V)
    return o_lat, o_ctx


def mixer_c(p_lat, p_ctx, qn, kn, sink, cos, sin, need_ctx):
    B, L = p_lat.shape[:2]
    n_ctx = p_ctx.shape[1]
    G, R = C_KV_HEADS, C_HEADS // C_KV_HEADS
    nb = L // Q_BLOCK
    scale = HEAD_DIM ** -0.5
    sink_gr = sink.reshape(G, R).astype(jnp.float32)
    qc, kc, vc = qkv_heads(p_ctx, C_HEADS, C_KV_HEADS, qn, kn)
    ql, kl, vl = qkv_heads(p_lat, C_HEADS, C_KV_HEADS, qn, kn)
    ql = apply_rope(ql, cos, sin)
    kl = apply_rope(kl, cos, sin)
    qb = ql.reshape(B, nb, Q_BLOCK, G, R, HEAD_DIM)

    def band(t):
        tp = jnp.pad(t, ((0, 0), (Q_BLOCK, Q_BLOCK), (0, 0), (0, 0)))
        tb = tp.reshape(B, nb + 2, Q_BLOCK, G, HEAD_DIM)
        return jnp.concatenate([tb[:, :-2], tb[:, 1:-1], tb[:, 2:]], axis=2)

    kb, vb = band(kl), band(vl)
    n_idx = jnp.arange(nb)[:, None, None]
    qpos = n_idx * Q_BLOCK + jnp.arange(Q_BLOCK)[None, :, None]
    kpos = (n_idx - 1) * Q_BLOCK + jnp.arange(3 * Q_BLOCK)[None, None, :]
    valid = (jnp.abs(kpos - qpos) <= C_WINDOW) & (kpos >= 0) & (kpos < L)
    s_win = jnp.einsum('bnqgrd,bnkgd->bngrqk', qb, kb, preferred_element_type=jnp.float32) * scale
    s_win = jnp.where(valid[None, :, None, None], s_win, NEG)
    s_ctx = jnp.einsum('bnqgrd,bkgd->bngrqk', qb, kc, preferred_element_type=jnp.float32) * scale
    s_sink = jnp.broadcast_to(sink_gr[None, None, :, :, None, None], s_ctx.shape[:-1] + (1,))
    p = jax.nn.softmax(jnp.concatenate([s_ctx, s_win, s_sink], axis=-1), axis=-1).astype(vb.dtype)
    o = (jnp.einsum('bngrqk,bkgd->bnqgrd', p[..., :n_ctx], vc)
         + jnp.einsum('bngrqk,bnkgd->bnqgrd', p[..., n_ctx:n_ctx + 3 * Q_BLOCK], vb))
    o_lat = o.reshape(B, L, C_HEADS * HEAD_DIM)
    o_ctx = None
    if need_ctx:
        qcg = qc.reshape(B, n_ctx, G, R, HEAD_DIM)
        s = jnp.einsum('bqgrd,bkgd->bgrqk', qcg, kc, preferred_element_type=jnp.float32) * scale
        s_sink_c = jnp.broadcast_to(sink_gr[None, :, :, None, None], s.shape[:-1] + (1,))
        pc = jax.nn.softmax(jnp.concatenate([s, s_sink_c], axis=-1), axis=-1)[..., :n_ctx].astype(vc.dtype)
        o_ctx = jnp.einsum('bgrqk,bkgd->bqgrd', pc, vc).reshape(B, n_ctx, C_HEADS * HEAD_DIM)
    return o_lat, o_ctx


def mixer_d(p_lat, p_ctx, qn, kn, rpb, need_ctx):
    B, L = p_lat.shape[:2]
    n_ctx = p_ctx.shape[1]
    rows = L // GRID_W
    kh = min(NA_KH, rows)
    scale = HEAD_DIM ** -0.5
    qc, kc, vc = qkv_heads(p_ctx, D_HEADS, D_HEADS, qn, kn)
    ql, kl, vl = qkv_heads(p_lat, D_HEADS, D_HEADS, qn, kn)
    grid = (B, rows, GRID_W, D_HEADS, HEAD_DIM)
    qg, kg, vg = ql.reshape(grid), kl.reshape(grid), vl.reshape(grid)
    cols = jnp.arange(GRID_W)
    col_start = jnp.clip(cols - NA_KW // 2, 0, GRID_W - NA_KW)
    col_idx = col_start[:, None] + jnp.arange(NA_KW)[None, :]
    bias_cols = rpb[:, :, col_idx - cols[:, None] + NA_KW - 1].astype(jnp.float32)

    def row_block(args):
        qr, r = args
        rs = jnp.clip(r - kh // 2, 0, rows - kh)
        kq = lax.dynamic_slice_in_dim(kg, rs, kh, axis=1)[:, :, col_idx]
        vq = lax.dynamic_slice_in_dim(vg, rs, kh, axis=1)[:, :, col_idx]
        s_nb = jnp.einsum('bchd,bicjhd->bhcij', qr, kq, preferred_element_type=jnp.float32) * scale
        di = rs + jnp.arange(kh) - r
        s_nb = s_nb + jnp.transpose(bias_cols[:, di + NA_KH - 1], (0, 2, 1, 3))[None]
        s_ctx = jnp.einsum('bchd,bkhd->bhck', qr, kc, preferred_element_type=jnp.float32) * scale
        s = jnp.concatenate([s_ctx, s_nb.reshape(B, D_HEADS, GRID_W, kh * NA_KW)], axis=-1)
        p = jax.nn.softmax(s, axis=-1).astype(vq.dtype)
        p_nb = p[..., n_ctx:].reshape(B, D_HEADS, GRID_W, kh, NA_KW)
        return (jnp.einsum('bhck,bkhd->bchd', p[..., :n_ctx], vc)
                + jnp.einsum('bhcij,bicjhd->bchd', p_nb, vq))

    o = lax.map(row_block, (jnp.moveaxis(qg, 1, 0), jnp.arange(rows)))
    o_lat = jnp.moveaxis(o, 0, 1).reshape(B, L, D_HEADS * HEAD_DIM)
    o_ctx = None
    if need_ctx:
        o_ctx = softmax_attend(qc[:, :, :, None, :], kc, vc, scale).reshape(B, n_ctx, D_HEADS * HEAD_DIM)
    return o_lat, o_ctx


def setup_inputs(seed: int = 0) -> dict:
    key = jax.random.key(seed)
    ks = iter(jax.random.split(key, 32))
    f32 = jnp.float32

    def nrm(shape, s=1.0):
        return jax.random.normal(next(ks), shape, f32) * s

    def gain(shape):
        return 1.0 + nrm(shape, 0.02)

    D = D_MODEL
    return {
        'x': nrm((BATCH, SEQ, D)),
        'c': nrm((BATCH, D)),
        'ctx': nrm((BATCH, CTX_LEN, D)),
        'c_ctx': nrm((D,)),
        'w_mod': nrm((DEPTH, D, 6 * D), D ** -0.5),
        'b_mod': nrm((DEPTH, 6 * D), 0.02),
        'g_mix_pre': gain((DEPTH, D)),
        'g_mix_post': gain((DEPTH, D)),
        'g_ffn_pre': gain((DEPTH, D)),
        'g_ffn_post': gain((DEPTH, D)),
        'w_gate': nrm((DEPTH, D, D_FF), D ** -0.5),
        'w_up': nrm((DEPTH, D, D_FF), D ** -0.5),
        'w_down': nrm((DEPTH, D_FF, D), D_FF ** -0.5),
        'ab_w_in': nrm((N_AB, D, AB_IN), D ** -0.5),
        'ab_w_out': nrm((N_AB, AB_MIX, D), AB_MIX ** -0.5),
        'a_q_norm': gain((N_AB, HEAD_DIM)),
        'a_k_norm': gain((N_AB, HEAD_DIM)),
        'b_q_norm': gain((N_AB, B_Q_RANK)),
        'b_kv_norm': gain((N_AB, B_KV_RANK)),
        'b_w_uq': nrm((N_AB, B_Q_RANK, B_HEADS * (B_NOPE + B_ROPE)), B_Q_RANK ** -0.5),
        'b_w_ukv': nrm((N_AB, B_KV_RANK, B_HEADS * (B_NOPE + B_V)), B_KV_RANK ** -0.5),
        'cd_w_in': nrm((N_CD, D, CD_IN), D ** -0.5),
        'cd_w_out': nrm((N_CD, CD_MIX, D), CD_MIX ** -0.5),
        'c_q_norm': gain((N_CD, HEAD_DIM)),
        'c_k_norm': gain((N_CD, HEAD_DIM)),
        'c_sink': nrm((N_CD, C_HEADS), 0.5),
        'd_q_norm': gain((N_CD, HEAD_DIM)),
        'd_k_norm': gain((N_CD, HEAD_DIM)),
        'd_rpb': nrm((N_CD, D_HEADS, 2 * NA_KH - 1, 2 * NA_KW - 1), 0.1),
    }


def reference(x, c, ctx, c_ctx, w_mod, b_mod, g_mix_pre, g_mix_post, g_ffn_pre, g_ffn_post,
              w_gate, w_up, w_down, ab_w_in, ab_w_out, a_q_norm, a_k_norm, b_q_norm, b_kv_norm,
              b_w_uq, b_w_ukv, cd_w_in, cd_w_out, c_q_norm, c_k_norm, c_sink, d_q_norm, d_k_norm, d_rpb):
    L = x.shape[1]
    cos_h, sin_h = rope_2d(L, HEAD_DIM, x.dtype)
    cos_r, sin_r = rope_2d(L, B_ROPE, x.dtype)
    h_lat, h_ctx = x, ctx
    for i in range(DEPTH):
        last = i == DEPTH - 1
        need_ctx = not last
        mod_lat = (jax.nn.silu(c) @ w_mod[i] + b_mod[i])[:, None, :]
        mod_ctx = (jax.nn.silu(c_ctx) @ w_mod[i] + b_mod[i])[None, None, :]
        sh1_l, sc1_l, gt1_l, sh2_l, sc2_l, gt2_l = jnp.split(mod_lat, 6, axis=-1)
        sh1_c, sc1_c, gt1_c, sh2_c, sc2_c, gt2_c = jnp.split(mod_ctx, 6, axis=-1)

        u_lat = modulate(rmsnorm(h_lat, g_mix_pre[i]), sh1_l, sc1_l)
        u_ctx = modulate(rmsnorm(h_ctx, g_mix_pre[i]), sh1_c, sc1_c)
        if i % 2 == 0:
            j = i // 2
            p_lat = u_lat @ ab_w_in[j]
            p_ctx = u_ctx @ ab_w_in[j]
            oa_l, oa_c = mixer_a(p_lat[..., :A_IN], p_ctx[..., :A_IN], a_q_norm[j], a_k_norm[j], cos_h, sin_h, need_ctx)
            ob_l, ob_c = mixer_b(p_lat[..., A_IN:], p_ctx[..., A_IN:], b_q_norm[j], b_kv_norm[j],
                                 b_w_uq[j], b_w_ukv[j], cos_r, sin_r, need_ctx)
            o_lat = jnp.concatenate([oa_l, ob_l], axis=-1) @ ab_w_out[j]
            if need_ctx:
                o_ctx = jnp.concatenate([oa_c, ob_c], axis=-1) @ ab_w_out[j]
        else:
            j = i // 2
            p_lat = u_lat @ cd_w_in[j]
            p_ctx = u_ctx @ cd_w_in[j]
            oc_l, oc_c = mixer_c(p_lat[..., :C_IN], p_ctx[..., :C_IN], c_q_norm[j], c_k_norm[j], c_sink[j],
                                 cos_h, sin_h, need_ctx)
            od_l, od_c = mixer_d(p_lat[..., C_IN:], p_ctx[..., C_IN:], d_q_norm[j], d_k_norm[j], d_rpb[j], need_ctx)
            o_lat = jnp.concatenate([oc_l, od_l], axis=-1) @ cd_w_out[j]
            if need_ctx:
                o_ctx = jnp.concatenate([oc_c, od_c], axis=-1) @ cd_w_out[j]
        h_lat = h_lat + gt1_l * rmsnorm(o_lat, g_mix_post[i])
        if need_ctx:
            h_ctx = h_ctx + gt1_c * rmsnorm(o_ctx, g_mix_post[i])

        f_lat = swiglu(modulate(rmsnorm(h_lat, g_ffn_pre[i]), sh2_l, sc2_l), w_gate[i], w_up[i], w_down[i])
        h_lat = h_lat + gt2_l * rmsnorm(f_lat, g_ffn_post[i])
        if need_ctx:
            f_ctx = swiglu(modulate(rmsnorm(h_ctx, g_ffn_pre[i]), sh2_c, sc2_c), w_gate[i], w_up[i], w_down[i])
            h_ctx = h_ctx + gt2_c * rmsnorm(f_ctx, g_ffn_post[i])
    return h_lat
```

```python
import numpy as np
from contextlib import ExitStack
import concourse.bass as bass
import concourse.mybir as mybir
from concourse.bass_utils import run_bass_kernel_spmd

F32 = mybir.dt.float32
BF16 = mybir.dt.bfloat16
AF = mybir.ActivationFunctionType
ALU = mybir.AluOpType
AX = mybir.AxisListType
ND = 12

D = 2048
KC = 16
DFF = 5632
FC = 44
EPS = 1e-6
NEG = -1e30


class Res:
    __slots__ = ("w", "rs", "name")

    def __init__(self, name=""):
        self.w = None
        self.rs = {}
        self.name = name


class T:
    def __init__(self, t, name):
        self.t = t
        self.r = Res(name)

    def __getitem__(self, k):
        return self.t[k]


def _res(x):
    return x.r if isinstance(x, T) else x


class Sch:
    def __init__(self, nc, es):
        self.nc = nc
        self.eng = {"pe": nc.tensor, "act": nc.scalar, "dve": nc.vector, "pool": nc.gpsimd, "sp": nc.sync}
        self.semobj = {}
        self.cnt = {}
        for k in ["pe", "act", "dve", "pool"]:
            self.semobj[k] = es.enter_context(nc.semaphore("s_" + k))
            self.cnt[k] = 0
        self.known = {k: {} for k in self.eng}
        self.dcnt = {}
        self.dnext = {}
        for q in ["sp", "pool"]:
            for i in range(ND):
                self.semobj[(q, i)] = es.enter_context(nc.semaphore("d_%s_%d" % (q, i)))
                self.dcnt[(q, i)] = 0
            self.dnext[q] = 0
        self.nins = 0

    def _wait(self, e, deps):
        need = {}
        kn = self.known[e]
        for d in deps:
            if d is None:
                continue
            k, v = d
            if kn.get(k, 0) >= v:
                continue
            if need.get(k, 0) < v:
                need[k] = v
        for k, v in need.items():
            self.eng[e].wait_ge(self.semobj[k], v)
            kn[k] = v
            self.nins += 1

    @staticmethod
    def _deps(reads, writes):
        deps = []
        for r in reads:
            deps.append(_res(r).w)
        for w in writes:
            w = _res(w)
            deps.append(w.w)
            deps.extend(w.rs.items())
        return deps

    @staticmethod
    def _mark(ev, reads, writes):
        k, v = ev
        for r in reads:
            r = _res(r)
            if r.rs.get(k, 0) < v:
                r.rs[k] = v
        for w in writes:
            w = _res(w)
            w.w = ev
            w.rs = {}

    def op(self, e, fn, reads=(), writes=()):
        deps = self._deps(reads, writes)
        if e == "pe":
            deps = [d for d in deps if d is not None and d[0] != "pe"]
        self._wait(e, deps)
        ins = fn(self.eng[e])
        self.cnt[e] += 1
        ins.then_inc(self.semobj[e], 1)
        self.nins += 1
        ev = (e, self.cnt[e])
        self._mark(ev, reads, writes)
        return ev

    def dma(self, q, out, in_, reads=(), writes=()):
        i = self.dnext[q]
        self.dnext[q] = (i + 1) % ND
        key = (q, i)
        deps = self._deps(reads, writes)
        if self.dcnt[key] > 0:
            deps.append((key, self.dcnt[key]))
        self._wait(q, deps)
        ins = self.eng[q].dma_start(out=out, in_=in_)
        self.dcnt[key] += 16
        ins.then_inc(self.semobj[key], 16)
        self.nins += 1
        ev = (key, self.dcnt[key])
        self._mark(ev, reads, writes)
        return ev

    def barrier(self):
        deps = [(k, v) for k, v in self.cnt.items() if v > 0]
        deps += [(k, v) for k, v in self.dcnt.items() if v > 0]
        for e in ("pe", "act", "dve", "pool", "sp"):
            self._wait(e, deps)


class Cfg:
    def __init__(self, nqt=22, nkt=66, own0=2, nown=16, dbg=False):
        self.NQT = nqt
        self.NKT = nkt
        self.OWN0 = own0
        self.NOWN = nown
        self.dbg = dbg


class Prog:
    def __init__(self, cfg):
        self.cfg = cfg
        self.nc = bass.Bass("TRN2", target_bir_lowering=False)
        self.es = ExitStack()
        self.S = None
        self.din = {}
        self.uid = 0

    def inp(self, name, shape, dt=F32):
        self.din[name] = self.nc.dram_tensor(name, list(shape), dt, kind="ExternalInput").ap()
        return self.din[name]

    def scratch(self, name, shape, dt):
        return self.nc.dram_tensor(name, list(shape), dt, kind="Internal").ap()

    def sb(self, ps, name, shape, dt):
        self.uid += 1
        return T(ps.enter_context(self.nc.sbuf_tensor("%s_%d" % (name, self.uid), list(shape), dt)), name)

    def pb(self, ps, name, shape, dt):
        self.uid += 1
        return T(ps.enter_context(self.nc.psum_tensor("%s_%d" % (name, self.uid), list(shape), dt)), name)

    def rstd_from_ss(self, ss, col, n):
        S = self.S
        a = ss[:, col]
        S.op("act", lambda e: e.activation(out=a, in_=a, func=AF.Sqrt, bias=EPS, scale=1.0 / n), reads=[ss], writes=[ss])
        S.op("dve", lambda e: e.reciprocal(a, a), reads=[ss], writes=[ss])

    def load_bcast(self, q, dst, src_ap):
        self.S.dma(q, dst[:], src_ap.partition_broadcast(128), writes=[dst])

    def front(self, ph, xt, Avec, shvec):
        S = self.S
        ss, junk, ub, uT, pT, ident = ph.ss, ph.junk, ph.ub, ph.uT, ph.pT, ph.ident
        S.op("act", lambda e: e.activation(out=junk[:], in_=xt[:], func=AF.Square, accum_out=ss[:, 0:1]), reads=[xt], writes=[junk, ss])
        self.rstd_from_ss(ss, slice(0, 1), D)
        S.op("dve", lambda e: e.scalar_tensor_tensor(xt[:], xt[:], ss[:, 0:1], Avec[:], ALU.mult, ALU.mult), reads=[xt, ss, Avec], writes=[xt])
        S.op("dve", lambda e: e.tensor_tensor(ub[:], xt[:], shvec[:], ALU.add), reads=[xt, shvec], writes=[ub])
        for half in range(2):
            p = pT[half]
            for j in range(8):
                kc = half * 8 + j
                S.op("pe", lambda e, p=p, j=j, kc=kc: e.transpose(p[:, j * 128:(j + 1) * 128], ub[:, kc * 128:(kc + 1) * 128], ident[:]), reads=[ub, ident], writes=[p])
            eng = "act" if half == 0 else "dve"
            dst = uT[:, half * 8:(half + 1) * 8, :].rearrange("p a b -> p (a b)")
            if eng == "act":
                S.op("act", lambda e, p=p, dst=dst: e.copy(dst, p[:]), reads=[p], writes=[uT])
            else:
                S.op("dve", lambda e, p=p, dst=dst: e.tensor_copy(dst, p[:]), reads=[p], writes=[uT])

    def front_alloc(self, ps, ph, share=None):
        ph.ss = self.sb(ps, "ss", [128, 16], F32)
        ph.junk = self.sb(ps, "junk", [128, D], BF16)
        ph.ub = self.sb(ps, "ub", [128, D], BF16)
        ph.uT = self.sb(ps, "uT", [128, KC, 128], BF16)
        if share is not None:
            ph.pT = share.pT
            ph.ident = share.ident
            return
        ph.pT = [self.pb(ps, "pT%d" % i, [128, 1024], BF16) for i in range(2)]
        ph.ident = self.sb(ps, "ident", [128, 128], BF16)
        self.S.dma("pool", ph.ident[:], self.din["ident"][:, :], writes=[ph.ident])

    def mod_vecs(self, ps, ph, layer, which, names):
        S = self.S
        for attr, kind, k, gname in names:
            t = self.sb(ps, attr, [128, D], F32)
            self.load_bcast("sp", t, self.modv[layer, which, k * D:(k + 1) * D])
            if kind != "S":
                g = ph.gtmp
                self.load_bcast("sp", g, self.din[gname][layer, :])
                if kind == "A":
                    S.op("dve", lambda e, t=t, g=g: e.scalar_tensor_tensor(t[:], t[:], 1.0, g[:], ALU.add, ALU.mult), reads=[t, g], writes=[t])
                else:
                    S.op("dve", lambda e, t=t, g=g: e.tensor_tensor(t[:], t[:], g[:], ALU.mult), reads=[t, g], writes=[t])
            setattr(ph, attr, t)

    def head_norm_rope(self, ph, src_ps, src_ap, H, hd, gain, cs, dstT, rope=True, norm=True):
        S = self.S
        qf = ph.qf
        n = H * hd
        v3 = lambda ap: ap.rearrange("p (h d) -> p h d", d=hd)
        S.op("act", lambda e: e.copy(qf[:, 0:n], src_ap), reads=[src_ps], writes=[qf])
        if norm:
            sq = ph.qsq
            S.op("dve", lambda e: e.tensor_tensor(sq[:, 0:n], qf[:, 0:n], qf[:, 0:n], ALU.mult), reads=[qf], writes=[sq])
            S.op("dve", lambda e: e.tensor_reduce(ph.ss[:, 2:2 + H], v3(sq[:, 0:n]), AX.X, ALU.add), reads=[sq], writes=[ph.ss])
            self.rstd_from_ss(ph.ss, slice(2, 2 + H), hd)
            S.op("dve", lambda e: e.tensor_tensor(v3(qf[:, 0:n]), v3(qf[:, 0:n]), ph.ss[:, 2:2 + H].unsqueeze(2).to_broadcast([128, H, hd]), ALU.mult), reads=[qf, ph.ss], writes=[qf])
            S.op("dve", lambda e: e.tensor_tensor(v3(qf[:, 0:n]), v3(qf[:, 0:n]), gain[:, 0:hd].unsqueeze(1).to_broadcast([128, H, hd]), ALU.mult), reads=[qf, gain], writes=[qf])
        if rope:
            csT, ci_, si_ = cs
            cosv = csT[:, ci_, :]
            sinv = csT[:, si_, :]
            q4 = hd // 4
            rot = ph.qsq
            x5 = lambda ap: ap.rearrange("p (h a b d) -> p h a b d", a=2, b=2, d=q4)
            s4 = lambda ap: ap.rearrange("p (a b d) -> p a b d", a=2, b=2, d=q4)
            for b0 in range(2):
                S.op("dve", lambda e, b0=b0: e.tensor_tensor(x5(rot[:, 0:n])[:, :, :, b0, :], x5(qf[:, 0:n])[:, :, :, 1 - b0, :],
                                                            s4(sinv[:, 0:hd])[:, :, b0, :].unsqueeze(1).to_broadcast([128, H, 2, q4]), ALU.mult), reads=[qf, csT], writes=[rot])
            S.op("dve", lambda e: e.tensor_tensor(v3(qf[:, 0:n]), v3(qf[:, 0:n]), cosv[:, 0:hd].unsqueeze(1).to_broadcast([128, H, hd]), ALU.mult), reads=[qf, csT], writes=[qf])
            S.op("dve", lambda e: e.tensor_tensor(dstT[:, 0:n], qf[:, 0:n], rot[:, 0:n], ALU.add), reads=[qf, rot], writes=[dstT])
        else:
            S.op("dve", lambda e: e.tensor_copy(dstT[:, 0:n], qf[:, 0:n]), reads=[qf], writes=[dstT])

    def transpose_out(self, ph, src, nblk, pTt, dst_sb, dram_ap, rows=128):
        S = self.S
        for j in range(nblk):
            S.op("pe", lambda e, j=j: e.transpose(pTt[:, j * 128:(j + 1) * 128], src[:, j * 128:(j + 1) * 128], ph.ident[:]), reads=[src, ph.ident], writes=[pTt])
        S.op("act", lambda e: e.copy(dst_sb[:, 0:nblk * 128], pTt[:, 0:nblk * 128]), reads=[pTt], writes=[dst_sb])
        S.dma("pool", dram_ap, dst_sb[0:rows, 0:nblk * 128].rearrange("p (h t) -> p h t", t=128), reads=[dst_sb])


def _lin(S, ps_tile, out_ap, uT, W, c0, c1, kcn=KC):
    for kc in range(kcn):
        S.op("pe", lambda e, kc=kc: e.matmul(out_ap, uT[:, kc, :], W[:, kc, c0:c1], start=(kc == 0), stop=(kc == kcn - 1)), reads=[uT, W], writes=[ps_tile])


class Ph:
    pass


def build(cfg):
    P = Prog(cfg)
    nc = P.nc
    NQT, NKT = cfg.NQT, cfg.NKT
    NQ = NQT * 128
    NK = NKT * 128
    NOWN = cfg.NOWN
    OWN0 = cfg.OWN0
    inp = P.inp
    xq = inp("xq", [NQ, D])
    xkv = inp("xkv", [NK, D])
    cT = inp("cT", [128, KC, 2])
    inp("ident", [128, 128])
    ones_in = inp("ones", [128, 128])
    w_mod = inp("w_mod", [2, D, 6 * D])
    b_mod = inp("b_mod", [2, 6 * D])
    for n in ["g_mix_pre", "g_mix_post", "g_ffn_pre", "g_ffn_post"]:
        inp(n, [2, D])
    w_gate = inp("w_gate", [2, D, DFF])
    w_up = inp("w_up", [2, D, DFF])
    w_down = inp("w_down", [2, DFF, D])
    ab_w_in = inp("ab_w_in", [D, 2624])
    ab_w_out = inp("ab_w_out", [D, D])
    a_q_norm = inp("a_q_norm", [128])
    a_k_norm = inp("a_k_norm", [128])
    b_q_norm = inp("b_q_norm", [512])
    b_kv_norm = inp("b_kv_norm", [512])
    w_uq = inp("w_uq", [512, 1536])
    w_ukv = inp("w_ukv", [512, 2048])
    cd_w_in = inp("cd_w_in", [D, 4608])
    cd_w_out = inp("cd_w_out", [D, D])
    c_q_norm = inp("c_q_norm", [128])
    c_k_norm = inp("c_k_norm", [128])
    c_sink = inp("c_sink", [8])
    d_q_norm = inp("d_q_norm", [128])
    d_k_norm = inp("d_k_norm", [128])
    ropeq = inp("ropeq", [4, NQ, 128])
    ropek = inp("ropek", [4, NK, 128])
    maskC = inp("maskC", [3, 3, 128, 128])
    biasD = inp("biasD", [5, 8, 6, 128, 128])
    yout = nc.dram_tensor("y", [NOWN * 128, D], F32, kind="ExternalOutput").ap()
    sc = P.scratch
    P.modv = sc("modv", [2, 2, 6 * D], F32)
    KAT = sc("KAT", [2, 128, NK], BF16)
    VA = sc("VA", [2, NK, 128], BF16)
    KBT = sc("KBT", [8, 128, NK], BF16)
    KRT = sc("KRT", [128, NK], BF16)
    VB = sc("VB", [8, NK, 128], BF16)
    QAT = sc("QAT", [8, 128, NQ], BF16)
    QBT = sc("QBT", [8, 128, NQ], BF16)
    QRT = sc("QRT", [4, 128, NQ], BF16)
    OT = sc("OT", [16, 128, NQ], BF16)
    hmid = sc("hmid", [NQ, D], F32)
    h1 = sc("h1", [NQ, D], F32)
    QCT = sc("QCT", [8, 128, NQ], BF16)
    KCT = sc("KCT", [2, 128, NQ], BF16)
    VC = sc("VC", [2, NQ, 128], BF16)
    QDT = sc("QDT", [8, 128, NQ], BF16)
    KDT = sc("KDT", [8, 128, NQ], BF16)
    VD = sc("VD", [8, NQ, 128], BF16)
    OT1 = sc("OT1", [16, 128, NQ], BF16)
    hmid1 = sc("hmid1", [NQ, D], F32)
    dbg = {}
    if cfg.dbg:
        for n, shp in [("d_modv", [2, 2, 6 * D]), ("d_hmid", [NQ, D]), ("d_h1", [NQ, D]), ("d_hmid1", [NQ, D])]:
            dbg[n] = nc.dram_tensor(n, shp, F32, kind="ExternalOutput").ap()
        for n, shp in [("d_KAT", [2, 128, NK]), ("d_KBT", [8, 128, NK]), ("d_KRT", [128, NK]), ("d_VA", [2, NK, 128]), ("d_VB", [8, NK, 128]),
                       ("d_QAT", [8, 128, NQ]), ("d_QBT", [8, 128, NQ]), ("d_QRT", [4, 128, NQ]), ("d_OT", [16, 128, NQ]), ("d_OT1", [16, 128, NQ])]:
            dbg[n] = nc.dram_tensor(n, shp, BF16, kind="ExternalOutput").ap()

    with P.es as es:
        S = Sch(nc, es)
        P.S = S

        with ExitStack() as ps:
            cTt = P.sb(ps, "cT", [128, KC, 2], F32)
            scb = P.sb(ps, "scb", [128, KC, 2], BF16)
            wm = [P.sb(ps, "wm%d" % i, [128, KC, 512], BF16) for i in range(2)]
            bmt = [P.sb(ps, "bm%d" % i, [2, 512], F32) for i in range(2)]
            mo = [P.sb(ps, "mo%d" % i, [2, 512], F32) for i in range(2)]
            pm = [P.pb(ps, "pm%d" % i, [128, 512], F32) for i in range(2)]
            S.dma("sp", cTt[:], cT[:, :, :], writes=[cTt])
            S.op("act", lambda e: e.activation(out=scb[:], in_=cTt[:], func=AF.Silu), reads=[cTt], writes=[scb])
            it = 0
            for l in range(2):
                for cg in range(24):
                    b = it % 2
                    it += 1
                    S.dma("pool", wm[b][:], w_mod[l, :, cg * 512:(cg + 1) * 512].rearrange("(kc p) n -> p kc n", p=128), writes=[wm[b]])
                    S.dma("sp", bmt[b][:], b_mod[l, cg * 512:(cg + 1) * 512].partition_broadcast(2), writes=[bmt[b]])
                    for kc in range(KC):
                        S.op("pe", lambda e, kc=kc, b=b: e.matmul(pm[b][0:2, :], scb[:, kc, :], wm[b][:, kc, :], start=(kc == 0), stop=(kc == KC - 1)), reads=[scb, wm[b]], writes=[pm[b]])
                    S.op("dve", lambda e, b=b: e.tensor_tensor(mo[b][:], pm[b][0:2, :], bmt[b][:], ALU.add), reads=[pm[b], bmt[b]], writes=[mo[b]])
                    S.dma("sp", P.modv[l, :, cg * 512:(cg + 1) * 512], mo[b][:], reads=[mo[b]])
            S.barrier()
        if cfg.dbg:
            S.dma("sp", dbg["d_modv"][:, :, :], P.modv[:, :, :])
            S.barrier()

        def run_rr(gens):
            gens = list(gens)
            while gens:
                for g_ in list(gens):
                    try:
                        next(g_)
                    except StopIteration:
                        gens.remove(g_)

        def qkv_phase(layer, nt, src, rope_t, ctx_from, Wdram, col_specs, mod_names_gain):
            with ExitStack() as ps:
                ph0 = Ph()
                P.front_alloc(ps, ph0)
                ph1 = Ph()
                P.front_alloc(ps, ph1, share=ph0)
                phs = [ph0, ph1]
                ident = ph0.ident
                xin = [P.sb(ps, "xin%d" % i, [128, D], F32) for i in range(2)]
                ph0.gtmp = xin[1]
                for which, sfx in ((0, "l"), (1, "c")):
                    P.mod_vecs(ps, ph0, layer, which, [("A" + sfx, "A", 1, mod_names_gain), ("S" + sfx, "S", 0, None)])
                ncols = sum(c["n"] for c in col_specs)
                Wsb = P.sb(ps, "Wsb", [128, KC, ncols], BF16)
                off = 0
                for c in col_specs:
                    c["off"] = off
                    S.dma("pool", Wsb[:, :, off:off + c["n"]], Wdram[:, c["c0"]:c["c0"] + c["n"]].rearrange("(kc p) n -> p kc n", p=128), writes=[Wsb])
                    off += c["n"]
                gains = {}
                gl_all = {"a_q_norm": a_q_norm, "a_k_norm": a_k_norm, "b_q_norm": b_q_norm, "b_kv_norm": b_kv_norm,
                          "c_q_norm": c_q_norm, "c_k_norm": c_k_norm, "d_q_norm": d_q_norm, "d_k_norm": d_k_norm}
                for c in col_specs:
                    gn = c.get("gain")
                    if gn is not None and gn not in gains:
                        n = 512 if gn.startswith("b_") else 128
                        gt = P.sb(ps, gn, [128, n], F32)
                        P.load_bcast("sp", gt, gl_all[gn][:])
                        gains[gn] = gt
                cst = [P.sb(ps, "cs%d" % i, [128, 4, 128], F32) for i in range(2)]
                pq = [P.pb(ps, "pq%d" % i, [128, 512], F32) for i in range(4)]
                pT2 = [P.pb(ps, "pT2%d" % i, [128, 1024], BF16) for i in range(2)]
                ctr = {"pq": 0, "pt": 0}

                def next_pq():
                    ctr["pq"] += 1
                    return pq[ctr["pq"] % 4]

                def next_pt():
                    ctr["pt"] += 1
                    return pT2[ctr["pt"] % 2]

                extra = {}
                for c in col_specs:
                    if c["kind"] in ("mla_q", "mla_kv"):
                        extra["cT"] = P.sb(ps, "cnT", [128, 4, 128], BF16)
                        wn = "wuq" if c["kind"] == "mla_q" else "wukv"
                        wsrc = w_uq if c["kind"] == "mla_q" else w_ukv
                        ncol = 1536 if c["kind"] == "mla_q" else 2048
                        extra[wn] = P.sb(ps, wn, [128, 4, ncol], BF16)
                        S.dma("pool", extra[wn][:], wsrc[:, :].rearrange("(kc p) n -> p kc n", p=128), writes=[extra[wn]])
                chains = []
                for ci, c in enumerate(col_specs):
                    for s0 in range(0, c["n"], 512):
                        s1 = min(c["n"], s0 + 512)
                        w = s1 - s0
                        b = Ph()
                        b.c, b.s0, b.s1, b.w = c, s0, s1, w
                        b.pp = pq[len(chains) % 4]
                        b.ss = P.sb(ps, "css", [128, 16], F32)
                        if c["kind"] != "v":
                            b.qf = P.sb(ps, "cqf", [128, max(w, 128)], F32)
                            b.qsq = P.sb(ps, "cqsq", [128, max(w, 128)], F32)
                            b.hT = P.sb(ps, "chT", [128, 512], BF16)
                        b.hb = P.sb(ps, "chb", [128, max(w, 128)], BF16)
                        if c["kind"] == "mla_q":
                            b.qf2 = P.sb(ps, "cqf2", [128, 512], F32)
                            b.qsq2 = P.sb(ps, "cqsq2", [128, 512], F32)
                            b.hb2 = P.sb(ps, "chb2", [128, 512], BF16)
                            b.hT2 = P.sb(ps, "chT2", [128, 512], BF16)
                        if c["kind"] == "mla_kv":
                            b.hb2 = P.sb(ps, "chb2", [128, 512], BF16)
                            b.hT2 = P.sb(ps, "chT2", [128, 512], BF16)
                        chains.append(b)

                def g_norm_rope(b, qf, qsq, n, H, hd, gain, cs, dst, rope, norm):
                    v3 = lambda ap: ap.rearrange("p (h d) -> p h d", d=hd)
                    if norm:
                        S.op("dve", lambda e: e.tensor_tensor(qsq[:, 0:n], qf[:, 0:n], qf[:, 0:n], ALU.mult), reads=[qf], writes=[qsq])
                        yield
                        S.op("dve", lambda e: e.tensor_reduce(b.ss[:, 0:H], v3(qsq[:, 0:n]), AX.X, ALU.add), reads=[qsq], writes=[b.ss])
                        yield
                        S.op("act", lambda e: e.activation(out=b.ss[:, 0:H], in_=b.ss[:, 0:H], func=AF.Sqrt, bias=EPS, scale=1.0 / hd), reads=[b.ss], writes=[b.ss])
                        yield
                        S.op("dve", lambda e: e.reciprocal(b.ss[:, 0:H], b.ss[:, 0:H]), reads=[b.ss], writes=[b.ss])
                        yield
                        S.op("dve", lambda e: e.tensor_tensor(v3(qf[:, 0:n]), v3(qf[:, 0:n]), b.ss[:, 0:H].unsqueeze(2).to_broadcast([128, H, hd]), ALU.mult), reads=[qf, b.ss], writes=[qf])
                        yield
                        S.op("dve", lambda e: e.tensor_tensor(v3(qf[:, 0:n]), v3(qf[:, 0:n]), gain[:, 0:hd].unsqueeze(1).to_broadcast([128, H, hd]), ALU.mult), reads=[qf, gain], writes=[qf])
                        yield
                    if rope:
                        csT, ci_, si_ = cs
                        cosv = csT[:, ci_, :]
                        sinv = csT[:, si_, :]
                        q4 = hd // 4
                        x5 = lambda ap: ap.rearrange("p (h a b d) -> p h a b d", a=2, b=2, d=q4)
                        s4 = lambda ap: ap.rearrange("p (a b d) -> p a b d", a=2, b=2, d=q4)
                        for b0 in range(2):
                            S.op("dve", lambda e, b0=b0: e.tensor_tensor(x5(qsq[:, 0:n])[:, :, :, b0, :], x5(qf[:, 0:n])[:, :, :, 1 - b0, :],
                                                                        s4(sinv[:, 0:hd])[:, :, b0, :].unsqueeze(1).to_broadcast([128, H, 2, q4]), ALU.mult), reads=[qf, csT], writes=[qsq])
                            yield
                        S.op("dve", lambda e: e.tensor_tensor(v3(qf[:, 0:n]), v3(qf[:, 0:n]), cosv[:, 0:hd].unsqueeze(1).to_broadcast([128, H, hd]), ALU.mult), reads=[qf, csT], writes=[qf])
                        yield
                        S.op("dve", lambda e: e.tensor_tensor(dst[:, 0:n], qf[:, 0:n], qsq[:, 0:n], ALU.add), reads=[qf, qsq], writes=[dst])
                        yield
                    else:
                        S.op("dve", lambda e: e.tensor_copy(dst[:, 0:n], qf[:, 0:n]), reads=[qf], writes=[dst])
                        yield

                def g_tr_out(srcb, nblk, dstb, dram_ap):
                    pTt = next_pt()
                    for j in range(nblk):
                        S.op("pe", lambda e, j=j: e.transpose(pTt[:, j * 128:(j + 1) * 128], srcb[:, j * 128:(j + 1) * 128], ident[:]), reads=[srcb, ident], writes=[pTt])
                    S.op("act", lambda e: e.copy(dstb[:, 0:nblk * 128], pTt[:, 0:nblk * 128]), reads=[pTt], writes=[dstb])
                    yield
                    S.dma("pool", dram_ap, dstb[:, 0:nblk * 128].rearrange("p (h t) -> p h t", t=128), reads=[dstb])
                    yield

                def g_chain(t, b, uT, cs):
                    c, s0, s1, w = b.c, b.s0, b.s1, b.w
                    kind = c["kind"]
                    if c.get("own_only") and not (OWN0 <= t < OWN0 + NOWN):
                        return
                    tsl = slice(t * 128, (t + 1) * 128)
                    H = max(w // 128, 1)
                    h0 = s0 // 128
                    pp = b.pp
                    _lin(S, pp, pp[:, 0:w], uT, Wsb, c["off"] + s0, c["off"] + s1)
                    yield
                    if kind == "v":
                        S.op("act", lambda e: e.copy(b.hb[:, 0:w], pp[:, 0:w]), reads=[pp], writes=[b.hb])
                        yield
                        S.dma("pool", c["dst"][h0:h0 + H, tsl, :].rearrange("h p d -> p h d"), b.hb[:, 0:w].rearrange("p (h d) -> p h d", d=128), reads=[b.hb])
                        yield
                        return
                    S.op("act", lambda e: e.copy(b.qf[:, 0:w], pp[:, 0:w]), reads=[pp], writes=[b.qf])
                    yield
                    if kind == "heads":
                        yield from g_norm_rope(b, b.qf, b.qsq, w, H, 128, gains[c["gain"]], (cs, 0, 1), b.hb, c["rope"], True)
                        yield from g_tr_out(b.hb, H, b.hT, c["dst"][h0:h0 + H, :, tsl].rearrange("h p t -> p h t"))
                    elif kind == "kr":
                        yield from g_norm_rope(b, b.qf, b.qsq, 64, 1, 64, None, (cs, 2, 3), b.hb, True, False)
                        S.op("dve", lambda e: e.tensor_copy(b.hb[:, 64:128], b.hb[:, 0:64]), reads=[b.hb], writes=[b.hb])
                        yield
                        yield from g_tr_out(b.hb, 1, b.hT, c["dst"][:, tsl].rearrange("p (h t) -> p h t", h=1))
                    else:
                        yield from g_norm_rope(b, b.qf, b.qsq, 512, 1, 512, gains[c["gain"]], None, b.hb, False, True)
                        cnT = extra["cT"]
                        pTt = next_pt()
                        for j in range(4):
                            S.op("pe", lambda e, j=j: e.transpose(pTt[:, j * 128:(j + 1) * 128], b.hb[:, j * 128:(j + 1) * 128], ident[:]), reads=[b.hb, ident], writes=[pTt])
                        S.op("act", lambda e: e.copy(cnT[:].rearrange("p a b -> p (a b)"), pTt[:, 0:512]), reads=[pTt], writes=[cnT])
                        yield
                        W2 = extra["wuq" if kind == "mla_q" else "wukv"]
                        for hh in range(2):
                            pp2 = b.pp
                            for h in range(4):
                                hd_ = hh * 4 + h
                                for kc in range(4):
                                    S.op("pe", lambda e, h=h, hd_=hd_, kc=kc: e.matmul(pp2[:, h * 128:(h + 1) * 128], W2[:, kc, hd_ * 128:(hd_ + 1) * 128], cnT[:, kc, :],
                                                                                      start=(kc == 0), stop=(kc == 3)), reads=[W2, cnT], writes=[pp2])
                            yield
                            hT_ = b.hT if hh == 0 else b.hT2
                            S.op("act", lambda e: e.copy(hT_[:, 0:512], pp2[:, 0:512]), reads=[pp2], writes=[hT_])
                            yield
                            S.dma("pool", c["dst"][hh * 4:hh * 4 + 4, :, tsl].rearrange("h p t -> p h t"), hT_[:, 0:512].rearrange("p (h t) -> p h t", t=128), reads=[hT_])
                            yield
                        if kind == "mla_q":
                            pp2 = b.pp
                            _lin(S, pp2, pp2[:, 0:512], cnT, W2, 1024, 1536, kcn=4)
                            yield
                            S.op("act", lambda e: e.copy(b.qf2[:, 0:512], pp2[:, 0:512]), reads=[pp2], writes=[b.qf2])
                            yield
                            yield from g_norm_rope(b, b.qf2, b.qsq2, 512, 8, 64, None, (cs, 2, 3), b.hb2, True, False)
                            yield from g_tr_out(b.hb2, 4, b.hT, c["dst_r"][:, :, tsl].rearrange("h p t -> p h t"))
                        else:
                            for hh in range(2):
                                pp2 = b.pp
                                _lin(S, pp2, pp2[:, 0:512], cnT, W2, 1024 + hh * 512, 1024 + (hh + 1) * 512, kcn=4)
                                yield
                                hb_ = b.hb if hh == 0 else b.hb2
                                S.op("act", lambda e: e.copy(hb_[:, 0:512], pp2[:, 0:512]), reads=[pp2], writes=[hb_])
                                yield
                                S.dma("pool", c["dst_v"][hh * 4:hh * 4 + 4, tsl, :].rearrange("h p d -> p h d"), hb_[:, 0:512].rearrange("p (h d) -> p h d", d=128), reads=[hb_])
                                yield

                def g_front(t):
                    is_ctx = t >= ctx_from
                    xt = xin[t % 2]
                    ph = phs[t % 2]
                    Avec = ph0.Ac if is_ctx else ph0.Al
                    shvec = ph0.Sc if is_ctx else ph0.Sl
                    S.dma("sp", xt[:], src[t * 128:(t + 1) * 128, :], writes=[xt])
                    S.dma("sp", cst[t % 2][:], rope_t[:, t * 128:(t + 1) * 128, :].rearrange("f p d -> p f d"), writes=[cst[t % 2]])
                    yield
                    S.op("act", lambda e: e.activation(out=ph.junk[:], in_=xt[:], func=AF.Square, accum_out=ph.ss[:, 0:1]), reads=[xt], writes=[ph.junk, ph.ss])
                    yield
                    S.op("act", lambda e: e.activation(out=ph.ss[:, 0:1], in_=ph.ss[:, 0:1], func=AF.Sqrt, bias=EPS, scale=1.0 / D), reads=[ph.ss], writes=[ph.ss])
                    yield
                    S.op("dve", lambda e: e.reciprocal(ph.ss[:, 0:1], ph.ss[:, 0:1]), reads=[ph.ss], writes=[ph.ss])
                    yield
                    S.op("dve", lambda e: e.scalar_tensor_tensor(xt[:], xt[:], ph.ss[:, 0:1], Avec[:], ALU.mult, ALU.mult), reads=[xt, ph.ss, Avec], writes=[xt])
                    yield
                    S.op("dve", lambda e: e.tensor_tensor(ph.ub[:], xt[:], shvec[:], ALU.add), reads=[xt, shvec], writes=[ph.ub])
                    yield
                    for half in range(2):
                        p = ph.pT[half]
                        for j in range(8):
                            kc = half * 8 + j
                            S.op("pe", lambda e, j=j, kc=kc: e.transpose(p[:, j * 128:(j + 1) * 128], ph.ub[:, kc * 128:(kc + 1) * 128], ident[:]), reads=[ph.ub, ident], writes=[p])
                        yield
                        dst = ph.uT[:, half * 8:(half + 1) * 8, :].rearrange("p a b -> p (a b)")
                        if half == 0:
                            S.op("act", lambda e: e.copy(dst, p[:]), reads=[p], writes=[ph.uT])
                        else:
                            S.op("dve", lambda e: e.tensor_copy(dst, p[:]), reads=[p], writes=[ph.uT])
                        yield

                run_rr([g_front(0)])
                for t in range(nt):
                    gens = [g_chain(t, b, phs[t % 2].uT, cst[t % 2]) for b in chains]
                    if t + 1 < nt:
                        gens.append(g_front(t + 1))
                    run_rr(gens)
                S.barrier()


        qkv_phase(0, NKT, xkv, ropek, NKT - 2, ab_w_in, [
            dict(kind="heads", c0=1024, n=256, gain="a_k_norm", rope=True, dst=KAT),
            dict(kind="v", c0=1280, n=256, dst=VA),
            dict(kind="mla_kv", c0=1536 + 512, n=512, gain="b_kv_norm", dst=KBT, dst_v=VB),
            dict(kind="kr", c0=1536 + 1024, n=64, dst=KRT),
        ], "g_mix_pre")
        qkv_phase(0, NQT, xq, ropeq, NQT - 2, ab_w_in, [
            dict(kind="heads", c0=0, n=1024, gain="a_q_norm", rope=True, dst=QAT),
            dict(kind="mla_q", c0=1536, n=512, gain="b_q_norm", dst=QBT, dst_r=QRT),
        ], "g_mix_pre")
        if cfg.dbg:
            for n_, src_ in [("d_KAT", KAT), ("d_KBT", KBT), ("d_VA", VA), ("d_VB", VB), ("d_QAT", QAT), ("d_QBT", QBT), ("d_QRT", QRT)]:
                S.dma("sp", dbg[n_][:, :, :], src_[:, :, :])
            S.dma("sp", dbg["d_KRT"][:, :], KRT[:, :])
            S.barrier()

        with ExitStack() as ps:
            ones = P.sb(ps, "ones", [128, 128], BF16)
            S.dma("pool", ones[:], ones_in[:, :], writes=[ones])
            Ksb = [P.sb(ps, "Ksb%d" % i, [128, NK], BF16) for i in range(2)]
            Vsb = [P.sb(ps, "Vsb%d" % i, [128, NKT, 128], BF16) for i in range(2)]
            KRsb = P.sb(ps, "KRsb", [128, NK], BF16)
            qsb = [P.sb(ps, "qsb%d" % i, [128, 512], BF16) for i in range(2)]
            qrsb = [P.sb(ps, "qrsb%d" % i, [128, 512], BF16) for i in range(2)]
            pt = [P.sb(ps, "pt%d" % i, [128, 512], BF16) for i in range(3)]
            rden = P.sb(ps, "rden", [128, 512], F32)
            osb = [P.sb(ps, "osb%d" % i, [128, 512], BF16) for i in range(2)]
            pS = [P.pb(ps, "pS%d" % i, [128, 512], F32) for i in range(3)]
            pO = [P.pb(ps, "pO%d" % i, [128, 512], F32) for i in range(2)]
            pD = [P.pb(ps, "pD%d" % i, [128, 512], F32) for i in range(2)]
            S.dma("sp", KRsb[:], KRT[:, :], writes=[KRsb])

            def load_kv(hh):
                kb = hh % 2
                if hh < 2:
                    S.dma("sp", Ksb[kb][:], KAT[hh, :, :], writes=[Ksb[kb]])
                    vsrc = VA[hh]
                else:
                    S.dma("sp", Ksb[kb][:], KBT[hh - 2, :, :], writes=[Ksb[kb]])
                    vsrc = VB[hh - 2]
                for t0_ in range(0, NKT, 11):
                    t1_ = min(NKT, t0_ + 11)
                    S.dma("sp", Vsb[kb][:, t0_:t1_, :], vsrc[t0_ * 128:t1_ * 128, :].rearrange("(t p) d -> p t d", p=128), writes=[Vsb[kb]])

            glist = []
            for hh in range(10):
                if hh < 2:
                    glist += [(hh, qt * 128, 512) for qt in range(NQT)]
                else:
                    glist += [(hh, t0 * 128, min(4, NQT - t0) * 128) for t0 in range(0, NQT, 4)]

            def load_q(gi):
                hh, tok0, N = glist[gi]
                g = gi % 2
                if hh < 2:
                    S.dma("sp", qsb[g][:].rearrange("p (h t) -> p h t", t=128), QAT[hh * 4:hh * 4 + 4, :, tok0:tok0 + 128].rearrange("h p t -> p h t"), writes=[qsb[g]])
                else:
                    h = hh - 2
                    S.dma("sp", qsb[g][:, 0:N], QBT[h, :, tok0:tok0 + N], writes=[qsb[g]])
                    S.dma("sp", qrsb[g][:, 0:N], QRT[h // 2, :, tok0:tok0 + N], writes=[qrsb[g]])

            load_kv(0)
            load_q(0)
            step = 0
            for gi, (hh, tok0, N) in enumerate(glist):
                g = gi % 2
                kb = hh % 2
                first_of_head = gi == 0 or glist[gi - 1][0] != hh
                if first_of_head and hh + 1 < 10:
                    load_kv(hh + 1)
                if gi + 1 < len(glist):
                    load_q(gi + 1)
                qs = qsb[g]
                qr = qrsb[g]
                scale = 128 ** -0.5 if hh < 2 else 192 ** -0.5
                if hh >= 2:
                    h = hh - 2
                    half = slice(0, 64) if h % 2 == 0 else slice(64, 128)
                kts = list(range(NKT - 2, NKT)) if tok0 >= (NQT - 2) * 128 else list(range(NKT))
                nk = len(kts)
                base = step
                step += nk

                def qk(i):
                    sp_ = pS[(base + i) % 3]
                    ksl = slice(kts[i] * 128, (kts[i] + 1) * 128)
                    if hh < 2:
                        S.op("pe", lambda e: e.matmul(sp_[:, 0:N], Ksb[kb][:, ksl], qs[:, 0:N], start=True, stop=True), reads=[Ksb[kb], qs], writes=[sp_])
                    else:
                        S.op("pe", lambda e: e.matmul(sp_[:, 0:N], Ksb[kb][:, ksl], qs[:, 0:N], start=True, stop=False), reads=[Ksb[kb], qs], writes=[sp_])
                        S.op("pe", lambda e: e.matmul(sp_[:, 0:N], KRsb[half, ksl], qr[half, 0:N], start=False, stop=True), reads=[KRsb, qr], writes=[sp_])

                def ex(i):
                    sp_ = pS[(base + i) % 3]
                    ptt = pt[(base + i) % 3]
                    S.op("act", lambda e: e.activation(out=ptt[:, 0:N], in_=sp_[:, 0:N], func=AF.Exp, scale=scale), reads=[sp_], writes=[ptt])

                def pv(i):
                    ptt = pt[(base + i) % 3]
                    S.op("pe", lambda e: e.matmul(pO[g][:, 0:N], Vsb[kb][:, kts[i], :], ptt[:, 0:N], start=(i == 0), stop=(i == nk - 1)), reads=[Vsb[kb], ptt], writes=[pO[g]])
                    S.op("pe", lambda e: e.matmul(pD[g][:, 0:N], ones[:], ptt[:, 0:N], start=(i == 0), stop=(i == nk - 1)), reads=[ones, ptt], writes=[pD[g]])

                qk(0)
                for i in range(nk):
                    if i + 1 < nk:
                        qk(i + 1)
                    ex(i)
                    pv(i)
                S.op("dve", lambda e: e.reciprocal(rden[:, 0:N], pD[g][:, 0:N]), reads=[pD[g]], writes=[rden])
                ob = osb[g]
                S.op("dve", lambda e: e.tensor_tensor(ob[:, 0:N], pO[g][:, 0:N], rden[:, 0:N], ALU.mult), reads=[pO[g], rden], writes=[ob])
                if hh < 2:
                    S.dma("pool", OT[hh * 4:hh * 4 + 4, :, tok0:tok0 + 128].rearrange("h p t -> p h t"), ob[:].rearrange("p (h t) -> p h t", t=128), reads=[ob])
                else:
                    S.dma("pool", OT[8 + h, :, tok0:tok0 + N], ob[:, 0:N], reads=[ob])
            S.barrier()
        if cfg.dbg:
            S.dma("sp", dbg["d_OT"][:, :, :], OT[:, :, :])
            S.barrier()

        def out_phase(layer, tiles, OTd, Wout, resid, dst):
            with ExitStack() as ps:
                ph = Ph()
                ph.gtmp = P.sb(ps, "gtmp", [128, D], F32)
                P.mod_vecs(ps, ph, layer, 0, [("Gl", "G", 2, "g_mix_post")])
                if layer == 0:
                    P.mod_vecs(ps, ph, layer, 1, [("Gc", "G", 2, "g_mix_post")])
                Wsb = P.sb(ps, "Wo", [128, KC, D], BF16)
                S.dma("pool", Wsb[:], Wout[:, :].rearrange("(kc p) n -> p kc n", p=128), writes=[Wsb])
                oT = [P.sb(ps, "oT%d" % i, [128, 16, 128], BF16) for i in range(2)]
                xr = [P.sb(ps, "xr%d" % i, [128, D], F32) for i in range(2)]
                of = P.sb(ps, "of", [128, D], F32)
                junk = P.sb(ps, "junk", [128, D], BF16)
                ss = P.sb(ps, "ss", [128, 16], F32)
                pp = [P.pb(ps, "po%d" % i, [128, 512], F32) for i in range(8)]
                for i, t in enumerate(tiles):
                    is_ctx = (layer == 0) and t >= NQT - 2
                    o_ = oT[i % 2]
                    x_ = xr[i % 2]
                    tsl = slice(t * 128, (t + 1) * 128)
                    S.dma("sp", o_[:], OTd[:, :, tsl].rearrange("h p t -> p h t"), writes=[o_])
                    S.dma("sp", x_[:], resid[tsl, :], writes=[x_])
                    for cg in range(4):
                        pb_ = pp[(i % 2) * 4 + cg]
                        _lin(S, pb_, pb_[:, :], o_, Wsb, cg * 512, (cg + 1) * 512)
                        if cg % 2 == 0:
                            S.op("act", lambda e, pb_=pb_, cg=cg: e.copy(of[:, cg * 512:(cg + 1) * 512], pb_[:, :]), reads=[pb_], writes=[of])
                        else:
                            S.op("dve", lambda e, pb_=pb_, cg=cg: e.tensor_copy(of[:, cg * 512:(cg + 1) * 512], pb_[:, :]), reads=[pb_], writes=[of])
                    S.op("act", lambda e: e.activation(out=junk[:], in_=of[:], func=AF.Square, accum_out=ss[:, 0:1]), reads=[of], writes=[junk, ss])
                    P.rstd_from_ss(ss, slice(0, 1), D)
                    G = ph.Gc if is_ctx else ph.Gl
                    S.op("dve", lambda e, G=G: e.scalar_tensor_tensor(of[:], of[:], ss[:, 0:1], G[:], ALU.mult, ALU.mult), reads=[of, ss, G], writes=[of])
                    S.op("dve", lambda e, x_=x_: e.tensor_tensor(x_[:], of[:], x_[:], ALU.add), reads=[of, x_], writes=[x_])
                    S.dma("pool", dst[tsl, :], x_[:], reads=[x_])
                S.barrier()

        def ffn_phase(layer, tiles, src, dst_fn):
            JB = 2
            NB = FC // JB
            with ExitStack() as ps:
                ph = Ph()
                P.front_alloc(ps, ph)
                hx = [P.sb(ps, "hx%d" % i, [128, D], F32) for i in range(2)]
                ph.gtmp = hx[0]
                MV = [("A", "A", 4, "g_ffn_pre"), ("Sh", "S", 3, None), ("G", "G", 5, "g_ffn_post")]
                P.mod_vecs(ps, ph, layer, 0, MV)

                def switch_ctx(attrs):
                    for attr, kind, k, gname in MV:
                        if attr not in attrs:
                            continue
                        t_ = getattr(ph, attr)
                        P.load_bcast("sp", t_, P.modv[layer, 1, k * D:(k + 1) * D])
                        if kind != "S":
                            g_ = ph.gtmp
                            P.load_bcast("sp", g_, P.din[gname][layer, :])
                            if kind == "A":
                                S.op("dve", lambda e, t_=t_, g_=g_: e.scalar_tensor_tensor(t_[:], t_[:], 1.0, g_[:], ALU.add, ALU.mult), reads=[t_, g_], writes=[t_])
                            else:
                                S.op("dve", lambda e, t_=t_, g_=g_: e.tensor_tensor(t_[:], t_[:], g_[:], ALU.mult), reads=[t_, g_], writes=[t_])

                facc = [P.sb(ps, "facc%d" % i, [128, D], F32) for i in range(4)]
                u2T = [P.sb(ps, "u2T%d" % i, [128, KC, 512], BF16) for i in range(2)]
                wg = [P.sb(ps, "wg%d" % i, [128, KC, JB * 128], BF16) for i in range(2)]
                wu = [P.sb(ps, "wu%d" % i, [128, KC, JB * 128], BF16) for i in range(2)]
                wd = [P.sb(ps, "wd%d" % i, [128, JB, D], BF16) for i in range(2)]
                aT = [P.sb(ps, "aT%d" % i, [128, JB, 512], BF16) for i in range(2)]
                sg = [P.sb(ps, "sg%d" % i, [128, 512], F32) for i in range(2)]
                pg = [P.pb(ps, "pg%d" % i, [128, 512], F32) for i in range(2)]
                pu = [P.pb(ps, "pu%d" % i, [128, 512], F32) for i in range(2)]
                pd = [P.pb(ps, "pd%d" % i, [128, 512], F32) for i in range(2)]
                groups = []
                i = 0
                while i < len(tiles):
                    ctxf = (layer == 0) and tiles[i] >= NQT - 2
                    j = i
                    while j < len(tiles) and j - i < 4 and (((layer == 0) and tiles[j] >= NQT - 2) == ctxf):
                        j += 1
                    groups.append((tiles[i:j], ctxf))
                    i = j
                state = {"pro_ctx": False, "epi_ctx": False, "hxi": 0}

                def prologue(gidx):
                    gt, ctxf = groups[gidx]
                    if ctxf and not state["pro_ctx"]:
                        state["pro_ctx"] = True
                        switch_ctx(("A", "Sh"))
                    for ti, t in enumerate(gt):
                        x_ = hx[state["hxi"] % 2]
                        state["hxi"] += 1
                        S.dma("sp", x_[:], src[t * 128:(t + 1) * 128, :], writes=[x_])
                        P.front(ph, x_, ph.A, ph.Sh)
                        S.op("dve", lambda e, ti=ti: e.tensor_copy(u2T[gidx % 2][:, :, ti * 128:(ti + 1) * 128], ph.uT[:]), reads=[ph.uT], writes=[u2T[gidx % 2]])

                blk = 0
                gstep = 0
                dstep = 0
                prologue(0)
                for gidx, (gt, ctxf) in enumerate(groups):
                    G = len(gt) * 128
                    uT_ = u2T[gidx % 2]
                    bpar = {}

                    def gu(bi):
                        nonlocal blk, gstep, dstep
                        b = blk % 2
                        blk += 1
                        bpar[bi] = b
                        c0 = bi * JB * 128
                        S.dma("pool", wg[b][:], w_gate[layer, :, c0:c0 + JB * 128].rearrange("(kc p) n -> p kc n", p=128), writes=[wg[b]])
                        S.dma("pool", wu[b][:], w_up[layer, :, c0:c0 + JB * 128].rearrange("(kc p) n -> p kc n", p=128), writes=[wu[b]])
                        S.dma("pool", wd[b][:], w_down[layer, c0:c0 + JB * 128, :].rearrange("(j p) n -> p j n", p=128), writes=[wd[b]])
                        for jj in range(JB):
                            pgt = pg[gstep % 2]
                            sgt = sg[gstep % 2]
                            put = pu[gstep % 2]
                            gstep += 1
                            for kc in range(KC):
                                S.op("pe", lambda e, kc=kc: e.matmul(pgt[:, 0:G], wg[b][:, kc, jj * 128:(jj + 1) * 128], uT_[:, kc, 0:G], start=(kc == 0), stop=(kc == KC - 1)), reads=[wg[b], uT_], writes=[pgt])
                            for kc in range(KC):
                                S.op("pe", lambda e, kc=kc: e.matmul(put[:, 0:G], wu[b][:, kc, jj * 128:(jj + 1) * 128], uT_[:, kc, 0:G], start=(kc == 0), stop=(kc == KC - 1)), reads=[wu[b], uT_], writes=[put])
                            S.op("act", lambda e: e.activation(out=sgt[:, 0:G], in_=pgt[:, 0:G], func=AF.Silu), reads=[pgt], writes=[sgt])
                            S.op("dve", lambda e: e.tensor_tensor(aT[b][:, jj, 0:G], put[:, 0:G], sgt[:, 0:G], ALU.mult), reads=[put, sgt], writes=[aT[b]])

                    def down(bi):
                        nonlocal dstep
                        b = bpar[bi]
                        for ti in range(len(gt)):
                            for cg in range(4):
                                pdt = pd[dstep % 2]
                                dstep += 1
                                for jj in range(JB):
                                    S.op("pe", lambda e, jj=jj: e.matmul(pdt[:, :], aT[b][:, jj, ti * 128:(ti + 1) * 128], wd[b][:, jj, cg * 512:(cg + 1) * 512], start=(jj == 0), stop=(jj == JB - 1)), reads=[aT[b], wd[b]], writes=[pdt])
                                fa = facc[ti]
                                if bi == 0:
                                    S.op("act", lambda e: e.copy(fa[:, cg * 512:(cg + 1) * 512], pdt[:, :]), reads=[pdt], writes=[fa])
                                else:
                                    S.op("dve", lambda e: e.tensor_tensor(fa[:, cg * 512:(cg + 1) * 512], pdt[:, :], fa[:, cg * 512:(cg + 1) * 512], ALU.add), reads=[pdt, fa], writes=[fa])

                    gu(0)
                    for bi in range(NB):
                        if bi + 1 < NB:
                            gu(bi + 1)
                        down(bi)
                        if bi == 3 and gidx + 1 < len(groups):
                            prologue(gidx + 1)
                    if ctxf and not state["epi_ctx"]:
                        state["epi_ctx"] = True
                        switch_ctx(("G",))
                    for ti, t in enumerate(gt):
                        fa = facc[ti]
                        x_ = hx[state["hxi"] % 2]
                        state["hxi"] += 1
                        S.dma("sp", x_[:], src[t * 128:(t + 1) * 128, :], writes=[x_])
                        S.op("act", lambda e: e.activation(out=ph.junk[:], in_=fa[:], func=AF.Square, accum_out=ph.ss[:, 4:5]), reads=[fa], writes=[ph.junk, ph.ss])
                        P.rstd_from_ss(ph.ss, slice(4, 5), D)
                        S.op("dve", lambda e: e.scalar_tensor_tensor(fa[:], fa[:], ph.ss[:, 4:5], ph.G[:], ALU.mult, ALU.mult), reads=[fa, ph.ss, ph.G], writes=[fa])
                        S.op("dve", lambda e: e.tensor_tensor(x_[:], fa[:], x_[:], ALU.add), reads=[fa, x_], writes=[x_])
                        S.dma("sp", dst_fn(t), x_[:], reads=[x_])
                S.barrier()


        out_phase(0, list(range(NQT)), OT, ab_w_out, xq, hmid)
        if cfg.dbg:
            S.dma("sp", dbg["d_hmid"][:, :], hmid[:, :])
            S.barrier()
        ffn_phase(0, list(range(NQT)), hmid, lambda t: h1[t * 128:(t + 1) * 128, :])
        if cfg.dbg:
            S.dma("sp", dbg["d_h1"][:, :], h1[:, :])
            S.barrier()

        qkv_phase(1, NQT, h1, ropeq, NQT - 2, cd_w_in, [
            dict(kind="heads", c0=0, n=1024, gain="c_q_norm", rope=True, dst=QCT, own_only=True),
            dict(kind="heads", c0=1024, n=256, gain="c_k_norm", rope=True, dst=KCT),
            dict(kind="v", c0=1280, n=256, dst=VC),
        ], "g_mix_pre")
        qkv_phase(1, NQT, h1, ropeq, NQT - 2, cd_w_in, [
            dict(kind="heads", c0=1536, n=1024, gain="d_q_norm", rope=False, dst=QDT, own_only=True),
            dict(kind="heads", c0=2560, n=1024, gain="d_k_norm", rope=False, dst=KDT),
        ], "g_mix_pre")
        qkv_phase(1, NQT, h1, ropeq, NQT - 2, cd_w_in, [
            dict(kind="v", c0=3584, n=1024, dst=VD),
        ], "g_mix_pre")

        with ExitStack() as ps:
            ones = P.sb(ps, "ones", [128, 128], BF16)
            S.dma("pool", ones[:], ones_in[:, :], writes=[ones])
            KCs = P.sb(ps, "KCs", [128, 2, NQ], BF16)
            VCs = P.sb(ps, "VCs", [128, 2, NQT, 128], BF16)
            KDs = P.sb(ps, "KDs", [128, 8, NQ], BF16)
            VDs = P.sb(ps, "VDs", [128, 8, NQT, 128], BF16)
            S.dma("sp", KCs[:], KCT[:, :, :].rearrange("h p t -> p h t"), writes=[KCs])
            S.dma("sp", KDs[:], KDT[:, :, :].rearrange("h p t -> p h t"), writes=[KDs])
            for g in range(2):
                S.dma("sp", VCs[:, g, :, :], VC[g, :, :].rearrange("(t p) d -> p t d", p=128), writes=[VCs])
            for h in range(8):
                for t0_ in range(0, NQT, 11):
                    t1_ = min(NQT, t0_ + 11)
                    S.dma("sp", VDs[:, h, t0_:t1_, :], VD[h, t0_ * 128:t1_ * 128, :].rearrange("(t p) d -> p t d", p=128), writes=[VDs])
            esink = P.sb(ps, "esink", [128, 8], F32)
            P.load_bcast("sp", esink, c_sink[:])
            S.op("act", lambda e: e.activation(out=esink[:], in_=esink[:], func=AF.Exp), reads=[esink], writes=[esink])
            mC = P.sb(ps, "mC", [128, 3, 3, 128], F32)
            S.dma("sp", mC[:].rearrange("p v s q -> p (v s) q"), maskC[:, :, :, :].rearrange("v s k q -> k (v s) q"), writes=[mC])
            bD = [P.sb(ps, "bD%d" % i, [128, 6, 128], F32) for i in range(3)]
            qsb = [P.sb(ps, "qsb%d" % i, [128, 512], BF16) for i in range(3)]
            pt = [P.sb(ps, "pt%d" % i, [128, 512], BF16) for i in range(3)]
            sb_ = [P.sb(ps, "sbias%d" % i, [128, 512], F32) for i in range(3)]
            rden = [P.sb(ps, "rden%d" % i, [128, 512], F32) for i in range(2)]
            osb = [P.sb(ps, "osb%d" % i, [128, 512], BF16) for i in range(2)]
            pS = [P.pb(ps, "pS%d" % i, [128, 512], F32) for i in range(3)]
            pO = [P.pb(ps, "pO%d" % i, [128, 512], F32) for i in range(2)]
            pD = [P.pb(ps, "pD%d" % i, [128, 512], F32) for i in range(2)]
            scale = 128 ** -0.5
            glist = []
            for i in range(NOWN):
                glist += [("C", i, g) for g in range(2)]
                glist += [("D", i, h) for h in range(8)]

            def ginfo(gi):
                kind, i, hg = glist[gi]
                e_ = OWN0 + i
                if kind == "C":
                    keys = [(NQT - 2, None), (NQT - 1, None), (e_ - 1, 0), (e_, 1), (e_ + 1, 2)]
                    return kind, i, hg, e_, keys, 512
                lo = -3 if i == NOWN - 1 else -2
                hi = 3 if i == 0 else 2
                rel = list(range(lo, hi + 1))
                keys = [(NQT - 2, None), (NQT - 1, None)] + [(e_ + r_, si) for si, r_ in enumerate(rel)]
                return kind, i, hg, e_, keys, 128

            def load_q(gi):
                kind, i, hg, e_, keys, N = ginfo(gi)
                tsl = slice(e_ * 128, (e_ + 1) * 128)
                qs = qsb[gi % 3]
                if kind == "C":
                    S.dma("sp", qs[:].rearrange("p (h t) -> p h t", t=128), QCT[hg * 4:hg * 4 + 4, :, tsl].rearrange("h p t -> p h t"), writes=[qs])
                else:
                    vD = 0 if i == 0 else (1 if i == 1 else (3 if i == NOWN - 2 else (4 if i == NOWN - 1 else 2)))
                    bt = bD[gi % 3]
                    nsl = len(keys) - 2
                    S.dma("sp", qs[:, 0:128], QDT[hg, :, tsl], writes=[qs])
                    S.dma("sp", bt[:, 0:nsl, :], biasD[vD, hg, 0:nsl, :, :].rearrange("s k q -> k s q"), writes=[bt])

            load_q(0)
            load_q(1)
            step = 0
            for gi in range(len(glist)):
                kind, i, hg, e_, keys, N = ginfo(gi)
                if gi + 2 < len(glist):
                    load_q(gi + 2)
                tsl = slice(e_ * 128, (e_ + 1) * 128)
                gg = gi % 2
                qs = qsb[gi % 3]
                bt = bD[gi % 3]
                vC = 0 if i == 0 else (2 if i == NOWN - 1 else 1)
                nk = len(keys)
                base = step
                step += nk

                def qk(j):
                    kt, slot = keys[j]
                    sp_ = pS[(base + j) % 3]
                    ksl = slice(kt * 128, (kt + 1) * 128)
                    if kind == "C":
                        S.op("pe", lambda e: e.matmul(sp_[:, 0:N], KCs[:, hg, ksl], qs[:, 0:N], start=True, stop=True), reads=[KCs, qs], writes=[sp_])
                    else:
                        S.op("pe", lambda e: e.matmul(sp_[:, 0:N], KDs[:, hg, ksl], qs[:, 0:N], start=True, stop=True), reads=[KDs, qs], writes=[sp_])

                def ex(j):
                    kt, slot = keys[j]
                    sp_ = pS[(base + j) % 3]
                    ptt = pt[(base + j) % 3]
                    if slot is None:
                        S.op("act", lambda e: e.activation(out=ptt[:, 0:N], in_=sp_[:, 0:N], func=AF.Exp, scale=scale), reads=[sp_], writes=[ptt])
                        return
                    sbt = sb_[(base + j) % 3]
                    if kind == "C":
                        S.op("dve", lambda e: e.scalar_tensor_tensor(sbt[:].rearrange("p (h t) -> p h t", t=128), sp_[:, :].rearrange("p (h t) -> p h t", t=128), scale,
                                                                      mC[:, vC, slot, :].unsqueeze(1).to_broadcast([128, 4, 128]), ALU.mult, ALU.add), reads=[sp_, mC], writes=[sbt])
                    else:
                        S.op("dve", lambda e: e.scalar_tensor_tensor(sbt[:, 0:128], sp_[:, 0:128], scale, bt[:, slot, :], ALU.mult, ALU.add), reads=[sp_, bt], writes=[sbt])
                    S.op("act", lambda e: e.activation(out=ptt[:, 0:N], in_=sbt[:, 0:N], func=AF.Exp), reads=[sbt], writes=[ptt])

                def pv(j):
                    kt, slot = keys[j]
                    ptt = pt[(base + j) % 3]
                    vv = VCs[:, hg, kt, :] if kind == "C" else VDs[:, hg, kt, :]
                    S.op("pe", lambda e: e.matmul(pO[gg][:, 0:N], vv, ptt[:, 0:N], start=(j == 0), stop=(j == nk - 1)), reads=[VCs if kind == "C" else VDs, ptt], writes=[pO[gg]])
                    S.op("pe", lambda e: e.matmul(pD[gg][:, 0:N], ones[:], ptt[:, 0:N], start=(j == 0), stop=(j == nk - 1)), reads=[ones, ptt], writes=[pD[gg]])

                qk(0)
                for j in range(nk):
                    if j + 1 < nk:
                        qk(j + 1)
                    ex(j)
                    pv(j)
                rd = rden[gg]
                ob = osb[gg]
                if kind == "C":
                    S.op("dve", lambda e: e.tensor_tensor(rd[:].rearrange("p (h t) -> p h t", t=128), pD[gg][:, :].rearrange("p (h t) -> p h t", t=128),
                                                          esink[:, hg * 4:hg * 4 + 4].unsqueeze(2).to_broadcast([128, 4, 128]), ALU.add), reads=[pD[gg], esink], writes=[rd])
                    S.op("dve", lambda e: e.reciprocal(rd[:], rd[:]), reads=[rd], writes=[rd])
                    S.op("dve", lambda e: e.tensor_tensor(ob[:], pO[gg][:, :], rd[:], ALU.mult), reads=[pO[gg], rd], writes=[ob])
                    S.dma("pool", OT1[hg * 4:hg * 4 + 4, :, tsl].rearrange("h p t -> p h t"), ob[:].rearrange("p (h t) -> p h t", t=128), reads=[ob])
                else:
                    S.op("dve", lambda e: e.reciprocal(rd[:, 0:128], pD[gg][:, 0:128]), reads=[pD[gg]], writes=[rd])
                    S.op("dve", lambda e: e.tensor_tensor(ob[:, 0:128], pO[gg][:, 0:128], rd[:, 0:128], ALU.mult), reads=[pO[gg], rd], writes=[ob])
                    S.dma("pool", OT1[8 + hg, :, tsl], ob[:, 0:128], reads=[ob])
            S.barrier()
        if cfg.dbg:
            S.dma("sp", dbg["d_OT1"][:, :, :], OT1[:, :, :])
            S.barrier()


        own = list(range(OWN0, OWN0 + NOWN))
        out_phase(1, own, OT1, cd_w_out, h1, hmid1)
        if cfg.dbg:
            S.dma("sp", dbg["d_hmid1"][:, :], hmid1[:, :])
            S.barrier()
        ffn_phase(1, own, hmid1, lambda t: yout[(t - OWN0) * 128:(t - OWN0 + 1) * 128, :])
        S.barrier()
        P.nins = S.nins
    return P


GRID_W = 64
L_SEQ = 8192


def _rope_tables(tok, dim):
    tok = np.asarray(tok)
    valid = tok >= 0
    t = np.where(valid, tok, 0)
    row = (t // GRID_W).astype(np.float32)
    col = (t % GRID_W).astype(np.float32)
    half = dim // 2
    inv = (np.float32(10000.0) ** (-np.arange(0, half, 2, dtype=np.float32) / np.float32(half))).astype(np.float32)
    ar = row[:, None] * inv[None, :]
    ac = col[:, None] * inv[None, :]
    ang = np.concatenate([ar, ar, ac, ac], axis=-1).astype(np.float32)
    cos = np.cos(ang).astype(np.float32)
    sin = np.sin(ang).astype(np.float32)
    q = dim // 4
    sgn = np.concatenate([-np.ones(q), np.ones(q), -np.ones(q), np.ones(q)]).astype(np.float32)
    sin = sin * sgn[None, :]
    cos[~valid] = 1.0
    sin[~valid] = 0.0
    return cos, sin


def _rope_pack(tok):
    n = len(tok)
    out = np.zeros((4, n, 128), np.float32)
    c, s = _rope_tables(tok, 128)
    out[0], out[1] = c, s
    c, s = _rope_tables(tok, 64)
    out[2, :, :64], out[3, :, :64] = c, s
    return out


def _mask_c(cc, nown=16):
    kk = np.arange(128)[:, None]
    qq = np.arange(128)[None, :]
    m = np.zeros((3, 3, 128, 128), np.float32)
    for v in range(3):
        m[v, 0] = np.where(kk >= qq, 0.0, NEG)
        m[v, 2] = np.where(kk <= qq, 0.0, NEG)
    if cc == 0:
        m[0, 0] = NEG
    if cc == 3:
        m[2, 2] = NEG
    return m


def _bias_d(cc, rpb, nown=16):
    out = np.full((5, 8, 6, 128, 128), NEG, np.float32)
    rows = L_SEQ // GRID_W
    var_tiles = [0, 1, 2, nown - 2, nown - 1]
    for v, i in enumerate(var_tiles):
        lo = -3 if i == nown - 1 else -2
        hi = 3 if i == 0 else 2
        qtok = 2048 * cc + 128 * i + np.arange(128)
        qr = qtok // GRID_W
        qc = qtok % GRID_W
        rs = np.clip(qr - 4, 0, rows - 8)
        cs = np.clip(qc - 8, 0, GRID_W - 16)
        for si, r_ in enumerate(range(lo, hi + 1)):
            ktok = 2048 * cc + 128 * (i + r_) + np.arange(128)
            inb = (ktok >= 0) & (ktok < L_SEQ)
            kr = np.where(inb, ktok, 0) // GRID_W
            kc = np.where(inb, ktok, 0) % GRID_W
            ok = inb[:, None] & (kr[:, None] >= rs[None, :]) & (kr[:, None] < rs[None, :] + 8) & (kc[:, None] >= cs[None, :]) & (kc[:, None] < cs[None, :] + 16)
            di = np.clip(kr[:, None] - qr[None, :] + 7, 0, 14)
            dj = np.clip(kc[:, None] - qc[None, :] + 15, 0, 30)
            g = rpb[:, di, dj]
            out[v, :, si] = np.where(ok[None], g, NEG)
    return out


def _host_inputs(inputs, cfg=None):
    f = lambda a: np.ascontiguousarray(np.asarray(a, dtype=np.float32))
    x = f(inputs["x"])
    ctx = f(inputs["ctx"])
    c = f(inputs["c"])
    c_ctx = f(inputs["c_ctx"])
    shared = {
        "ident": np.eye(128, dtype=np.float32), "ones": np.ones((128, 128), np.float32),
        "w_mod": f(inputs["w_mod"]), "b_mod": f(inputs["b_mod"]),
        "g_mix_pre": f(inputs["g_mix_pre"]), "g_mix_post": f(inputs["g_mix_post"]),
        "g_ffn_pre": f(inputs["g_ffn_pre"]), "g_ffn_post": f(inputs["g_ffn_post"]),
        "w_gate": f(inputs["w_gate"]), "w_up": f(inputs["w_up"]), "w_down": f(inputs["w_down"]),
        "ab_w_in": f(inputs["ab_w_in"][0]), "ab_w_out": f(inputs["ab_w_out"][0]),
        "a_q_norm": f(inputs["a_q_norm"][0]), "a_k_norm": f(inputs["a_k_norm"][0]),
        "b_q_norm": f(inputs["b_q_norm"][0]), "b_kv_norm": f(inputs["b_kv_norm"][0]),
        "cd_w_in": f(inputs["cd_w_in"][0]), "cd_w_out": f(inputs["cd_w_out"][0]),
        "c_q_norm": f(inputs["c_q_norm"][0]), "c_k_norm": f(inputs["c_k_norm"][0]), "c_sink": f(inputs["c_sink"][0]),
        "d_q_norm": f(inputs["d_q_norm"][0]), "d_k_norm": f(inputs["d_k_norm"][0]),
    }
    wuq = f(inputs["b_w_uq"][0]).reshape(512, 8, 192)
    shared["w_uq"] = np.ascontiguousarray(np.concatenate([wuq[:, :, :128].reshape(512, 1024), wuq[:, :, 128:].reshape(512, 512)], axis=1))
    wukv = f(inputs["b_w_ukv"][0]).reshape(512, 8, 256)
    shared["w_ukv"] = np.ascontiguousarray(np.concatenate([wukv[:, :, :128].reshape(512, 1024), wukv[:, :, 128:].reshape(512, 1024)], axis=1))
    rpb = f(inputs["d_rpb"][0])
    maps = []
    ktok = np.concatenate([np.arange(L_SEQ), -np.ones(256, np.int64)])
    ropek = _rope_pack(ktok)
    for core in range(8):
        b, cc = core // 4, core % 4
        m = dict(shared)
        tok = np.arange(2048 * cc - 256, 2048 * cc + 2048 + 256)
        inb = (tok >= 0) & (tok < L_SEQ)
        tokc = np.clip(tok, 0, L_SEQ - 1)
        m["xq"] = np.ascontiguousarray(np.concatenate([x[b][tokc], ctx[b]], axis=0))
        m["xkv"] = np.ascontiguousarray(np.concatenate([x[b], ctx[b]], axis=0))
        m["cT"] = np.ascontiguousarray(np.stack([c[b].reshape(KC, 128).T, c_ctx.reshape(KC, 128).T], axis=-1))
        m["ropeq"] = _rope_pack(np.concatenate([np.where(inb, tok, -1), -np.ones(256, np.int64)]))
        m["ropek"] = ropek
        m["maskC"] = _mask_c(cc)
        m["biasD"] = _bias_d(cc, rpb)
        maps.append(m)
    return maps


_CACHE = {}


def kernel(**inputs):
    cfg = Cfg()
    if "prog" not in _CACHE:
        _CACHE["prog"] = build(cfg)
    P = _CACHE["prog"]
    maps = _host_inputs(inputs)
    res = run_bass_kernel_spmd(P.nc, maps, core_ids=list(range(8)))
    out = np.zeros((2, L_SEQ, D), np.float32)
    for core in range(8):
        b, cc = core // 4, core % 4
        out[b, 2048 * cc:2048 * cc + 2048] = np.asarray(res.results[core]["y"], dtype=np.float32)
    return out
```

```python
import numpy as np
from contextlib import ExitStack
import concourse.bass as bass
import concourse.mybir as mybir
from concourse.bass_utils import run_bass_kernel_spmd

F32 = mybir.dt.float32
BF16 = mybir.dt.bfloat16
AF = mybir.ActivationFunctionType
ALU = mybir.AluOpType
AX = mybir.AxisListType
ND = 12

D = 2048
KC = 16
DFF = 5632
FC = 44
EPS = 1e-6
NEG = -1e30


class Res:
    __slots__ = ("w", "rs", "name")

    def __init__(self, name=""):
        self.w = None
        self.rs = {}
        self.name = name


class T:
    def __init__(self, t, name):
        self.t = t
        self.r = Res(name)

    def __getitem__(self, k):
        return self.t[k]


def _res(x):
    return x.r if isinstance(x, T) else x


class Sch:
    def __init__(self, nc, es):
        self.nc = nc
        self.eng = {"pe": nc.tensor, "act": nc.scalar, "dve": nc.vector, "pool": nc.gpsimd, "sp": nc.sync}
        self.semobj = {}
        self.cnt = {}
        for k in ["pe", "act", "dve", "pool"]:
            self.semobj[k] = es.enter_context(nc.semaphore("s_" + k))
            self.cnt[k] = 0
        self.known = {k: {} for k in self.eng}
        self.dcnt = {}
        self.dnext = {}
        for q in ["sp", "pool"]:
            for i in range(ND):
                self.semobj[(q, i)] = es.enter_context(nc.semaphore("d_%s_%d" % (q, i)))
                self.dcnt[(q, i)] = 0
            self.dnext[q] = 0
        self.nins = 0

    def _wait(self, e, deps):
        need = {}
        kn = self.known[e]
        for d in deps:
            if d is None:
                continue
            k, v = d
            if kn.get(k, 0) >= v:
                continue
            if need.get(k, 0) < v:
                need[k] = v
        for k, v in need.items():
            self.eng[e].wait_ge(self.semobj[k], v)
            kn[k] = v
            self.nins += 1

    @staticmethod
    def _deps(reads, writes):
        deps = []
        for r in reads:
            deps.append(_res(r).w)
        for w in writes:
            w = _res(w)
            deps.append(w.w)
            deps.extend(w.rs.items())
        return deps

    @staticmethod
    def _mark(ev, reads, writes):
        k, v = ev
        for r in reads:
            r = _res(r)
            if r.rs.get(k, 0) < v:
                r.rs[k] = v
        for w in writes:
            w = _res(w)
            w.w = ev
            w.rs = {}

    def op(self, e, fn, reads=(), writes=()):
        deps = self._deps(reads, writes)
        if e == "pe":
            deps = [d for d in deps if d is not None and d[0] != "pe"]
        self._wait(e, deps)
        ins = fn(self.eng[e])
        self.cnt[e] += 1
        ins.then_inc(self.semobj[e], 1)
        self.nins += 1
        ev = (e, self.cnt[e])
        self._mark(ev, reads, writes)
        return ev

    def dma(self, q, out, in_, reads=(), writes=()):
        i = self.dnext[q]
        self.dnext[q] = (i + 1) % ND
        key = (q, i)
        deps = self._deps(reads, writes)
        if self.dcnt[key] > 0:
            deps.append((key, self.dcnt[key]))
        self._wait(q, deps)
        ins = self.eng[q].dma_start(out=out, in_=in_)
        self.dcnt[key] += 16
        ins.then_inc(self.semobj[key], 16)
        self.nins += 1
        ev = (key, self.dcnt[key])
        self._mark(ev, reads, writes)
        return ev

    def barrier(self):
        deps = [(k, v) for k, v in self.cnt.items() if v > 0]
        deps += [(k, v) for k, v in self.dcnt.items() if v > 0]
        for e in ("pe", "act", "dve", "pool", "sp"):
            self._wait(e, deps)


class Cfg:
    def __init__(self, nqt=22, nkt=66, own0=2, nown=16, dbg=False):
        self.NQT = nqt
        self.NKT = nkt
        self.OWN0 = own0
        self.NOWN = nown
        self.dbg = dbg


class Prog:
    def __init__(self, cfg):
        self.cfg = cfg
        self.nc = bass.Bass("TRN2", target_bir_lowering=False)
        self.es = ExitStack()
        self.S = None
        self.din = {}
        self.uid = 0

    def inp(self, name, shape, dt=F32):
        self.din[name] = self.nc.dram_tensor(name, list(shape), dt, kind="ExternalInput").ap()
        return self.din[name]

    def scratch(self, name, shape, dt):
        return self.nc.dram_tensor(name, list(shape), dt, kind="Internal").ap()

    def sb(self, ps, name, shape, dt):
        self.uid += 1
        return T(ps.enter_context(self.nc.sbuf_tensor("%s_%d" % (name, self.uid), list(shape), dt)), name)

    def pb(self, ps, name, shape, dt):
        self.uid += 1
        return T(ps.enter_context(self.nc.psum_tensor("%s_%d" % (name, self.uid), list(shape), dt)), name)

    def rstd_from_ss(self, ss, col, n):
        S = self.S
        a = ss[:, col]
        S.op("act", lambda e: e.activation(out=a, in_=a, func=AF.Sqrt, bias=EPS, scale=1.0 / n), reads=[ss], writes=[ss])
        S.op("dve", lambda e: e.reciprocal(a, a), reads=[ss], writes=[ss])

    def load_bcast(self, q, dst, src_ap):
        self.S.dma(q, dst[:], src_ap.partition_broadcast(128), writes=[dst])

    def front(self, ph, xt, Avec, shvec):
        S = self.S
        ss, junk, ub, uT, pT, ident = ph.ss, ph.junk, ph.ub, ph.uT, ph.pT, ph.ident
        S.op("act", lambda e: e.activation(out=junk[:], in_=xt[:], func=AF.Square, accum_out=ss[:, 0:1]), reads=[xt], writes=[junk, ss])
        self.rstd_from_ss(ss, slice(0, 1), D)
        S.op("dve", lambda e: e.scalar_tensor_tensor(xt[:], xt[:], ss[:, 0:1], Avec[:], ALU.mult, ALU.mult), reads=[xt, ss, Avec], writes=[xt])
        S.op("dve", lambda e: e.tensor_tensor(ub[:], xt[:], shvec[:], ALU.add), reads=[xt, shvec], writes=[ub])
        for half in range(2):
            p = pT[half]
            for j in range(8):
                kc = half * 8 + j
                S.op("pe", lambda e, p=p, j=j, kc=kc: e.transpose(p[:, j * 128:(j + 1) * 128], ub[:, kc * 128:(kc + 1) * 128], ident[:]), reads=[ub, ident], writes=[p])
            eng = "act" if half == 0 else "dve"
            dst = uT[:, half * 8:(half + 1) * 8, :].rearrange("p a b -> p (a b)")
            if eng == "act":
                S.op("act", lambda e, p=p, dst=dst: e.copy(dst, p[:]), reads=[p], writes=[uT])
            else:
                S.op("dve", lambda e, p=p, dst=dst: e.tensor_copy(dst, p[:]), reads=[p], writes=[uT])

    def front_alloc(self, ps, ph, share=None):
        ph.ss = self.sb(ps, "ss", [128, 16], F32)
        ph.junk = self.sb(ps, "junk", [128, D], BF16)
        ph.ub = self.sb(ps, "ub", [128, D], BF16)
        ph.uT = self.sb(ps, "uT", [128, KC, 128], BF16)
        if share is not None:
            ph.pT = share.pT
            ph.ident = share.ident
            return
        ph.pT = [self.pb(ps, "pT%d" % i, [128, 1024], BF16) for i in range(2)]
        ph.ident = self.sb(ps, "ident", [128, 128], BF16)
        self.S.dma("pool", ph.ident[:], self.din["ident"][:, :], writes=[ph.ident])

    def mod_vecs(self, ps, ph, layer, which, names):
        S = self.S
        for attr, kind, k, gname in names:
            t = self.sb(ps, attr, [128, D], F32)
            self.load_bcast("sp", t, self.modv[layer, which, k * D:(k + 1) * D])
            if kind != "S":
                g = ph.gtmp
                self.load_bcast("sp", g, self.din[gname][layer, :])
                if kind == "A":
                    S.op("dve", lambda e, t=t, g=g: e.scalar_tensor_tensor(t[:], t[:], 1.0, g[:], ALU.add, ALU.mult), reads=[t, g], writes=[t])
                else:
                    S.op("dve", lambda e, t=t, g=g: e.tensor_tensor(t[:], t[:], g[:], ALU.mult), reads=[t, g], writes=[t])
            setattr(ph, attr, t)

    def head_norm_rope(self, ph, src_ps, src_ap, H, hd, gain, cs, dstT, rope=True, norm=True):
        S = self.S
        qf = ph.qf
        n = H * hd
        v3 = lambda ap: ap.rearrange("p (h d) -> p h d", d=hd)
        S.op("act", lambda e: e.copy(qf[:, 0:n], src_ap), reads=[src_ps], writes=[qf])
        if norm:
            sq = ph.qsq
            S.op("dve", lambda e: e.tensor_tensor(sq[:, 0:n], qf[:, 0:n], qf[:, 0:n], ALU.mult), reads=[qf], writes=[sq])
            S.op("dve", lambda e: e.tensor_reduce(ph.ss[:, 2:2 + H], v3(sq[:, 0:n]), AX.X, ALU.add), reads=[sq], writes=[ph.ss])
            self.rstd_from_ss(ph.ss, slice(2, 2 + H), hd)
            S.op("dve", lambda e: e.tensor_tensor(v3(qf[:, 0:n]), v3(qf[:, 0:n]), ph.ss[:, 2:2 + H].unsqueeze(2).to_broadcast([128, H, hd]), ALU.mult), reads=[qf, ph.ss], writes=[qf])
            S.op("dve", lambda e: e.tensor_tensor(v3(qf[:, 0:n]), v3(qf[:, 0:n]), gain[:, 0:hd].unsqueeze(1).to_broadcast([128, H, hd]), ALU.mult), reads=[qf, gain], writes=[qf])
        if rope:
            csT, ci_, si_ = cs
            cosv = csT[:, ci_, :]
            sinv = csT[:, si_, :]
            q4 = hd // 4
            rot = ph.qsq
            x5 = lambda ap: ap.rearrange("p (h a b d) -> p h a b d", a=2, b=2, d=q4)
            s4 = lambda ap: ap.rearrange("p (a b d) -> p a b d", a=2, b=2, d=q4)
            for b0 in range(2):
                S.op("dve", lambda e, b0=b0: e.tensor_tensor(x5(rot[:, 0:n])[:, :, :, b0, :], x5(qf[:, 0:n])[:, :, :, 1 - b0, :],
                                                            s4(sinv[:, 0:hd])[:, :, b0, :].unsqueeze(1).to_broadcast([128, H, 2, q4]), ALU.mult), reads=[qf, csT], writes=[rot])
            S.op("dve", lambda e: e.tensor_tensor(v3(qf[:, 0:n]), v3(qf[:, 0:n]), cosv[:, 0:hd].unsqueeze(1).to_broadcast([128, H, hd]), ALU.mult), reads=[qf, csT], writes=[qf])
            S.op("dve", lambda e: e.tensor_tensor(dstT[:, 0:n], qf[:, 0:n], rot[:, 0:n], ALU.add), reads=[qf, rot], writes=[dstT])
        else:
            S.op("dve", lambda e: e.tensor_copy(dstT[:, 0:n], qf[:, 0:n]), reads=[qf], writes=[dstT])

    def transpose_out(self, ph, src, nblk, pTt, dst_sb, dram_ap, rows=128):
        S = self.S
        for j in range(nblk):
            S.op("pe", lambda e, j=j: e.transpose(pTt[:, j * 128:(j + 1) * 128], src[:, j * 128:(j + 1) * 128], ph.ident[:]), reads=[src, ph.ident], writes=[pTt])
        S.op("act", lambda e: e.copy(dst_sb[:, 0:nblk * 128], pTt[:, 0:nblk * 128]), reads=[pTt], writes=[dst_sb])
        S.dma("pool", dram_ap, dst_sb[0:rows, 0:nblk * 128].rearrange("p (h t) -> p h t", t=128), reads=[dst_sb])


def _lin(S, ps_tile, out_ap, uT, W, c0, c1, kcn=KC):
    for kc in range(kcn):
        S.op("pe", lambda e, kc=kc: e.matmul(out_ap, uT[:, kc, :], W[:, kc, c0:c1], start=(kc == 0), stop=(kc == kcn - 1)), reads=[uT, W], writes=[ps_tile])


class Ph:
    pass


def build(cfg):
    P = Prog(cfg)
    nc = P.nc
    NQT, NKT = cfg.NQT, cfg.NKT
    NQ = NQT * 128
    NK = NKT * 128
    NOWN = cfg.NOWN
    OWN0 = cfg.OWN0
    inp = P.inp
    xq = inp("xq", [NQ, D])
    xkv = inp("xkv", [NK, D])
    cT = inp("cT", [128, KC, 2])
    inp("ident", [128, 128])
    ones_in = inp("ones", [128, 128])
    w_mod = inp("w_mod", [2, D, 6 * D])
    b_mod = inp("b_mod", [2, 6 * D])
    for n in ["g_mix_pre", "g_mix_post", "g_ffn_pre", "g_ffn_post"]:
        inp(n, [2, D])
    w_gate = inp("w_gate", [2, D, DFF])
    w_up = inp("w_up", [2, D, DFF])
    w_down = inp("w_down", [2, DFF, D])
    ab_w_in = inp("ab_w_in", [D, 2624])
    ab_w_out = inp("ab_w_out", [D, D])
    a_q_norm = inp("a_q_norm", [128])
    a_k_norm = inp("a_k_norm", [128])
    b_q_norm = inp("b_q_norm", [512])
    b_kv_norm = inp("b_kv_norm", [512])
    w_uq = inp("w_uq", [512, 1536])
    w_ukv = inp("w_ukv", [512, 2048])
    cd_w_in = inp("cd_w_in", [D, 4608])
    cd_w_out = inp("cd_w_out", [D, D])
    c_q_norm = inp("c_q_norm", [128])
    c_k_norm = inp("c_k_norm", [128])
    c_sink = inp("c_sink", [8])
    d_q_norm = inp("d_q_norm", [128])
    d_k_norm = inp("d_k_norm", [128])
    ropeq = inp("ropeq", [4, NQ, 128])
    ropek = inp("ropek", [4, NK, 128])
    maskC = inp("maskC", [3, 3, 128, 128])
    biasD = inp("biasD", [5, 8, 6, 128, 128])
    yout = nc.dram_tensor("y", [NOWN * 128, D], F32, kind="ExternalOutput").ap()
    sc = P.scratch
    P.modv = sc("modv", [2, 2, 6 * D], F32)
    KAT = sc("KAT", [2, 128, NK], BF16)
    VA = sc("VA", [2, NK, 128], BF16)
    KBT = sc("KBT", [8, 128, NK], BF16)
    KRT = sc("KRT", [128, NK], BF16)
    VB = sc("VB", [8, NK, 128], BF16)
    QAT = sc("QAT", [8, 128, NQ], BF16)
    QBT = sc("QBT", [8, 128, NQ], BF16)
    QRT = sc("QRT", [4, 128, NQ], BF16)
    OT = sc("OT", [16, 128, NQ], BF16)
    hmid = sc("hmid", [NQ, D], F32)
    h1 = sc("h1", [NQ, D], F32)
    QCT = sc("QCT", [8, 128, NQ], BF16)
    KCT = sc("KCT", [2, 128, NQ], BF16)
    VC = sc("VC", [2, NQ, 128], BF16)
    QDT = sc("QDT", [8, 128, NQ], BF16)
    KDT = sc("KDT", [8, 128, NQ], BF16)
    VD = sc("VD", [8, NQ, 128], BF16)
    OT1 = sc("OT1", [16, 128, NQ], BF16)
    hmid1 = sc("hmid1", [NQ, D], F32)
    dbg = {}
    if cfg.dbg:
        for n, shp in [("d_modv", [2, 2, 6 * D]), ("d_hmid", [NQ, D]), ("d_h1", [NQ, D]), ("d_hmid1", [NQ, D])]:
            dbg[n] = nc.dram_tensor(n, shp, F32, kind="ExternalOutput").ap()
        for n, shp in [("d_KAT", [2, 128, NK]), ("d_KBT", [8, 128, NK]), ("d_KRT", [128, NK]), ("d_VA", [2, NK, 128]), ("d_VB", [8, NK, 128]),
                       ("d_QAT", [8, 128, NQ]), ("d_QBT", [8, 128, NQ]), ("d_QRT", [4, 128, NQ]), ("d_OT", [16, 128, NQ]), ("d_OT1", [16, 128, NQ])]:
            dbg[n] = nc.dram_tensor(n, shp, BF16, kind="ExternalOutput").ap()

    with P.es as es:
        S = Sch(nc, es)
        P.S = S

        with ExitStack() as ps:
            cTt = P.sb(ps, "cT", [128, KC, 2], F32)
            scb = P.sb(ps, "scb", [128, KC, 2], BF16)
            wm = [P.sb(ps, "wm%d" % i, [128, KC, 512], BF16) for i in range(2)]
            bmt = [P.sb(ps, "bm%d" % i, [2, 512], F32) for i in range(2)]
            mo = [P.sb(ps, "mo%d" % i, [2, 512], F32) for i in range(2)]
            pm = [P.pb(ps, "pm%d" % i, [128, 512], F32) for i in range(2)]
            S.dma("sp", cTt[:], cT[:, :, :], writes=[cTt])
            S.op("act", lambda e: e.activation(out=scb[:], in_=cTt[:], func=AF.Silu), reads=[cTt], writes=[scb])
            it = 0
            for l in range(2):
                for cg in range(24):
                    b = it % 2
                    it += 1
                    S.dma("pool", wm[b][:], w_mod[l, :, cg * 512:(cg + 1) * 512].rearrange("(kc p) n -> p kc n", p=128), writes=[wm[b]])
                    S.dma("sp", bmt[b][:], b_mod[l, cg * 512:(cg + 1) * 512].partition_broadcast(2), writes=[bmt[b]])
                    for kc in range(KC):
                        S.op("pe", lambda e, kc=kc, b=b: e.matmul(pm[b][0:2, :], scb[:, kc, :], wm[b][:, kc, :], start=(kc == 0), stop=(kc == KC - 1)), reads=[scb, wm[b]], writes=[pm[b]])
                    S.op("dve", lambda e, b=b: e.tensor_tensor(mo[b][:], pm[b][0:2, :], bmt[b][:], ALU.add), reads=[pm[b], bmt[b]], writes=[mo[b]])
                    S.dma("sp", P.modv[l, :, cg * 512:(cg + 1) * 512], mo[b][:], reads=[mo[b]])
            S.barrier()
        if cfg.dbg:
            S.dma("sp", dbg["d_modv"][:, :, :], P.modv[:, :, :])
            S.barrier()

        def run_rr(gens):
            gens = list(gens)
            while gens:
                for g_ in list(gens):
                    try:
                        next(g_)
                    except StopIteration:
                        gens.remove(g_)

        def qkv_phase(layer, nt, src, rope_t, ctx_from, Wdram, col_specs, mod_names_gain):
            with ExitStack() as ps:
                ph0 = Ph()
                P.front_alloc(ps, ph0)
                ph1 = Ph()
                P.front_alloc(ps, ph1, share=ph0)
                phs = [ph0, ph1]
                ident = ph0.ident
                xin = [P.sb(ps, "xin%d" % i, [128, D], F32) for i in range(2)]
                ph0.gtmp = xin[1]
                for which, sfx in ((0, "l"), (1, "c")):
                    P.mod_vecs(ps, ph0, layer, which, [("A" + sfx, "A", 1, mod_names_gain), ("S" + sfx, "S", 0, None)])
                ncols = sum(c["n"] for c in col_specs)
                Wsb = P.sb(ps, "Wsb", [128, KC, ncols], BF16)
                off = 0
                for c in col_specs:
                    c["off"] = off
                    S.dma("pool", Wsb[:, :, off:off + c["n"]], Wdram[:, c["c0"]:c["c0"] + c["n"]].rearrange("(kc p) n -> p kc n", p=128), writes=[Wsb])
                    off += c["n"]
                gains = {}
                gl_all = {"a_q_norm": a_q_norm, "a_k_norm": a_k_norm, "b_q_norm": b_q_norm, "b_kv_norm": b_kv_norm,
                          "c_q_norm": c_q_norm, "c_k_norm": c_k_norm, "d_q_norm": d_q_norm, "d_k_norm": d_k_norm}
                for c in col_specs:
                    gn = c.get("gain")
                    if gn is not None and gn not in gains:
                        n = 512 if gn.startswith("b_") else 128
                        gt = P.sb(ps, gn, [128, n], F32)
                        P.load_bcast("sp", gt, gl_all[gn][:])
                        gains[gn] = gt
                cst = [P.sb(ps, "cs%d" % i, [128, 4, 128], F32) for i in range(2)]
                pq = [P.pb(ps, "pq%d" % i, [128, 512], F32) for i in range(4)]
                pT2 = [P.pb(ps, "pT2%d" % i, [128, 1024], BF16) for i in range(1)]
                pq_extra = P.pb(ps, "pqx", [128, 512], F32)
                ctr = {"pq": 0, "pt": 0}

                def next_pq():
                    ctr["pq"] += 1
                    return pq[ctr["pq"] % 4]

                def next_pt():
                    ctr["pt"] += 1
                    return pT2[0]

                extra = {}
                for c in col_specs:
                    if c["kind"] in ("mla_q", "mla_kv"):
                        extra["cT"] = P.sb(ps, "cnT", [128, 4, 128], BF16)
                        wn = "wuq" if c["kind"] == "mla_q" else "wukv"
                        wsrc = w_uq if c["kind"] == "mla_q" else w_ukv
                        ncol = 1536 if c["kind"] == "mla_q" else 2048
                        extra[wn] = P.sb(ps, wn, [128, 4, ncol], BF16)
                        S.dma("pool", extra[wn][:], wsrc[:, :].rearrange("(kc p) n -> p kc n", p=128), writes=[extra[wn]])
                chains = []
                for ci, c in enumerate(col_specs):
                    for s0 in range(0, c["n"], 512):
                        s1 = min(c["n"], s0 + 512)
                        w = s1 - s0
                        b = Ph()
                        b.c, b.s0, b.s1, b.w = c, s0, s1, w
                        b.pp = pq[len(chains) % 4]
                        b.ss = P.sb(ps, "css", [128, 16], F32)
                        if c["kind"] != "v":
                            b.qf = P.sb(ps, "cqf", [128, max(w, 128)], F32)
                            b.qsq = P.sb(ps, "cqsq", [128, max(w, 128)], F32)
                            b.hT = P.sb(ps, "chT", [128, 512], BF16)
                        b.hb = P.sb(ps, "chb", [128, max(w, 128)], BF16)
                        if c["kind"] == "mla_q":
                            b.qf2 = P.sb(ps, "cqf2", [128, 512], F32)
                            b.qsq2 = P.sb(ps, "cqsq2", [128, 512], F32)
                            b.hb2 = P.sb(ps, "chb2", [128, 512], BF16)
                            b.hT2 = P.sb(ps, "chT2", [128, 512], BF16)
                        if c["kind"] == "mla_kv":
                            b.hb2 = P.sb(ps, "chb2", [128, 512], BF16)
                            b.hT2 = P.sb(ps, "chT2", [128, 512], BF16)
                        chains.append(b)

                def g_norm_rope(b, qf, qsq, n, H, hd, gain, cs, dst, rope, norm):
                    v3 = lambda ap: ap.rearrange("p (h d) -> p h d", d=hd)
                    if norm:
                        S.op("dve", lambda e: e.tensor_tensor(qsq[:, 0:n], qf[:, 0:n], qf[:, 0:n], ALU.mult), reads=[qf], writes=[qsq])
                        yield
                        S.op("dve", lambda e: e.tensor_reduce(b.ss[:, 0:H], v3(qsq[:, 0:n]), AX.X, ALU.add), reads=[qsq], writes=[b.ss])
                        yield
                        S.op("act", lambda e: e.activation(out=b.ss[:, 0:H], in_=b.ss[:, 0:H], func=AF.Sqrt, bias=EPS, scale=1.0 / hd), reads=[b.ss], writes=[b.ss])
                        yield
                        S.op("dve", lambda e: e.reciprocal(b.ss[:, 0:H], b.ss[:, 0:H]), reads=[b.ss], writes=[b.ss])
                        yield
                        S.op("dve", lambda e: e.tensor_tensor(v3(qf[:, 0:n]), v3(qf[:, 0:n]), b.ss[:, 0:H].unsqueeze(2).to_broadcast([128, H, hd]), ALU.mult), reads=[qf, b.ss], writes=[qf])
                        yield
                        S.op("dve", lambda e: e.tensor_tensor(v3(qf[:, 0:n]), v3(qf[:, 0:n]), gain[:, 0:hd].unsqueeze(1).to_broadcast([128, H, hd]), ALU.mult), reads=[qf, gain], writes=[qf])
                        yield
                    if rope:
                        csT, ci_, si_ = cs
                        cosv = csT[:, ci_, :]
                        sinv = csT[:, si_, :]
                        q4 = hd // 4
                        x5 = lambda ap: ap.rearrange("p (h a b d) -> p h a b d", a=2, b=2, d=q4)
                        s4 = lambda ap: ap.rearrange("p (a b d) -> p a b d", a=2, b=2, d=q4)
                        for b0 in range(2):
                            S.op("dve", lambda e, b0=b0: e.tensor_tensor(x5(qsq[:, 0:n])[:, :, :, b0, :], x5(qf[:, 0:n])[:, :, :, 1 - b0, :],
                                                                        s4(sinv[:, 0:hd])[:, :, b0, :].unsqueeze(1).to_broadcast([128, H, 2, q4]), ALU.mult), reads=[qf, csT], writes=[qsq])
                            yield
                        S.op("dve", lambda e: e.tensor_tensor(v3(qf[:, 0:n]), v3(qf[:, 0:n]), cosv[:, 0:hd].unsqueeze(1).to_broadcast([128, H, hd]), ALU.mult), reads=[qf, csT], writes=[qf])
                        yield
                        S.op("dve", lambda e: e.tensor_tensor(dst[:, 0:n], qf[:, 0:n], qsq[:, 0:n], ALU.add), reads=[qf, qsq], writes=[dst])
                        yield
                    else:
                        S.op("dve", lambda e: e.tensor_copy(dst[:, 0:n], qf[:, 0:n]), reads=[qf], writes=[dst])
                        yield

                def g_tr_out(srcb, nblk, dstb, dram_ap):
                    pTt = next_pt()
                    for j in range(nblk):
                        S.op("pe", lambda e, j=j: e.transpose(pTt[:, j * 128:(j + 1) * 128], srcb[:, j * 128:(j + 1) * 128], ident[:]), reads=[srcb, ident], writes=[pTt])
                    S.op("act", lambda e: e.copy(dstb[:, 0:nblk * 128], pTt[:, 0:nblk * 128]), reads=[pTt], writes=[dstb])
                    yield
                    S.dma("pool", dram_ap, dstb[:, 0:nblk * 128].rearrange("p (h t) -> p h t", t=128), reads=[dstb])
                    yield

                def g_chain(t, b, uT, cs):
                    c, s0, s1, w = b.c, b.s0, b.s1, b.w
                    kind = c["kind"]
                    if c.get("own_only") and not (OWN0 <= t < OWN0 + NOWN):
                        return
                    tsl = slice(t * 128, (t + 1) * 128)
                    H = max(w // 128, 1)
                    h0 = s0 // 128
                    pp = b.pp
                    _lin(S, pp, pp[:, 0:w], uT, Wsb, c["off"] + s0, c["off"] + s1)
                    yield
                    if kind == "v":
                        S.op("act", lambda e: e.copy(b.hb[:, 0:w], pp[:, 0:w]), reads=[pp], writes=[b.hb])
                        yield
                        S.dma("pool", c["dst"][h0:h0 + H, tsl, :].rearrange("h p d -> p h d"), b.hb[:, 0:w].rearrange("p (h d) -> p h d", d=128), reads=[b.hb])
                        yield
                        return
                    S.op("act", lambda e: e.copy(b.qf[:, 0:w], pp[:, 0:w]), reads=[pp], writes=[b.qf])
                    yield
                    if kind == "heads":
                        yield from g_norm_rope(b, b.qf, b.qsq, w, H, 128, gains[c["gain"]], (cs, 0, 1), b.hb, c["rope"], True)
                        yield from g_tr_out(b.hb, H, b.hT, c["dst"][h0:h0 + H, :, tsl].rearrange("h p t -> p h t"))
                    elif kind == "kr":
                        yield from g_norm_rope(b, b.qf, b.qsq, 64, 1, 64, None, (cs, 2, 3), b.hb, True, False)
                        S.op("dve", lambda e: e.tensor_copy(b.hb[:, 64:128], b.hb[:, 0:64]), reads=[b.hb], writes=[b.hb])
                        yield
                        yield from g_tr_out(b.hb, 1, b.hT, c["dst"][:, tsl].rearrange("p (h t) -> p h t", h=1))
                    else:
                        yield from g_norm_rope(b, b.qf, b.qsq, 512, 1, 512, gains[c["gain"]], None, b.hb, False, True)
                        cnT = extra["cT"]
                        pTt = next_pt()
                        for j in range(4):
                            S.op("pe", lambda e, j=j: e.transpose(pTt[:, j * 128:(j + 1) * 128], b.hb[:, j * 128:(j + 1) * 128], ident[:]), reads=[b.hb, ident], writes=[pTt])
                        S.op("act", lambda e: e.copy(cnT[:].rearrange("p a b -> p (a b)"), pTt[:, 0:512]), reads=[pTt], writes=[cnT])
                        yield
                        W2 = extra["wuq" if kind == "mla_q" else "wukv"]
                        for hh in range(2):
                            pp2 = b.pp if hh == 0 else pq_extra
                            for h in range(4):
                                hd_ = hh * 4 + h
                                for kc in range(4):
                                    S.op("pe", lambda e, h=h, hd_=hd_, kc=kc: e.matmul(pp2[:, h * 128:(h + 1) * 128], W2[:, kc, hd_ * 128:(hd_ + 1) * 128], cnT[:, kc, :],
                                                                                      start=(kc == 0), stop=(kc == 3)), reads=[W2, cnT], writes=[pp2])
                            yield
                            hT_ = b.hT if hh == 0 else b.hT2
                            S.op("act", lambda e: e.copy(hT_[:, 0:512], pp2[:, 0:512]), reads=[pp2], writes=[hT_])
                            yield
                            S.dma("pool", c["dst"][hh * 4:hh * 4 + 4, :, tsl].rearrange("h p t -> p h t"), hT_[:, 0:512].rearrange("p (h t) -> p h t", t=128), reads=[hT_])
                            yield
                        if kind == "mla_q":
                            pp2 = b.pp
                            _lin(S, pp2, pp2[:, 0:512], cnT, W2, 1024, 1536, kcn=4)
                            yield
                            S.op("act", lambda e: e.copy(b.qf2[:, 0:512], pp2[:, 0:512]), reads=[pp2], writes=[b.qf2])
                            yield
                            yield from g_norm_rope(b, b.qf2, b.qsq2, 512, 8, 64, None, (cs, 2, 3), b.hb2, True, False)
                            yield from g_tr_out(b.hb2, 4, b.hT, c["dst_r"][:, :, tsl].rearrange("h p t -> p h t"))
                        else:
                            for hh in range(2):
                                pp2 = b.pp if hh == 0 else pq_extra
                                _lin(S, pp2, pp2[:, 0:512], cnT, W2, 1024 + hh * 512, 1024 + (hh + 1) * 512, kcn=4)
                                yield
                                hb_ = b.hb if hh == 0 else b.hb2
                                S.op("act", lambda e: e.copy(hb_[:, 0:512], pp2[:, 0:512]), reads=[pp2], writes=[hb_])
                                yield
                                S.dma("pool", c["dst_v"][hh * 4:hh * 4 + 4, tsl, :].rearrange("h p d -> p h d"), hb_[:, 0:512].rearrange("p (h d) -> p h d", d=128), reads=[hb_])
                                yield

                def g_front(t):
                    is_ctx = t >= ctx_from
                    xt = xin[t % 2]
                    ph = phs[t % 2]
                    Avec = ph0.Ac if is_ctx else ph0.Al
                    shvec = ph0.Sc if is_ctx else ph0.Sl
                    S.dma("sp", xt[:], src[t * 128:(t + 1) * 128, :], writes=[xt])
                    S.dma("sp", cst[t % 2][:], rope_t[:, t * 128:(t + 1) * 128, :].rearrange("f p d -> p f d"), writes=[cst[t % 2]])
                    yield
                    S.op("act", lambda e: e.activation(out=ph.junk[:], in_=xt[:], func=AF.Square, accum_out=ph.ss[:, 0:1]), reads=[xt], writes=[ph.junk, ph.ss])
                    yield
                    S.op("act", lambda e: e.activation(out=ph.ss[:, 0:1], in_=ph.ss[:, 0:1], func=AF.Sqrt, bias=EPS, scale=1.0 / D), reads=[ph.ss], writes=[ph.ss])
                    yield
                    S.op("dve", lambda e: e.reciprocal(ph.ss[:, 0:1], ph.ss[:, 0:1]), reads=[ph.ss], writes=[ph.ss])
                    yield
                    S.op("dve", lambda e: e.scalar_tensor_tensor(xt[:], xt[:], ph.ss[:, 0:1], Avec[:], ALU.mult, ALU.mult), reads=[xt, ph.ss, Avec], writes=[xt])
                    yield
                    S.op("dve", lambda e: e.tensor_tensor(ph.ub[:], xt[:], shvec[:], ALU.add), reads=[xt, shvec], writes=[ph.ub])
                    yield
                    for half in range(2):
                        p = ph.pT[half]
                        for j in range(8):
                            kc = half * 8 + j
                            S.op("pe", lambda e, j=j, kc=kc: e.transpose(p[:, j * 128:(j + 1) * 128], ph.ub[:, kc * 128:(kc + 1) * 128], ident[:]), reads=[ph.ub, ident], writes=[p])
                        yield
                        dst = ph.uT[:, half * 8:(half + 1) * 8, :].rearrange("p a b -> p (a b)")
                        if half == 0:
                            S.op("act", lambda e: e.copy(dst, p[:]), reads=[p], writes=[ph.uT])
                        else:
                            S.op("dve", lambda e: e.tensor_copy(dst, p[:]), reads=[p], writes=[ph.uT])
                        yield

                nch = len(chains)
                done = set()
                pending = []
                for t in range(nt):
                    pending.append(("F", t, -1))
                    for ci in range(nch):
                        pending.append(("C", t, ci))
                active = []

                def can_start(key):
                    kind, t, ci = key
                    if kind == "F":
                        if t >= 1 and ("F", t - 1, -1) not in done:
                            return False
                        if t >= 2:
                            for cj in range(nch):
                                if ("C", t - 2, cj) not in done:
                                    return False
                        return True
                    if ("F", t, -1) not in done:
                        return False
                    if t >= 1 and ("C", t - 1, ci) not in done:
                        return False
                    return True

                while pending or active:
                    for key in list(pending):
                        if can_start(key):
                            pending.remove(key)
                            kind, t, ci = key
                            gen = g_front(t) if kind == "F" else g_chain(t, chains[ci], phs[t % 2].uT, cst[t % 2])
                            active.append((key, gen))
                    for key, gen in list(active):
                        try:
                            next(gen)
                        except StopIteration:
                            active.remove((key, gen))
                            done.add(key)
                S.barrier()


        qkv_phase(0, NKT, xkv, ropek, NKT - 2, ab_w_in, [
            dict(kind="heads", c0=1024, n=256, gain="a_k_norm", rope=True, dst=KAT),
            dict(kind="v", c0=1280, n=256, dst=VA),
            dict(kind="mla_kv", c0=1536 + 512, n=512, gain="b_kv_norm", dst=KBT, dst_v=VB),
            dict(kind="kr", c0=1536 + 1024, n=64, dst=KRT),
        ], "g_mix_pre")
        qkv_phase(0, NQT, xq, ropeq, NQT - 2, ab_w_in, [
            dict(kind="heads", c0=0, n=1024, gain="a_q_norm", rope=True, dst=QAT),
            dict(kind="mla_q", c0=1536, n=512, gain="b_q_norm", dst=QBT, dst_r=QRT),
        ], "g_mix_pre")
        if cfg.dbg:
            for n_, src_ in [("d_KAT", KAT), ("d_KBT", KBT), ("d_VA", VA), ("d_VB", VB), ("d_QAT", QAT), ("d_QBT", QBT), ("d_QRT", QRT)]:
                S.dma("sp", dbg[n_][:, :, :], src_[:, :, :])
            S.dma("sp", dbg["d_KRT"][:, :], KRT[:, :])
            S.barrier()

        with ExitStack() as ps:
            ones = P.sb(ps, "ones", [128, 128], BF16)
            S.dma("pool", ones[:], ones_in[:, :], writes=[ones])
            Ksb = [P.sb(ps, "Ksb%d" % i, [128, NK], BF16) for i in range(2)]
            Vsb = [P.sb(ps, "Vsb%d" % i, [128, NKT, 128], BF16) for i in range(2)]
            KRsb = P.sb(ps, "KRsb", [128, NK], BF16)
            qsb = [P.sb(ps, "qsb%d" % i, [128, 512], BF16) for i in range(2)]
            qrsb = [P.sb(ps, "qrsb%d" % i, [128, 512], BF16) for i in range(2)]
            pt = [P.sb(ps, "pt%d" % i, [128, 512], BF16) for i in range(3)]
            rden = P.sb(ps, "rden", [128, 512], F32)
            osb = [P.sb(ps, "osb%d" % i, [128, 512], BF16) for i in range(2)]
            pS = [P.pb(ps, "pS%d" % i, [128, 512], F32) for i in range(3)]
            pO = [P.pb(ps, "pO%d" % i, [128, 512], F32) for i in range(2)]
            pD = [P.pb(ps, "pD%d" % i, [128, 512], F32) for i in range(2)]
            S.dma("sp", KRsb[:], KRT[:, :], writes=[KRsb])

            def load_kv(hh):
                kb = hh % 2
                if hh < 2:
                    S.dma("sp", Ksb[kb][:], KAT[hh, :, :], writes=[Ksb[kb]])
                    vsrc = VA[hh]
                else:
                    S.dma("sp", Ksb[kb][:], KBT[hh - 2, :, :], writes=[Ksb[kb]])
                    vsrc = VB[hh - 2]
                for t0_ in range(0, NKT, 11):
                    t1_ = min(NKT, t0_ + 11)
                    S.dma("sp", Vsb[kb][:, t0_:t1_, :], vsrc[t0_ * 128:t1_ * 128, :].rearrange("(t p) d -> p t d", p=128), writes=[Vsb[kb]])

            glist = []
            for hh in range(10):
                if hh < 2:
                    glist += [(hh, qt * 128, 512) for qt in range(NQT)]
                else:
                    glist += [(hh, t0 * 128, min(4, NQT - t0) * 128) for t0 in range(0, NQT, 4)]

            def load_q(gi):
                hh, tok0, N = glist[gi]
                g = gi % 2
                if hh < 2:
                    S.dma("sp", qsb[g][:].rearrange("p (h t) -> p h t", t=128), QAT[hh * 4:hh * 4 + 4, :, tok0:tok0 + 128].rearrange("h p t -> p h t"), writes=[qsb[g]])
                else:
                    h = hh - 2
                    S.dma("sp", qsb[g][:, 0:N], QBT[h, :, tok0:tok0 + N], writes=[qsb[g]])
                    S.dma("sp", qrsb[g][:, 0:N], QRT[h // 2, :, tok0:tok0 + N], writes=[qrsb[g]])

            load_kv(0)
            load_q(0)
            step = 0
            for gi, (hh, tok0, N) in enumerate(glist):
                g = gi % 2
                kb = hh % 2
                first_of_head = gi == 0 or glist[gi - 1][0] != hh
                if first_of_head and hh + 1 < 10:
                    load_kv(hh + 1)
                if gi + 1 < len(glist):
                    load_q(gi + 1)
                qs = qsb[g]
                qr = qrsb[g]
                scale = 128 ** -0.5 if hh < 2 else 192 ** -0.5
                if hh >= 2:
                    h = hh - 2
                    half = slice(0, 64) if h % 2 == 0 else slice(64, 128)
                kts = list(range(NKT - 2, NKT)) if tok0 >= (NQT - 2) * 128 else list(range(NKT))
                nk = len(kts)
                base = step
                step += nk

                def qk(i):
                    sp_ = pS[(base + i) % 3]
                    ksl = slice(kts[i] * 128, (kts[i] + 1) * 128)
                    if hh < 2:
                        S.op("pe", lambda e: e.matmul(sp_[:, 0:N], Ksb[kb][:, ksl], qs[:, 0:N], start=True, stop=True), reads=[Ksb[kb], qs], writes=[sp_])
                    else:
                        S.op("pe", lambda e: e.matmul(sp_[:, 0:N], Ksb[kb][:, ksl], qs[:, 0:N], start=True, stop=False), reads=[Ksb[kb], qs], writes=[sp_])
                        S.op("pe", lambda e: e.matmul(sp_[:, 0:N], KRsb[half, ksl], qr[half, 0:N], start=False, stop=True), reads=[KRsb, qr], writes=[sp_])

                def ex(i):
                    sp_ = pS[(base + i) % 3]
                    ptt = pt[(base + i) % 3]
                    S.op("act", lambda e: e.activation(out=ptt[:, 0:N], in_=sp_[:, 0:N], func=AF.Exp, scale=scale), reads=[sp_], writes=[ptt])

                def pv(i):
                    ptt = pt[(base + i) % 3]
                    S.op("pe", lambda e: e.matmul(pO[g][:, 0:N], Vsb[kb][:, kts[i], :], ptt[:, 0:N], start=(i == 0), stop=(i == nk - 1)), reads=[Vsb[kb], ptt], writes=[pO[g]])
                    S.op("pe", lambda e: e.matmul(pD[g][:, 0:N], ones[:], ptt[:, 0:N], start=(i == 0), stop=(i == nk - 1)), reads=[ones, ptt], writes=[pD[g]])

                qk(0)
                for i in range(nk):
                    if i + 1 < nk:
                        qk(i + 1)
                    ex(i)
                    pv(i)
                S.op("dve", lambda e: e.reciprocal(rden[:, 0:N], pD[g][:, 0:N]), reads=[pD[g]], writes=[rden])
                ob = osb[g]
                S.op("dve", lambda e: e.tensor_tensor(ob[:, 0:N], pO[g][:, 0:N], rden[:, 0:N], ALU.mult), reads=[pO[g], rden], writes=[ob])
                if hh < 2:
                    S.dma("pool", OT[hh * 4:hh * 4 + 4, :, tok0:tok0 + 128].rearrange("h p t -> p h t"), ob[:].rearrange("p (h t) -> p h t", t=128), reads=[ob])
                else:
                    S.dma("pool", OT[8 + h, :, tok0:tok0 + N], ob[:, 0:N], reads=[ob])
            S.barrier()
        if cfg.dbg:
            S.dma("sp", dbg["d_OT"][:, :, :], OT[:, :, :])
            S.barrier()

        def out_phase(layer, tiles, OTd, Wout, resid, dst):
            with ExitStack() as ps:
                ph = Ph()
                ph.gtmp = P.sb(ps, "gtmp", [128, D], F32)
                P.mod_vecs(ps, ph, layer, 0, [("Gl", "G", 2, "g_mix_post")])
                if layer == 0:
                    P.mod_vecs(ps, ph, layer, 1, [("Gc", "G", 2, "g_mix_post")])
                Wsb = P.sb(ps, "Wo", [128, KC, D], BF16)
                S.dma("pool", Wsb[:], Wout[:, :].rearrange("(kc p) n -> p kc n", p=128), writes=[Wsb])
                oT = [P.sb(ps, "oT%d" % i, [128, 16, 128], BF16) for i in range(2)]
                xr = [P.sb(ps, "xr%d" % i, [128, D], F32) for i in range(2)]
                of = P.sb(ps, "of", [128, D], F32)
                junk = P.sb(ps, "junk", [128, D], BF16)
                ss = P.sb(ps, "ss", [128, 16], F32)
                pp = [P.pb(ps, "po%d" % i, [128, 512], F32) for i in range(8)]
                for i, t in enumerate(tiles):
                    is_ctx = (layer == 0) and t >= NQT - 2
                    o_ = oT[i % 2]
                    x_ = xr[i % 2]
                    tsl = slice(t * 128, (t + 1) * 128)
                    S.dma("sp", o_[:], OTd[:, :, tsl].rearrange("h p t -> p h t"), writes=[o_])
                    S.dma("sp", x_[:], resid[tsl, :], writes=[x_])
                    for cg in range(4):
                        pb_ = pp[(i % 2) * 4 + cg]
                        _lin(S, pb_, pb_[:, :], o_, Wsb, cg * 512, (cg + 1) * 512)
                        if cg % 2 == 0:
                            S.op("act", lambda e, pb_=pb_, cg=cg: e.copy(of[:, cg * 512:(cg + 1) * 512], pb_[:, :]), reads=[pb_], writes=[of])
                        else:
                            S.op("dve", lambda e, pb_=pb_, cg=cg: e.tensor_copy(of[:, cg * 512:(cg + 1) * 512], pb_[:, :]), reads=[pb_], writes=[of])
                    S.op("act", lambda e: e.activation(out=junk[:], in_=of[:], func=AF.Square, accum_out=ss[:, 0:1]), reads=[of], writes=[junk, ss])
                    P.rstd_from_ss(ss, slice(0, 1), D)
                    G = ph.Gc if is_ctx else ph.Gl
                    S.op("dve", lambda e, G=G: e.scalar_tensor_tensor(of[:], of[:], ss[:, 0:1], G[:], ALU.mult, ALU.mult), reads=[of, ss, G], writes=[of])
                    S.op("dve", lambda e, x_=x_: e.tensor_tensor(x_[:], of[:], x_[:], ALU.add), reads=[of, x_], writes=[x_])
                    S.dma("pool", dst[tsl, :], x_[:], reads=[x_])
                S.barrier()

        def ffn_phase(layer, tiles, src, dst_fn):
            JB = 2
            NB = FC // JB
            with ExitStack() as ps:
                ph = Ph()
                P.front_alloc(ps, ph)
                hx = [P.sb(ps, "hx%d" % i, [128, D], F32) for i in range(2)]
                ph.gtmp = hx[0]
                MV = [("A", "A", 4, "g_ffn_pre"), ("Sh", "S", 3, None), ("G", "G", 5, "g_ffn_post")]
                P.mod_vecs(ps, ph, layer, 0, MV)

                def switch_ctx(attrs):
                    for attr, kind, k, gname in MV:
                        if attr not in attrs:
                            continue
                        t_ = getattr(ph, attr)
                        P.load_bcast("sp", t_, P.modv[layer, 1, k * D:(k + 1) * D])
                        if kind != "S":
                            g_ = ph.gtmp
                            P.load_bcast("sp", g_, P.din[gname][layer, :])
                            if kind == "A":
                                S.op("dve", lambda e, t_=t_, g_=g_: e.scalar_tensor_tensor(t_[:], t_[:], 1.0, g_[:], ALU.add, ALU.mult), reads=[t_, g_], writes=[t_])
                            else:
                                S.op("dve", lambda e, t_=t_, g_=g_: e.tensor_tensor(t_[:], t_[:], g_[:], ALU.mult), reads=[t_, g_], writes=[t_])

                facc = [P.sb(ps, "facc%d" % i, [128, D], F32) for i in range(4)]
                u2T = [P.sb(ps, "u2T%d" % i, [128, KC, 512], BF16) for i in range(2)]
                wg = [P.sb(ps, "wg%d" % i, [128, KC, JB * 128], BF16) for i in range(2)]
                wu = [P.sb(ps, "wu%d" % i, [128, KC, JB * 128], BF16) for i in range(2)]
                wd = [P.sb(ps, "wd%d" % i, [128, JB, D], BF16) for i in range(2)]
                aT = [P.sb(ps, "aT%d" % i, [128, JB, 512], BF16) for i in range(2)]
                sg = [P.sb(ps, "sg%d" % i, [128, 512], F32) for i in range(2)]
                pg = [P.pb(ps, "pg%d" % i, [128, 512], F32) for i in range(2)]
                pu = [P.pb(ps, "pu%d" % i, [128, 512], F32) for i in range(2)]
                pd = [P.pb(ps, "pd%d" % i, [128, 512], F32) for i in range(2)]
                groups = []
                i = 0
                while i < len(tiles):
                    ctxf = (layer == 0) and tiles[i] >= NQT - 2
                    j = i
                    while j < len(tiles) and j - i < 4 and (((layer == 0) and tiles[j] >= NQT - 2) == ctxf):
                        j += 1
                    groups.append((tiles[i:j], ctxf))
                    i = j
                state = {"pro_ctx": False, "epi_ctx": False, "hxi": 0}

                def prologue(gidx):
                    gt, ctxf = groups[gidx]
                    if ctxf and not state["pro_ctx"]:
                        state["pro_ctx"] = True
                        switch_ctx(("A", "Sh"))
                    for ti, t in enumerate(gt):
                        x_ = hx[state["hxi"] % 2]
                        state["hxi"] += 1
                        S.dma("sp", x_[:], src[t * 128:(t + 1) * 128, :], writes=[x_])
                        P.front(ph, x_, ph.A, ph.Sh)
                        S.op("dve", lambda e, ti=ti: e.tensor_copy(u2T[gidx % 2][:, :, ti * 128:(ti + 1) * 128], ph.uT[:]), reads=[ph.uT], writes=[u2T[gidx % 2]])

                blk = 0
                gstep = 0
                dstep = 0
                prologue(0)
                for gidx, (gt, ctxf) in enumerate(groups):
                    G = len(gt) * 128
                    uT_ = u2T[gidx % 2]
                    bpar = {}

                    def gu(bi):
                        nonlocal blk, gstep, dstep
                        b = blk % 2
                        blk += 1
                        bpar[bi] = b
                        c0 = bi * JB * 128
                        S.dma("pool", wg[b][:], w_gate[layer, :, c0:c0 + JB * 128].rearrange("(kc p) n -> p kc n", p=128), writes=[wg[b]])
                        S.dma("pool", wu[b][:], w_up[layer, :, c0:c0 + JB * 128].rearrange("(kc p) n -> p kc n", p=128), writes=[wu[b]])
                        S.dma("pool", wd[b][:], w_down[layer, c0:c0 + JB * 128, :].rearrange("(j p) n -> p j n", p=128), writes=[wd[b]])
                        for jj in range(JB):
                            pgt = pg[gstep % 2]
                            sgt = sg[gstep % 2]
                            put = pu[gstep % 2]
                            gstep += 1
                            for kc in range(KC):
                                S.op("pe", lambda e, kc=kc: e.matmul(pgt[:, 0:G], wg[b][:, kc, jj * 128:(jj + 1) * 128], uT_[:, kc, 0:G], start=(kc == 0), stop=(kc == KC - 1)), reads=[wg[b], uT_], writes=[pgt])
                                if kc % 4 == 3:
                                    yield
                            S.op("act", lambda e: e.activation(out=sgt[:, 0:G], in_=pgt[:, 0:G], func=AF.Silu), reads=[pgt], writes=[sgt])
                            for kc in range(KC):
                                S.op("pe", lambda e, kc=kc: e.matmul(put[:, 0:G], wu[b][:, kc, jj * 128:(jj + 1) * 128], uT_[:, kc, 0:G], start=(kc == 0), stop=(kc == KC - 1)), reads=[wu[b], uT_], writes=[put])
                                if kc % 4 == 3 and kc != KC - 1:
                                    yield
                            S.op("dve", lambda e: e.tensor_tensor(aT[b][:, jj, 0:G], put[:, 0:G], sgt[:, 0:G], ALU.mult), reads=[put, sgt], writes=[aT[b]])
                            yield

                    def down(bi):
                        nonlocal dstep
                        b = bpar[bi]
                        for ti in range(len(gt)):
                            for cg in range(4):
                                pdt = pd[dstep % 2]
                                dstep += 1
                                for jj in range(JB):
                                    S.op("pe", lambda e, jj=jj: e.matmul(pdt[:, :], aT[b][:, jj, ti * 128:(ti + 1) * 128], wd[b][:, jj, cg * 512:(cg + 1) * 512], start=(jj == 0), stop=(jj == JB - 1)), reads=[aT[b], wd[b]], writes=[pdt])
                                fa = facc[ti]
                                if bi == 0:
                                    S.op("act", lambda e: e.copy(fa[:, cg * 512:(cg + 1) * 512], pdt[:, :]), reads=[pdt], writes=[fa])
                                else:
                                    S.op("dve", lambda e: e.tensor_tensor(fa[:, cg * 512:(cg + 1) * 512], pdt[:, :], fa[:, cg * 512:(cg + 1) * 512], ALU.add), reads=[pdt, fa], writes=[fa])
                                yield

                    run_rr([gu(0)])
                    for bi in range(NB):
                        gens = []
                        if bi + 1 < NB:
                            gens.append(gu(bi + 1))
                        gens.append(down(bi))
                        run_rr(gens)
                        if bi == 3 and gidx + 1 < len(groups):
                            prologue(gidx + 1)
                    if ctxf and not state["epi_ctx"]:
                        state["epi_ctx"] = True
                        switch_ctx(("G",))
                    for ti, t in enumerate(gt):
                        fa = facc[ti]
                        x_ = hx[state["hxi"] % 2]
                        state["hxi"] += 1
                        S.dma("sp", x_[:], src[t * 128:(t + 1) * 128, :], writes=[x_])
                        S.op("act", lambda e: e.activation(out=ph.junk[:], in_=fa[:], func=AF.Square, accum_out=ph.ss[:, 4:5]), reads=[fa], writes=[ph.junk, ph.ss])
                        P.rstd_from_ss(ph.ss, slice(4, 5), D)
                        S.op("dve", lambda e: e.scalar_tensor_tensor(fa[:], fa[:], ph.ss[:, 4:5], ph.G[:], ALU.mult, ALU.mult), reads=[fa, ph.ss, ph.G], writes=[fa])
                        S.op("dve", lambda e: e.tensor_tensor(x_[:], fa[:], x_[:], ALU.add), reads=[fa, x_], writes=[x_])
                        S.dma("sp", dst_fn(t), x_[:], reads=[x_])
                S.barrier()


        out_phase(0, list(range(NQT)), OT, ab_w_out, xq, hmid)
        if cfg.dbg:
            S.dma("sp", dbg["d_hmid"][:, :], hmid[:, :])
            S.barrier()
        ffn_phase(0, list(range(NQT)), hmid, lambda t: h1[t * 128:(t + 1) * 128, :])
        if cfg.dbg:
            S.dma("sp", dbg["d_h1"][:, :], h1[:, :])
            S.barrier()

        qkv_phase(1, NQT, h1, ropeq, NQT - 2, cd_w_in, [
            dict(kind="heads", c0=0, n=1024, gain="c_q_norm", rope=True, dst=QCT, own_only=True),
            dict(kind="heads", c0=1024, n=256, gain="c_k_norm", rope=True, dst=KCT),
            dict(kind="v", c0=1280, n=256, dst=VC),
        ], "g_mix_pre")
        qkv_phase(1, NQT, h1, ropeq, NQT - 2, cd_w_in, [
            dict(kind="heads", c0=1536, n=1024, gain="d_q_norm", rope=False, dst=QDT, own_only=True),
            dict(kind="heads", c0=2560, n=1024, gain="d_k_norm", rope=False, dst=KDT),
        ], "g_mix_pre")
        qkv_phase(1, NQT, h1, ropeq, NQT - 2, cd_w_in, [
            dict(kind="v", c0=3584, n=1024, dst=VD),
        ], "g_mix_pre")

        with ExitStack() as ps:
            ones = P.sb(ps, "ones", [128, 128], BF16)
            S.dma("pool", ones[:], ones_in[:, :], writes=[ones])
            KCs = P.sb(ps, "KCs", [128, 2, NQ], BF16)
            VCs = P.sb(ps, "VCs", [128, 2, NQT, 128], BF16)
            KDs = P.sb(ps, "KDs", [128, 8, NQ], BF16)
            VDs = P.sb(ps, "VDs", [128, 8, NQT, 128], BF16)
            S.dma("sp", KCs[:], KCT[:, :, :].rearrange("h p t -> p h t"), writes=[KCs])
            S.dma("sp", KDs[:], KDT[:, :, :].rearrange("h p t -> p h t"), writes=[KDs])
            for g in range(2):
                S.dma("sp", VCs[:, g, :, :], VC[g, :, :].rearrange("(t p) d -> p t d", p=128), writes=[VCs])
            for h in range(8):
                for t0_ in range(0, NQT, 11):
                    t1_ = min(NQT, t0_ + 11)
                    S.dma("sp", VDs[:, h, t0_:t1_, :], VD[h, t0_ * 128:t1_ * 128, :].rearrange("(t p) d -> p t d", p=128), writes=[VDs])
            esink = P.sb(ps, "esink", [128, 8], F32)
            P.load_bcast("sp", esink, c_sink[:])
            S.op("act", lambda e: e.activation(out=esink[:], in_=esink[:], func=AF.Exp), reads=[esink], writes=[esink])
            mC = P.sb(ps, "mC", [128, 3, 3, 128], F32)
            S.dma("sp", mC[:].rearrange("p v s q -> p (v s) q"), maskC[:, :, :, :].rearrange("v s k q -> k (v s) q"), writes=[mC])
            bD = [P.sb(ps, "bD%d" % i, [128, 6, 128], F32) for i in range(3)]
            qsb = [P.sb(ps, "qsb%d" % i, [128, 512], BF16) for i in range(3)]
            pt = [P.sb(ps, "pt%d" % i, [128, 512], BF16) for i in range(3)]
            sb_ = [P.sb(ps, "sbias%d" % i, [128, 512], F32) for i in range(3)]
            rden = [P.sb(ps, "rden%d" % i, [128, 512], F32) for i in range(2)]
            osb = [P.sb(ps, "osb%d" % i, [128, 512], BF16) for i in range(2)]
            pS = [P.pb(ps, "pS%d" % i, [128, 512], F32) for i in range(3)]
            pO = [P.pb(ps, "pO%d" % i, [128, 512], F32) for i in range(2)]
            pD = [P.pb(ps, "pD%d" % i, [128, 512], F32) for i in range(2)]
            scale = 128 ** -0.5
            glist = []
            for i in range(NOWN):
                glist += [("C", i, g) for g in range(2)]
                glist += [("D", i, h) for h in range(8)]

            def ginfo(gi):
                kind, i, hg = glist[gi]
                e_ = OWN0 + i
                if kind == "C":
                    keys = [(NQT - 2, None), (NQT - 1, None), (e_ - 1, 0), (e_, 1), (e_ + 1, 2)]
                    return kind, i, hg, e_, keys, 512
                lo = -3 if i == NOWN - 1 else -2
                hi = 3 if i == 0 else 2
                rel = list(range(lo, hi + 1))
                keys = [(NQT - 2, None), (NQT - 1, None)] + [(e_ + r_, si) for si, r_ in enumerate(rel)]
                return kind, i, hg, e_, keys, 128

            def load_q(gi):
                kind, i, hg, e_, keys, N = ginfo(gi)
                tsl = slice(e_ * 128, (e_ + 1) * 128)
                qs = qsb[gi % 3]
                if kind == "C":
                    S.dma("sp", qs[:].rearrange("p (h t) -> p h t", t=128), QCT[hg * 4:hg * 4 + 4, :, tsl].rearrange("h p t -> p h t"), writes=[qs])
                else:
                    vD = 0 if i == 0 else (1 if i == 1 else (3 if i == NOWN - 2 else (4 if i == NOWN - 1 else 2)))
                    bt = bD[gi % 3]
                    nsl = len(keys) - 2
                    S.dma("sp", qs[:, 0:128], QDT[hg, :, tsl], writes=[qs])
                    S.dma("sp", bt[:, 0:nsl, :], biasD[vD, hg, 0:nsl, :, :].rearrange("s k q -> k s q"), writes=[bt])

            load_q(0)
            load_q(1)
            step = 0
            for gi in range(len(glist)):
                kind, i, hg, e_, keys, N = ginfo(gi)
                if gi + 2 < len(glist):
                    load_q(gi + 2)
                tsl = slice(e_ * 128, (e_ + 1) * 128)
                gg = gi % 2
                qs = qsb[gi % 3]
                bt = bD[gi % 3]
                vC = 0 if i == 0 else (2 if i == NOWN - 1 else 1)
                nk = len(keys)
                base = step
                step += nk

                def qk(j):
                    kt, slot = keys[j]
                    sp_ = pS[(base + j) % 3]
                    ksl = slice(kt * 128, (kt + 1) * 128)
                    if kind == "C":
                        S.op("pe", lambda e: e.matmul(sp_[:, 0:N], KCs[:, hg, ksl], qs[:, 0:N], start=True, stop=True), reads=[KCs, qs], writes=[sp_])
                    else:
                        S.op("pe", lambda e: e.matmul(sp_[:, 0:N], KDs[:, hg, ksl], qs[:, 0:N], start=True, stop=True), reads=[KDs, qs], writes=[sp_])

                def ex(j):
                    kt, slot = keys[j]
                    sp_ = pS[(base + j) % 3]
                    ptt = pt[(base + j) % 3]
                    if slot is None:
                        S.op("act", lambda e: e.activation(out=ptt[:, 0:N], in_=sp_[:, 0:N], func=AF.Exp, scale=scale), reads=[sp_], writes=[ptt])
                        return
                    sbt = sb_[(base + j) % 3]
                    if kind == "C":
                        S.op("dve", lambda e: e.scalar_tensor_tensor(sbt[:].rearrange("p (h t) -> p h t", t=128), sp_[:, :].rearrange("p (h t) -> p h t", t=128), scale,
                                                                      mC[:, vC, slot, :].unsqueeze(1).to_broadcast([128, 4, 128]), ALU.mult, ALU.add), reads=[sp_, mC], writes=[sbt])
                    else:
                        S.op("dve", lambda e: e.scalar_tensor_tensor(sbt[:, 0:128], sp_[:, 0:128], scale, bt[:, slot, :], ALU.mult, ALU.add), reads=[sp_, bt], writes=[sbt])
                    S.op("act", lambda e: e.activation(out=ptt[:, 0:N], in_=sbt[:, 0:N], func=AF.Exp), reads=[sbt], writes=[ptt])

                def pv(j):
                    kt, slot = keys[j]
                    ptt = pt[(base + j) % 3]
                    vv = VCs[:, hg, kt, :] if kind == "C" else VDs[:, hg, kt, :]
                    S.op("pe", lambda e: e.matmul(pO[gg][:, 0:N], vv, ptt[:, 0:N], start=(j == 0), stop=(j == nk - 1)), reads=[VCs if kind == "C" else VDs, ptt], writes=[pO[gg]])
                    S.op("pe", lambda e: e.matmul(pD[gg][:, 0:N], ones[:], ptt[:, 0:N], start=(j == 0), stop=(j == nk - 1)), reads=[ones, ptt], writes=[pD[gg]])

                qk(0)
                for j in range(nk):
                    if j + 1 < nk:
                        qk(j + 1)
                    ex(j)
                    pv(j)
                rd = rden[gg]
                ob = osb[gg]
                if kind == "C":
                    S.op("dve", lambda e: e.tensor_tensor(rd[:].rearrange("p (h t) -> p h t", t=128), pD[gg][:, :].rearrange("p (h t) -> p h t", t=128),
                                                          esink[:, hg * 4:hg * 4 + 4].unsqueeze(2).to_broadcast([128, 4, 128]), ALU.add), reads=[pD[gg], esink], writes=[rd])
                    S.op("dve", lambda e: e.reciprocal(rd[:], rd[:]), reads=[rd], writes=[rd])
                    S.op("dve", lambda e: e.tensor_tensor(ob[:], pO[gg][:, :], rd[:], ALU.mult), reads=[pO[gg], rd], writes=[ob])
                    S.dma("pool", OT1[hg * 4:hg * 4 + 4, :, tsl].rearrange("h p t -> p h t"), ob[:].rearrange("p (h t) -> p h t", t=128), reads=[ob])
                else:
                    S.op("dve", lambda e: e.reciprocal(rd[:, 0:128], pD[gg][:, 0:128]), reads=[pD[gg]], writes=[rd])
                    S.op("dve", lambda e: e.tensor_tensor(ob[:, 0:128], pO[gg][:, 0:128], rd[:, 0:128], ALU.mult), reads=[pO[gg], rd], writes=[ob])
                    S.dma("pool", OT1[8 + hg, :, tsl], ob[:, 0:128], reads=[ob])
            S.barrier()
        if cfg.dbg:
            S.dma("sp", dbg["d_OT1"][:, :, :], OT1[:, :, :])
            S.barrier()


        own = list(range(OWN0, OWN0 + NOWN))
        out_phase(1, own, OT1, cd_w_out, h1, hmid1)
        if cfg.dbg:
            S.dma("sp", dbg["d_hmid1"][:, :], hmid1[:, :])
            S.barrier()
        ffn_phase(1, own, hmid1, lambda t: yout[(t - OWN0) * 128:(t - OWN0 + 1) * 128, :])
        S.barrier()
        P.nins = S.nins
    return P


GRID_W = 64
L_SEQ = 8192


def _rope_tables(tok, dim):
    tok = np.asarray(tok)
    valid = tok >= 0
    t = np.where(valid, tok, 0)
    row = (t // GRID_W).astype(np.float32)
    col = (t % GRID_W).astype(np.float32)
    half = dim // 2
    inv = (np.float32(10000.0) ** (-np.arange(0, half, 2, dtype=np.float32) / np.float32(half))).astype(np.float32)
    ar = row[:, None] * inv[None, :]
    ac = col[:, None] * inv[None, :]
    ang = np.concatenate([ar, ar, ac, ac], axis=-1).astype(np.float32)
    cos = np.cos(ang).astype(np.float32)
    sin = np.sin(ang).astype(np.float32)
    q = dim // 4
    sgn = np.concatenate([-np.ones(q), np.ones(q), -np.ones(q), np.ones(q)]).astype(np.float32)
    sin = sin * sgn[None, :]
    cos[~valid] = 1.0
    sin[~valid] = 0.0
    return cos, sin


def _rope_pack(tok):
    n = len(tok)
    out = np.zeros((4, n, 128), np.float32)
    c, s = _rope_tables(tok, 128)
    out[0], out[1] = c, s
    c, s = _rope_tables(tok, 64)
    out[2, :, :64], out[3, :, :64] = c, s
    return out


def _mask_c(cc, nown=16):
    kk = np.arange(128)[:, None]
    qq = np.arange(128)[None, :]
    m = np.zeros((3, 3, 128, 128), np.float32)
    for v in range(3):
        m[v, 0] = np.where(kk >= qq, 0.0, NEG)
        m[v, 2] = np.where(kk <= qq, 0.0, NEG)
    if cc == 0:
        m[0, 0] = NEG
    if cc == 3:
        m[2, 2] = NEG
    return m


def _bias_d(cc, rpb, nown=16):
    out = np.full((5, 8, 6, 128, 128), NEG, np.float32)
    rows = L_SEQ // GRID_W
    var_tiles = [0, 1, 2, nown - 2, nown - 1]
    for v, i in enumerate(var_tiles):
        lo = -3 if i == nown - 1 else -2
        hi = 3 if i == 0 else 2
        qtok = 2048 * cc + 128 * i + np.arange(128)
        qr = qtok // GRID_W
        qc = qtok % GRID_W
        rs = np.clip(qr - 4, 0, rows - 8)
        cs = np.clip(qc - 8, 0, GRID_W - 16)
        for si, r_ in enumerate(range(lo, hi + 1)):
            ktok = 2048 * cc + 128 * (i + r_) + np.arange(128)
            inb = (ktok >= 0) & (ktok < L_SEQ)
            kr = np.where(inb, ktok, 0) // GRID_W
            kc = np.where(inb, ktok, 0) % GRID_W
            ok = inb[:, None] & (kr[:, None] >= rs[None, :]) & (kr[:, None] < rs[None, :] + 8) & (kc[:, None] >= cs[None, :]) & (kc[:, None] < cs[None, :] + 16)
            di = np.clip(kr[:, None] - qr[None, :] + 7, 0, 14)
            dj = np.clip(kc[:, None] - qc[None, :] + 15, 0, 30)
            g = rpb[:, di, dj]
            out[v, :, si] = np.where(ok[None], g, NEG)
    return out


def _host_inputs(inputs, cfg=None):
    f = lambda a: np.ascontiguousarray(np.asarray(a, dtype=np.float32))
    x = f(inputs["x"])
    ctx = f(inputs["ctx"])
    c = f(inputs["c"])
    c_ctx = f(inputs["c_ctx"])
    shared = {
        "ident": np.eye(128, dtype=np.float32), "ones": np.ones((128, 128), np.float32),
        "w_mod": f(inputs["w_mod"]), "b_mod": f(inputs["b_mod"]),
        "g_mix_pre": f(inputs["g_mix_pre"]), "g_mix_post": f(inputs["g_mix_post"]),
        "g_ffn_pre": f(inputs["g_ffn_pre"]), "g_ffn_post": f(inputs["g_ffn_post"]),
        "w_gate": f(inputs["w_gate"]), "w_up": f(inputs["w_up"]), "w_down": f(inputs["w_down"]),
        "ab_w_in": f(inputs["ab_w_in"][0]), "ab_w_out": f(inputs["ab_w_out"][0]),
        "a_q_norm": f(inputs["a_q_norm"][0]), "a_k_norm": f(inputs["a_k_norm"][0]),
        "b_q_norm": f(inputs["b_q_norm"][0]), "b_kv_norm": f(inputs["b_kv_norm"][0]),
        "cd_w_in": f(inputs["cd_w_in"][0]), "cd_w_out": f(inputs["cd_w_out"][0]),
        "c_q_norm": f(inputs["c_q_norm"][0]), "c_k_norm": f(inputs["c_k_norm"][0]), "c_sink": f(inputs["c_sink"][0]),
        "d_q_norm": f(inputs["d_q_norm"][0]), "d_k_norm": f(inputs["d_k_norm"][0]),
    }
    wuq = f(inputs["b_w_uq"][0]).reshape(512, 8, 192)
    shared["w_uq"] = np.ascontiguousarray(np.concatenate([wuq[:, :, :128].reshape(512, 1024), wuq[:, :, 128:].reshape(512, 512)], axis=1))
    wukv = f(inputs["b_w_ukv"][0]).reshape(512, 8, 256)
    shared["w_ukv"] = np.ascontiguousarray(np.concatenate([wukv[:, :, :128].reshape(512, 1024), wukv[:, :, 128:].reshape(512, 1024)], axis=1))
    rpb = f(inputs["d_rpb"][0])
    maps = []
    ktok = np.concatenate([np.arange(L_SEQ), -np.ones(256, np.int64)])
    ropek = _rope_pack(ktok)
    for core in range(8):
        b, cc = core // 4, core % 4
        m = dict(shared)
        tok = np.arange(2048 * cc - 256, 2048 * cc + 2048 + 256)
        inb = (tok >= 0) & (tok < L_SEQ)
        tokc = np.clip(tok, 0, L_SEQ - 1)
        m["xq"] = np.ascontiguousarray(np.concatenate([x[b][tokc], ctx[b]], axis=0))
        m["xkv"] = np.ascontiguousarray(np.concatenate([x[b], ctx[b]], axis=0))
        m["cT"] = np.ascontiguousarray(np.stack([c[b].reshape(KC, 128).T, c_ctx.reshape(KC, 128).T], axis=-1))
        m["ropeq"] = _rope_pack(np.concatenate([np.where(inb, tok, -1), -np.ones(256, np.int64)]))
        m["ropek"] = ropek
        m["maskC"] = _mask_c(cc)
        m["biasD"] = _bias_d(cc, rpb)
        maps.append(m)
    return maps


_CACHE = {}


def kernel(**inputs):
    cfg = Cfg()
    if "prog" not in _CACHE:
        _CACHE["prog"] = build(cfg)
    P = _CACHE["prog"]
    maps = _host_inputs(inputs)
    res = run_bass_kernel_spmd(P.nc, maps, core_ids=list(range(8)))
    out = np.zeros((2, L_SEQ, D), np.float32)
    for core in range(8):
        b, cc = core // 4, core % 4
        out[b, 2048 * cc:2048 * cc + 2048] = np.asarray(res.results[core]["y"], dtype=np.float32)
    return out
```

```python
import numpy as np
from contextlib import ExitStack
import concourse.bass as bass
import concourse.mybir as mybir
from concourse.bass_utils import run_bass_kernel_spmd

F32 = mybir.dt.float32
BF16 = mybir.dt.bfloat16
AF = mybir.ActivationFunctionType
ALU = mybir.AluOpType
AX = mybir.AxisListType
ND = 12

D = 2048
KC = 16
DFF = 5632
FC = 44
EPS = 1e-6
NEG = -1e30


class Res:
    __slots__ = ("w", "rs", "name")

    def __init__(self, name=""):
        self.w = None
        self.rs = {}
        self.name = name


class T:
    def __init__(self, t, name):
        self.t = t
        self.r = Res(name)

    def __getitem__(self, k):
        return self.t[k]


def _res(x):
    return x.r if isinstance(x, T) else x


class Sch:
    def __init__(self, nc, es):
        self.nc = nc
        self.eng = {"pe": nc.tensor, "act": nc.scalar, "dve": nc.vector, "pool": nc.gpsimd, "sp": nc.sync}
        self.semobj = {}
        self.cnt = {}
        for k in ["pe", "act", "dve", "pool"]:
            self.semobj[k] = es.enter_context(nc.semaphore("s_" + k))
            self.cnt[k] = 0
        self.known = {k: {} for k in self.eng}
        self.dcnt = {}
        self.dnext = {}
        for q in ["sp", "pool"]:
            for i in range(ND):
                self.semobj[(q, i)] = es.enter_context(nc.semaphore("d_%s_%d" % (q, i)))
                self.dcnt[(q, i)] = 0
            self.dnext[q] = 0
        self.nins = 0

    def _wait(self, e, deps):
        need = {}
        kn = self.known[e]
        for d in deps:
            if d is None:
                continue
            k, v = d
            if kn.get(k, 0) >= v:
                continue
            if need.get(k, 0) < v:
                need[k] = v
        for k, v in need.items():
            self.eng[e].wait_ge(self.semobj[k], v)
            kn[k] = v
            self.nins += 1

    @staticmethod
    def _deps(reads, writes):
        deps = []
        for r in reads:
            deps.append(_res(r).w)
        for w in writes:
            w = _res(w)
            deps.append(w.w)
            deps.extend(w.rs.items())
        return deps

    @staticmethod
    def _mark(ev, reads, writes):
        k, v = ev
        for r in reads:
            r = _res(r)
            if r.rs.get(k, 0) < v:
                r.rs[k] = v
        for w in writes:
            w = _res(w)
            w.w = ev
            w.rs = {}

    def op(self, e, fn, reads=(), writes=()):
        deps = self._deps(reads, writes)
        if e == "pe":
            deps = [d for d in deps if d is not None and d[0] != "pe"]
        self._wait(e, deps)
        ins = fn(self.eng[e])
        self.cnt[e] += 1
        ins.then_inc(self.semobj[e], 1)
        self.nins += 1
        ev = (e, self.cnt[e])
        self._mark(ev, reads, writes)
        return ev

    def dma(self, q, out, in_, reads=(), writes=()):
        i = self.dnext[q]
        self.dnext[q] = (i + 1) % ND
        key = (q, i)
        deps = self._deps(reads, writes)
        if self.dcnt[key] > 0:
            deps.append((key, self.dcnt[key]))
        self._wait(q, deps)
        ins = self.eng[q].dma_start(out=out, in_=in_)
        self.dcnt[key] += 16
        ins.then_inc(self.semobj[key], 16)
        self.nins += 1
        ev = (key, self.dcnt[key])
        self._mark(ev, reads, writes)
        return ev

    def barrier(self):
        deps = [(k, v) for k, v in self.cnt.items() if v > 0]
        deps += [(k, v) for k, v in self.dcnt.items() if v > 0]
        for e in ("pe", "act", "dve", "pool", "sp"):
            self._wait(e, deps)


class Cfg:
    def __init__(self, nqt=22, nkt=66, own0=2, nown=16, dbg=False):
        self.NQT = nqt
        self.NKT = nkt
        self.OWN0 = own0
        self.NOWN = nown
        self.dbg = dbg


class Prog:
    def __init__(self, cfg):
        self.cfg = cfg
        self.nc = bass.Bass("TRN2", target_bir_lowering=False)
        self.es = ExitStack()
        self.S = None
        self.din = {}
        self.uid = 0

    def inp(self, name, shape, dt=F32):
        self.din[name] = self.nc.dram_tensor(name, list(shape), dt, kind="ExternalInput").ap()
        return self.din[name]

    def scratch(self, name, shape, dt):
        return self.nc.dram_tensor(name, list(shape), dt, kind="Internal").ap()

    def sb(self, ps, name, shape, dt):
        self.uid += 1
        return T(ps.enter_context(self.nc.sbuf_tensor("%s_%d" % (name, self.uid), list(shape), dt)), name)

    def pb(self, ps, name, shape, dt):
        self.uid += 1
        return T(ps.enter_context(self.nc.psum_tensor("%s_%d" % (name, self.uid), list(shape), dt)), name)

    def rstd_from_ss(self, ss, col, n):
        S = self.S
        a = ss[:, col]
        S.op("act", lambda e: e.activation(out=a, in_=a, func=AF.Sqrt, bias=EPS, scale=1.0 / n), reads=[ss], writes=[ss])
        S.op("dve", lambda e: e.reciprocal(a, a), reads=[ss], writes=[ss])

    def load_bcast(self, q, dst, src_ap):
        self.S.dma(q, dst[:], src_ap.partition_broadcast(128), writes=[dst])

    def front(self, ph, xt, Avec, shvec):
        S = self.S
        ss, junk, ub, uT, pT, ident = ph.ss, ph.junk, ph.ub, ph.uT, ph.pT, ph.ident
        S.op("act", lambda e: e.activation(out=junk[:], in_=xt[:], func=AF.Square, accum_out=ss[:, 0:1]), reads=[xt], writes=[junk, ss])
        self.rstd_from_ss(ss, slice(0, 1), D)
        S.op("dve", lambda e: e.scalar_tensor_tensor(xt[:], xt[:], ss[:, 0:1], Avec[:], ALU.mult, ALU.mult), reads=[xt, ss, Avec], writes=[xt])
        S.op("dve", lambda e: e.tensor_tensor(ub[:], xt[:], shvec[:], ALU.add), reads=[xt, shvec], writes=[ub])
        for half in range(2):
            p = pT[half]
            for j in range(8):
                kc = half * 8 + j
                S.op("pe", lambda e, p=p, j=j, kc=kc: e.transpose(p[:, j * 128:(j + 1) * 128], ub[:, kc * 128:(kc + 1) * 128], ident[:]), reads=[ub, ident], writes=[p])
            eng = "act" if half == 0 else "dve"
            dst = uT[:, half * 8:(half + 1) * 8, :].rearrange("p a b -> p (a b)")
            if eng == "act":
                S.op("act", lambda e, p=p, dst=dst: e.copy(dst, p[:]), reads=[p], writes=[uT])
            else:
                S.op("dve", lambda e, p=p, dst=dst: e.tensor_copy(dst, p[:]), reads=[p], writes=[uT])

    def front_alloc(self, ps, ph, share=None):
        ph.ss = self.sb(ps, "ss", [128, 16], F32)
        ph.junk = self.sb(ps, "junk", [128, D], BF16)
        ph.ub = self.sb(ps, "ub", [128, D], BF16)
        ph.uT = self.sb(ps, "uT", [128, KC, 128], BF16)
        if share is not None:
            ph.pT = share.pT
            ph.ident = share.ident
            return
        ph.pT = [self.pb(ps, "pT%d" % i, [128, 1024], BF16) for i in range(2)]
        ph.ident = self.sb(ps, "ident", [128, 128], BF16)
        self.S.dma("pool", ph.ident[:], self.din["ident"][:, :], writes=[ph.ident])

    def mod_vecs(self, ps, ph, layer, which, names):
        S = self.S
        for attr, kind, k, gname in names:
            t = self.sb(ps, attr, [128, D], F32)
            self.load_bcast("sp", t, self.modv[layer, which, k * D:(k + 1) * D])
            if kind != "S":
                g = ph.gtmp
                self.load_bcast("sp", g, self.din[gname][layer, :])
                if kind == "A":
                    S.op("dve", lambda e, t=t, g=g: e.scalar_tensor_tensor(t[:], t[:], 1.0, g[:], ALU.add, ALU.mult), reads=[t, g], writes=[t])
                else:
                    S.op("dve", lambda e, t=t, g=g: e.tensor_tensor(t[:], t[:], g[:], ALU.mult), reads=[t, g], writes=[t])
            setattr(ph, attr, t)

    def head_norm_rope(self, ph, src_ps, src_ap, H, hd, gain, cs, dstT, rope=True, norm=True):
        S = self.S
        qf = ph.qf
        n = H * hd
        v3 = lambda ap: ap.rearrange("p (h d) -> p h d", d=hd)
        S.op("act", lambda e: e.copy(qf[:, 0:n], src_ap), reads=[src_ps], writes=[qf])
        if norm:
            sq = ph.qsq
            S.op("dve", lambda e: e.tensor_tensor(sq[:, 0:n], qf[:, 0:n], qf[:, 0:n], ALU.mult), reads=[qf], writes=[sq])
            S.op("dve", lambda e: e.tensor_reduce(ph.ss[:, 2:2 + H], v3(sq[:, 0:n]), AX.X, ALU.add), reads=[sq], writes=[ph.ss])
            self.rstd_from_ss(ph.ss, slice(2, 2 + H), hd)
            S.op("dve", lambda e: e.tensor_tensor(v3(qf[:, 0:n]), v3(qf[:, 0:n]), ph.ss[:, 2:2 + H].unsqueeze(2).to_broadcast([128, H, hd]), ALU.mult), reads=[qf, ph.ss], writes=[qf])
            S.op("dve", lambda e: e.tensor_tensor(v3(qf[:, 0:n]), v3(qf[:, 0:n]), gain[:, 0:hd].unsqueeze(1).to_broadcast([128, H, hd]), ALU.mult), reads=[qf, gain], writes=[qf])
        if rope:
            csT, ci_, si_ = cs
            cosv = csT[:, ci_, :]
            sinv = csT[:, si_, :]
            q4 = hd // 4
            rot = ph.qsq
            x5 = lambda ap: ap.rearrange("p (h a b d) -> p h a b d", a=2, b=2, d=q4)
            s4 = lambda ap: ap.rearrange("p (a b d) -> p a b d", a=2, b=2, d=q4)
            for b0 in range(2):
                S.op("dve", lambda e, b0=b0: e.tensor_tensor(x5(rot[:, 0:n])[:, :, :, b0, :], x5(qf[:, 0:n])[:, :, :, 1 - b0, :],
                                                            s4(sinv[:, 0:hd])[:, :, b0, :].unsqueeze(1).to_broadcast([128, H, 2, q4]), ALU.mult), reads=[qf, csT], writes=[rot])
            S.op("dve", lambda e: e.tensor_tensor(v3(qf[:, 0:n]), v3(qf[:, 0:n]), cosv[:, 0:hd].unsqueeze(1).to_broadcast([128, H, hd]), ALU.mult), reads=[qf, csT], writes=[qf])
            S.op("dve", lambda e: e.tensor_tensor(dstT[:, 0:n], qf[:, 0:n], rot[:, 0:n], ALU.add), reads=[qf, rot], writes=[dstT])
        else:
            S.op("dve", lambda e: e.tensor_copy(dstT[:, 0:n], qf[:, 0:n]), reads=[qf], writes=[dstT])

    def transpose_out(self, ph, src, nblk, pTt, dst_sb, dram_ap, rows=128):
        S = self.S
        for j in range(nblk):
            S.op("pe", lambda e, j=j: e.transpose(pTt[:, j * 128:(j + 1) * 128], src[:, j * 128:(j + 1) * 128], ph.ident[:]), reads=[src, ph.ident], writes=[pTt])
        S.op("act", lambda e: e.copy(dst_sb[:, 0:nblk * 128], pTt[:, 0:nblk * 128]), reads=[pTt], writes=[dst_sb])
        S.dma("pool", dram_ap, dst_sb[0:rows, 0:nblk * 128].rearrange("p (h t) -> p h t", t=128), reads=[dst_sb])


def _lin(S, ps_tile, out_ap, uT, W, c0, c1, kcn=KC):
    for kc in range(kcn):
        S.op("pe", lambda e, kc=kc: e.matmul(out_ap, uT[:, kc, :], W[:, kc, c0:c1], start=(kc == 0), stop=(kc == kcn - 1)), reads=[uT, W], writes=[ps_tile])


class Ph:
    pass


def build(cfg):
    P = Prog(cfg)
    nc = P.nc
    NQT, NKT = cfg.NQT, cfg.NKT
    NQ = NQT * 128
    NK = NKT * 128
    NOWN = cfg.NOWN
    OWN0 = cfg.OWN0
    inp = P.inp
    xq = inp("xq", [NQ, D])
    xkv = inp("xkv", [NK, D])
    cT = inp("cT", [128, KC, 2])
    inp("ident", [128, 128])
    ones_in = inp("ones", [128, 128])
    w_mod = inp("w_mod", [2, D, 6 * D])
    b_mod = inp("b_mod", [2, 6 * D])
    for n in ["g_mix_pre", "g_mix_post", "g_ffn_pre", "g_ffn_post"]:
        inp(n, [2, D])
    w_gate = inp("w_gate", [2, D, DFF])
    w_up = inp("w_up", [2, D, DFF])
    w_down = inp("w_down", [2, DFF, D])
    ab_w_in = inp("ab_w_in", [D, 2624])
    ab_w_out = inp("ab_w_out", [D, D])
    a_q_norm = inp("a_q_norm", [128])
    a_k_norm = inp("a_k_norm", [128])
    b_q_norm = inp("b_q_norm", [512])
    b_kv_norm = inp("b_kv_norm", [512])
    w_uq = inp("w_uq", [512, 1536])
    w_ukv = inp("w_ukv", [512, 2048])
    cd_w_in = inp("cd_w_in", [D, 4608])
    cd_w_out = inp("cd_w_out", [D, D])
    c_q_norm = inp("c_q_norm", [128])
    c_k_norm = inp("c_k_norm", [128])
    c_sink = inp("c_sink", [8])
    d_q_norm = inp("d_q_norm", [128])
    d_k_norm = inp("d_k_norm", [128])
    ropeq = inp("ropeq", [4, NQ, 128])
    ropek = inp("ropek", [4, NK, 128])
    maskC = inp("maskC", [3, 3, 128, 128])
    biasD = inp("biasD", [5, 8, 6, 128, 128])
    yout = nc.dram_tensor("y", [NOWN * 128, D], F32, kind="ExternalOutput").ap()
    sc = P.scratch
    P.modv = sc("modv", [2, 2, 6 * D], F32)
    KAT = sc("KAT", [2, 128, NK], BF16)
    VA = sc("VA", [2, NK, 128], BF16)
    KBT = sc("KBT", [8, 128, NK], BF16)
    KRT = sc("KRT", [128, NK], BF16)
    VB = sc("VB", [8, NK, 128], BF16)
    QAT = sc("QAT", [8, 128, NQ], BF16)
    QBT = sc("QBT", [8, 128, NQ], BF16)
    QRT = sc("QRT", [4, 128, NQ], BF16)
    OT = sc("OT", [16, 128, NQ], BF16)
    hmid = sc("hmid", [NQ, D], F32)
    h1 = sc("h1", [NQ, D], F32)
    QCT = sc("QCT", [8, 128, NQ], BF16)
    KCT = sc("KCT", [2, 128, NQ], BF16)
    VC = sc("VC", [2, NQ, 128], BF16)
    QDT = sc("QDT", [8, 128, NQ], BF16)
    KDT = sc("KDT", [8, 128, NQ], BF16)
    VD = sc("VD", [8, NQ, 128], BF16)
    OT1 = sc("OT1", [16, 128, NQ], BF16)
    hmid1 = sc("hmid1", [NQ, D], F32)
    dbg = {}
    if cfg.dbg:
        for n, shp in [("d_modv", [2, 2, 6 * D]), ("d_hmid", [NQ, D]), ("d_h1", [NQ, D]), ("d_hmid1", [NQ, D])]:
            dbg[n] = nc.dram_tensor(n, shp, F32, kind="ExternalOutput").ap()
        for n, shp in [("d_KAT", [2, 128, NK]), ("d_KBT", [8, 128, NK]), ("d_KRT", [128, NK]), ("d_VA", [2, NK, 128]), ("d_VB", [8, NK, 128]),
                       ("d_QAT", [8, 128, NQ]), ("d_QBT", [8, 128, NQ]), ("d_QRT", [4, 128, NQ]), ("d_OT", [16, 128, NQ]), ("d_OT1", [16, 128, NQ])]:
            dbg[n] = nc.dram_tensor(n, shp, BF16, kind="ExternalOutput").ap()

    with P.es as es:
        S = Sch(nc, es)
        P.S = S

        with ExitStack() as ps:
            cTt = P.sb(ps, "cT", [128, KC, 2], F32)
            scb = P.sb(ps, "scb", [128, KC, 2], BF16)
            wm = [P.sb(ps, "wm%d" % i, [128, KC, 512], BF16) for i in range(2)]
            bmt = [P.sb(ps, "bm%d" % i, [2, 512], F32) for i in range(2)]
            mo = [P.sb(ps, "mo%d" % i, [2, 512], F32) for i in range(2)]
            pm = [P.pb(ps, "pm%d" % i, [128, 512], F32) for i in range(2)]
            S.dma("sp", cTt[:], cT[:, :, :], writes=[cTt])
            S.op("act", lambda e: e.activation(out=scb[:], in_=cTt[:], func=AF.Silu), reads=[cTt], writes=[scb])
            it = 0
            for l in range(2):
                for cg in range(24):
                    b = it % 2
                    it += 1
                    S.dma("pool", wm[b][:], w_mod[l, :, cg * 512:(cg + 1) * 512].rearrange("(kc p) n -> p kc n", p=128), writes=[wm[b]])
                    S.dma("sp", bmt[b][:], b_mod[l, cg * 512:(cg + 1) * 512].partition_broadcast(2), writes=[bmt[b]])
                    for kc in range(KC):
                        S.op("pe", lambda e, kc=kc, b=b: e.matmul(pm[b][0:2, :], scb[:, kc, :], wm[b][:, kc, :], start=(kc == 0), stop=(kc == KC - 1)), reads=[scb, wm[b]], writes=[pm[b]])
                    S.op("dve", lambda e, b=b: e.tensor_tensor(mo[b][:], pm[b][0:2, :], bmt[b][:], ALU.add), reads=[pm[b], bmt[b]], writes=[mo[b]])
                    S.dma("sp", P.modv[l, :, cg * 512:(cg + 1) * 512], mo[b][:], reads=[mo[b]])
            S.barrier()
        if cfg.dbg:
            S.dma("sp", dbg["d_modv"][:, :, :], P.modv[:, :, :])
            S.barrier()

        def run_rr(gens):
            gens = list(gens)
            while gens:
                for g_ in list(gens):
                    try:
                        next(g_)
                    except StopIteration:
                        gens.remove(g_)

        def qkv_phase(layer, nt, src, rope_t, ctx_from, Wdram, col_specs, mod_names_gain):
            with ExitStack() as ps:
                ph0 = Ph()
                P.front_alloc(ps, ph0)
                ph1 = Ph()
                P.front_alloc(ps, ph1, share=ph0)
                phs = [ph0, ph1]
                ident = ph0.ident
                xin = [P.sb(ps, "xin%d" % i, [128, D], F32) for i in range(2)]
                ph0.gtmp = xin[1]
                for which, sfx in ((0, "l"), (1, "c")):
                    P.mod_vecs(ps, ph0, layer, which, [("A" + sfx, "A", 1, mod_names_gain), ("S" + sfx, "S", 0, None)])
                ncols = sum(c["n"] for c in col_specs)
                Wsb = P.sb(ps, "Wsb", [128, KC, ncols], BF16)
                off = 0
                for c in col_specs:
                    c["off"] = off
                    S.dma("pool", Wsb[:, :, off:off + c["n"]], Wdram[:, c["c0"]:c["c0"] + c["n"]].rearrange("(kc p) n -> p kc n", p=128), writes=[Wsb])
                    off += c["n"]
                gains = {}
                gl_all = {"a_q_norm": a_q_norm, "a_k_norm": a_k_norm, "b_q_norm": b_q_norm, "b_kv_norm": b_kv_norm,
                          "c_q_norm": c_q_norm, "c_k_norm": c_k_norm, "d_q_norm": d_q_norm, "d_k_norm": d_k_norm}
                for c in col_specs:
                    gn = c.get("gain")
                    if gn is not None and gn not in gains:
                        n = 512 if gn.startswith("b_") else 128
                        gt = P.sb(ps, gn, [128, n], F32)
                        P.load_bcast("sp", gt, gl_all[gn][:])
                        gains[gn] = gt
                cst = [P.sb(ps, "cs%d" % i, [128, 4, 128], F32) for i in range(2)]
                pq = [P.pb(ps, "pq%d" % i, [128, 512], F32) for i in range(4)]
                pT2 = [P.pb(ps, "pT2%d" % i, [128, 1024], BF16) for i in range(1)]
                pq_extra = P.pb(ps, "pqx", [128, 512], F32)
                ctr = {"pq": 0, "pt": 0}

                def next_pq():
                    ctr["pq"] += 1
                    return pq[ctr["pq"] % 4]

                def next_pt():
                    ctr["pt"] += 1
                    return pT2[0]

                extra = {}
                for c in col_specs:
                    if c["kind"] in ("mla_q", "mla_kv"):
                        extra["cT"] = P.sb(ps, "cnT", [128, 4, 128], BF16)
                        wn = "wuq" if c["kind"] == "mla_q" else "wukv"
                        wsrc = w_uq if c["kind"] == "mla_q" else w_ukv
                        ncol = 1536 if c["kind"] == "mla_q" else 2048
                        extra[wn] = P.sb(ps, wn, [128, 4, ncol], BF16)
                        S.dma("pool", extra[wn][:], wsrc[:, :].rearrange("(kc p) n -> p kc n", p=128), writes=[extra[wn]])
                chains_par = []
                for par_ in range(2):
                  chains = []
                  for ci, c in enumerate(col_specs):
                    for s0 in range(0, c["n"], 512):
                        s1 = min(c["n"], s0 + 512)
                        w = s1 - s0
                        b = Ph()
                        b.c, b.s0, b.s1, b.w = c, s0, s1, w
                        b.pp = pq[len(chains) % 4]
                        b.ss = P.sb(ps, "css", [128, 16], F32)
                        if c["kind"] != "v":
                            b.qf = P.sb(ps, "cqf", [128, max(w, 128)], F32)
                            b.qsq = P.sb(ps, "cqsq", [128, max(w, 128)], F32)
                            b.hT = P.sb(ps, "chT", [128, 512], BF16)
                        b.hb = P.sb(ps, "chb", [128, max(w, 128)], BF16)
                        if c["kind"] == "mla_q":
                            b.qf2 = P.sb(ps, "cqf2", [128, 512], F32)
                            b.qsq2 = P.sb(ps, "cqsq2", [128, 512], F32)
                            b.hb2 = P.sb(ps, "chb2", [128, 512], BF16)
                            b.hT2 = P.sb(ps, "chT2", [128, 512], BF16)
                        if c["kind"] == "mla_kv":
                            b.hb2 = P.sb(ps, "chb2", [128, 512], BF16)
                            b.hT2 = P.sb(ps, "chT2", [128, 512], BF16)
                        chains.append(b)
                  chains_par.append(chains)
                mla_lock = {"held": False}

                def g_norm_rope(b, qf, qsq, n, H, hd, gain, cs, dst, rope, norm):
                    v3 = lambda ap: ap.rearrange("p (h d) -> p h d", d=hd)
                    if norm:
                        S.op("dve", lambda e: e.tensor_tensor(qsq[:, 0:n], qf[:, 0:n], qf[:, 0:n], ALU.mult), reads=[qf], writes=[qsq])
                        yield
                        S.op("dve", lambda e: e.tensor_reduce(b.ss[:, 0:H], v3(qsq[:, 0:n]), AX.X, ALU.add), reads=[qsq], writes=[b.ss])
                        yield
                        S.op("act", lambda e: e.activation(out=b.ss[:, 0:H], in_=b.ss[:, 0:H], func=AF.Sqrt, bias=EPS, scale=1.0 / hd), reads=[b.ss], writes=[b.ss])
                        yield
                        S.op("dve", lambda e: e.reciprocal(b.ss[:, 0:H], b.ss[:, 0:H]), reads=[b.ss], writes=[b.ss])
                        yield
                        S.op("dve", lambda e: e.tensor_tensor(v3(qf[:, 0:n]), v3(qf[:, 0:n]), b.ss[:, 0:H].unsqueeze(2).to_broadcast([128, H, hd]), ALU.mult), reads=[qf, b.ss], writes=[qf])
                        yield
                        S.op("dve", lambda e: e.tensor_tensor(v3(qf[:, 0:n]), v3(qf[:, 0:n]), gain[:, 0:hd].unsqueeze(1).to_broadcast([128, H, hd]), ALU.mult), reads=[qf, gain], writes=[qf])
                        yield
                    if rope:
                        csT, ci_, si_ = cs
                        cosv = csT[:, ci_, :]
                        sinv = csT[:, si_, :]
                        q4 = hd // 4
                        x5 = lambda ap: ap.rearrange("p (h a b d) -> p h a b d", a=2, b=2, d=q4)
                        s4 = lambda ap: ap.rearrange("p (a b d) -> p a b d", a=2, b=2, d=q4)
                        for b0 in range(2):
                            S.op("dve", lambda e, b0=b0: e.tensor_tensor(x5(qsq[:, 0:n])[:, :, :, b0, :], x5(qf[:, 0:n])[:, :, :, 1 - b0, :],
                                                                        s4(sinv[:, 0:hd])[:, :, b0, :].unsqueeze(1).to_broadcast([128, H, 2, q4]), ALU.mult), reads=[qf, csT], writes=[qsq])
                            yield
                        S.op("dve", lambda e: e.tensor_tensor(v3(qf[:, 0:n]), v3(qf[:, 0:n]), cosv[:, 0:hd].unsqueeze(1).to_broadcast([128, H, hd]), ALU.mult), reads=[qf, csT], writes=[qf])
                        yield
                        S.op("dve", lambda e: e.tensor_tensor(dst[:, 0:n], qf[:, 0:n], qsq[:, 0:n], ALU.add), reads=[qf, qsq], writes=[dst])
                        yield
                    else:
                        S.op("dve", lambda e: e.tensor_copy(dst[:, 0:n], qf[:, 0:n]), reads=[qf], writes=[dst])
                        yield

                def g_tr_out(srcb, nblk, dstb, dram_ap):
                    pTt = next_pt()
                    for j in range(nblk):
                        S.op("pe", lambda e, j=j: e.transpose(pTt[:, j * 128:(j + 1) * 128], srcb[:, j * 128:(j + 1) * 128], ident[:]), reads=[srcb, ident], writes=[pTt])
                    S.op("act", lambda e: e.copy(dstb[:, 0:nblk * 128], pTt[:, 0:nblk * 128]), reads=[pTt], writes=[dstb])
                    yield
                    S.dma("pool", dram_ap, dstb[:, 0:nblk * 128].rearrange("p (h t) -> p h t", t=128), reads=[dstb])
                    yield

                def g_chain(t, b, uT, cs):
                    c, s0, s1, w = b.c, b.s0, b.s1, b.w
                    kind = c["kind"]
                    if c.get("own_only") and not (OWN0 <= t < OWN0 + NOWN):
                        return
                    tsl = slice(t * 128, (t + 1) * 128)
                    H = max(w // 128, 1)
                    h0 = s0 // 128
                    pp = b.pp
                    _lin(S, pp, pp[:, 0:w], uT, Wsb, c["off"] + s0, c["off"] + s1)
                    yield
                    if kind == "v":
                        S.op("act", lambda e: e.copy(b.hb[:, 0:w], pp[:, 0:w]), reads=[pp], writes=[b.hb])
                        yield
                        S.dma("pool", c["dst"][h0:h0 + H, tsl, :].rearrange("h p d -> p h d"), b.hb[:, 0:w].rearrange("p (h d) -> p h d", d=128), reads=[b.hb])
                        yield
                        return
                    S.op("act", lambda e: e.copy(b.qf[:, 0:w], pp[:, 0:w]), reads=[pp], writes=[b.qf])
                    yield
                    if kind == "heads":
                        yield from g_norm_rope(b, b.qf, b.qsq, w, H, 128, gains[c["gain"]], (cs, 0, 1), b.hb, c["rope"], True)
                        yield from g_tr_out(b.hb, H, b.hT, c["dst"][h0:h0 + H, :, tsl].rearrange("h p t -> p h t"))
                    elif kind == "kr":
                        yield from g_norm_rope(b, b.qf, b.qsq, 64, 1, 64, None, (cs, 2, 3), b.hb, True, False)
                        S.op("dve", lambda e: e.tensor_copy(b.hb[:, 64:128], b.hb[:, 0:64]), reads=[b.hb], writes=[b.hb])
                        yield
                        yield from g_tr_out(b.hb, 1, b.hT, c["dst"][:, tsl].rearrange("p (h t) -> p h t", h=1))
                    else:
                        yield from g_norm_rope(b, b.qf, b.qsq, 512, 1, 512, gains[c["gain"]], None, b.hb, False, True)
                        cnT = extra["cT"]
                        while mla_lock["held"]:
                            yield
                        mla_lock["held"] = True
                        pTt = next_pt()
                        for j in range(4):
                            S.op("pe", lambda e, j=j: e.transpose(pTt[:, j * 128:(j + 1) * 128], b.hb[:, j * 128:(j + 1) * 128], ident[:]), reads=[b.hb, ident], writes=[pTt])
                        S.op("act", lambda e: e.copy(cnT[:].rearrange("p a b -> p (a b)"), pTt[:, 0:512]), reads=[pTt], writes=[cnT])
                        yield
                        W2 = extra["wuq" if kind == "mla_q" else "wukv"]
                        for hh in range(2):
                            pp2 = pq_extra
                            for h in range(4):
                                hd_ = hh * 4 + h
                                for kc in range(4):
                                    S.op("pe", lambda e, h=h, hd_=hd_, kc=kc: e.matmul(pp2[:, h * 128:(h + 1) * 128], W2[:, kc, hd_ * 128:(hd_ + 1) * 128], cnT[:, kc, :],
                                                                                      start=(kc == 0), stop=(kc == 3)), reads=[W2, cnT], writes=[pp2])
                            yield
                            hT_ = b.hT if hh == 0 else b.hT2
                            S.op("act", lambda e: e.copy(hT_[:, 0:512], pp2[:, 0:512]), reads=[pp2], writes=[hT_])
                            yield
                            S.dma("pool", c["dst"][hh * 4:hh * 4 + 4, :, tsl].rearrange("h p t -> p h t"), hT_[:, 0:512].rearrange("p (h t) -> p h t", t=128), reads=[hT_])
                            yield
                        if kind == "mla_q":
                            pp2 = pq_extra
                            _lin(S, pp2, pp2[:, 0:512], cnT, W2, 1024, 1536, kcn=4)
                            yield
                            S.op("act", lambda e: e.copy(b.qf2[:, 0:512], pp2[:, 0:512]), reads=[pp2], writes=[b.qf2])
                            yield
                            yield from g_norm_rope(b, b.qf2, b.qsq2, 512, 8, 64, None, (cs, 2, 3), b.hb2, True, False)
                            yield from g_tr_out(b.hb2, 4, b.hT, c["dst_r"][:, :, tsl].rearrange("h p t -> p h t"))
                        else:
                            for hh in range(2):
                                pp2 = pq_extra
                                _lin(S, pp2, pp2[:, 0:512], cnT, W2, 1024 + hh * 512, 1024 + (hh + 1) * 512, kcn=4)
                                yield
                                hb_ = b.hb if hh == 0 else b.hb2
                                S.op("act", lambda e: e.copy(hb_[:, 0:512], pp2[:, 0:512]), reads=[pp2], writes=[hb_])
                                yield
                                S.dma("pool", c["dst_v"][hh * 4:hh * 4 + 4, tsl, :].rearrange("h p d -> p h d"), hb_[:, 0:512].rearrange("p (h d) -> p h d", d=128), reads=[hb_])
                                yield

                        mla_lock["held"] = False

                def g_front(t):
                    is_ctx = t >= ctx_from
                    xt = xin[t % 2]
                    ph = phs[t % 2]
                    Avec = ph0.Ac if is_ctx else ph0.Al
                    shvec = ph0.Sc if is_ctx else ph0.Sl
                    S.dma("sp", xt[:], src[t * 128:(t + 1) * 128, :], writes=[xt])
                    S.dma("sp", cst[t % 2][:], rope_t[:, t * 128:(t + 1) * 128, :].rearrange("f p d -> p f d"), writes=[cst[t % 2]])
                    yield
                    S.op("act", lambda e: e.activation(out=ph.junk[:], in_=xt[:], func=AF.Square, accum_out=ph.ss[:, 0:1]), reads=[xt], writes=[ph.junk, ph.ss])
                    yield
                    S.op("act", lambda e: e.activation(out=ph.ss[:, 0:1], in_=ph.ss[:, 0:1], func=AF.Sqrt, bias=EPS, scale=1.0 / D), reads=[ph.ss], writes=[ph.ss])
                    yield
                    S.op("dve", lambda e: e.reciprocal(ph.ss[:, 0:1], ph.ss[:, 0:1]), reads=[ph.ss], writes=[ph.ss])
                    yield
                    S.op("dve", lambda e: e.scalar_tensor_tensor(xt[:], xt[:], ph.ss[:, 0:1], Avec[:], ALU.mult, ALU.mult), reads=[xt, ph.ss, Avec], writes=[xt])
                    yield
                    S.op("dve", lambda e: e.tensor_tensor(ph.ub[:], xt[:], shvec[:], ALU.add), reads=[xt, shvec], writes=[ph.ub])
                    yield
                    for half in range(2):
                        p = ph.pT[half]
                        for j in range(8):
                            kc = half * 8 + j
                            S.op("pe", lambda e, j=j, kc=kc: e.transpose(p[:, j * 128:(j + 1) * 128], ph.ub[:, kc * 128:(kc + 1) * 128], ident[:]), reads=[ph.ub, ident], writes=[p])
                        yield
                        dst = ph.uT[:, half * 8:(half + 1) * 8, :].rearrange("p a b -> p (a b)")
                        if half == 0:
                            S.op("act", lambda e: e.copy(dst, p[:]), reads=[p], writes=[ph.uT])
                        else:
                            S.op("dve", lambda e: e.tensor_copy(dst, p[:]), reads=[p], writes=[ph.uT])
                        yield

                nch = len(chains_par[0])
                done = set()
                steps = {}
                pending = []
                for t in range(nt):
                    pending.append(("F", t, -1))
                    for ci in range(nch):
                        pending.append(("C", t, ci))
                active = []

                def can_start(key):
                    kind, t, ci = key
                    if kind == "F":
                        if t >= 1 and ("F", t - 1, -1) not in done:
                            return False
                        if t >= 2:
                            for cj in range(nch):
                                if ("C", t - 2, cj) not in done:
                                    return False
                        return True
                    if ("F", t, -1) not in done:
                        return False
                    if t >= 2 and ("C", t - 2, ci) not in done:
                        return False
                    if t >= 1 and ("C", t - 1, ci) not in done and steps.get(("C", t - 1, ci), 0) < 3:
                        return False
                    return True

                while pending or active:
                    for key in list(pending):
                        if can_start(key):
                            pending.remove(key)
                            kind, t, ci = key
                            gen = g_front(t) if kind == "F" else g_chain(t, chains_par[t % 2][ci], phs[t % 2].uT, cst[t % 2])
                            active.append((key, gen))
                    for key, gen in list(active):
                        try:
                            next(gen)
                            steps[key] = steps.get(key, 0) + 1
                        except StopIteration:
                            active.remove((key, gen))
                            done.add(key)
                S.barrier()


        qkv_phase(0, NKT, xkv, ropek, NKT - 2, ab_w_in, [
            dict(kind="heads", c0=1024, n=256, gain="a_k_norm", rope=True, dst=KAT),
            dict(kind="v", c0=1280, n=256, dst=VA),
            dict(kind="mla_kv", c0=1536 + 512, n=512, gain="b_kv_norm", dst=KBT, dst_v=VB),
            dict(kind="kr", c0=1536 + 1024, n=64, dst=KRT),
        ], "g_mix_pre")
        qkv_phase(0, NQT, xq, ropeq, NQT - 2, ab_w_in, [
            dict(kind="heads", c0=0, n=1024, gain="a_q_norm", rope=True, dst=QAT),
            dict(kind="mla_q", c0=1536, n=512, gain="b_q_norm", dst=QBT, dst_r=QRT),
        ], "g_mix_pre")
        if cfg.dbg:
            for n_, src_ in [("d_KAT", KAT), ("d_KBT", KBT), ("d_VA", VA), ("d_VB", VB), ("d_QAT", QAT), ("d_QBT", QBT), ("d_QRT", QRT)]:
                S.dma("sp", dbg[n_][:, :, :], src_[:, :, :])
            S.dma("sp", dbg["d_KRT"][:, :], KRT[:, :])
            S.barrier()

        with ExitStack() as ps:
            ones = P.sb(ps, "ones", [128, 128], BF16)
            S.dma("pool", ones[:], ones_in[:, :], writes=[ones])
            Ksb = [P.sb(ps, "Ksb%d" % i, [128, NK], BF16) for i in range(2)]
            Vsb = [P.sb(ps, "Vsb%d" % i, [128, NKT, 128], BF16) for i in range(2)]
            KRsb = P.sb(ps, "KRsb", [128, NK], BF16)
            qsb = [P.sb(ps, "qsb%d" % i, [128, 512], BF16) for i in range(2)]
            qrsb = [P.sb(ps, "qrsb%d" % i, [128, 512], BF16) for i in range(2)]
            pt = [P.sb(ps, "pt%d" % i, [128, 512], BF16) for i in range(3)]
            rden = P.sb(ps, "rden", [128, 512], F32)
            osb = [P.sb(ps, "osb%d" % i, [128, 512], BF16) for i in range(2)]
            pS = [P.pb(ps, "pS%d" % i, [128, 512], F32) for i in range(3)]
            pO = [P.pb(ps, "pO%d" % i, [128, 512], F32) for i in range(2)]
            pD = [P.pb(ps, "pD%d" % i, [128, 512], F32) for i in range(2)]
            S.dma("sp", KRsb[:], KRT[:, :], writes=[KRsb])

            def load_kv(hh):
                kb = hh % 2
                if hh < 2:
                    S.dma("sp", Ksb[kb][:], KAT[hh, :, :], writes=[Ksb[kb]])
                    vsrc = VA[hh]
                else:
                    S.dma("sp", Ksb[kb][:], KBT[hh - 2, :, :], writes=[Ksb[kb]])
                    vsrc = VB[hh - 2]
                for t0_ in range(0, NKT, 11):
                    t1_ = min(NKT, t0_ + 11)
                    S.dma("sp", Vsb[kb][:, t0_:t1_, :], vsrc[t0_ * 128:t1_ * 128, :].rearrange("(t p) d -> p t d", p=128), writes=[Vsb[kb]])

            glist = []
            for hh in range(10):
                if hh < 2:
                    glist += [(hh, qt * 128, 512) for qt in range(NQT)]
                else:
                    glist += [(hh, t0 * 128, min(4, NQT - t0) * 128) for t0 in range(0, NQT, 4)]

            def load_q(gi):
                hh, tok0, N = glist[gi]
                g = gi % 2
                if hh < 2:
                    S.dma("sp", qsb[g][:].rearrange("p (h t) -> p h t", t=128), QAT[hh * 4:hh * 4 + 4, :, tok0:tok0 + 128].rearrange("h p t -> p h t"), writes=[qsb[g]])
                else:
                    h = hh - 2
                    S.dma("sp", qsb[g][:, 0:N], QBT[h, :, tok0:tok0 + N], writes=[qsb[g]])
                    S.dma("sp", qrsb[g][:, 0:N], QRT[h // 2, :, tok0:tok0 + N], writes=[qrsb[g]])

            load_kv(0)
            load_q(0)
            step = 0
            for gi, (hh, tok0, N) in enumerate(glist):
                g = gi % 2
                kb = hh % 2
                first_of_head = gi == 0 or glist[gi - 1][0] != hh
                if first_of_head and hh + 1 < 10:
                    load_kv(hh + 1)
                if gi + 1 < len(glist):
                    load_q(gi + 1)
                qs = qsb[g]
                qr = qrsb[g]
                scale = 128 ** -0.5 if hh < 2 else 192 ** -0.5
                if hh >= 2:
                    h = hh - 2
                    half = slice(0, 64) if h % 2 == 0 else slice(64, 128)
                kts = list(range(NKT - 2, NKT)) if tok0 >= (NQT - 2) * 128 else list(range(NKT))
                nk = len(kts)
                base = step
                step += nk

                def qk(i):
                    sp_ = pS[(base + i) % 3]
                    ksl = slice(kts[i] * 128, (kts[i] + 1) * 128)
                    if hh < 2:
                        S.op("pe", lambda e: e.matmul(sp_[:, 0:N], Ksb[kb][:, ksl], qs[:, 0:N], start=True, stop=True), reads=[Ksb[kb], qs], writes=[sp_])
                    else:
                        S.op("pe", lambda e: e.matmul(sp_[:, 0:N], Ksb[kb][:, ksl], qs[:, 0:N], start=True, stop=False), reads=[Ksb[kb], qs], writes=[sp_])
                        S.op("pe", lambda e: e.matmul(sp_[:, 0:N], KRsb[half, ksl], qr[half, 0:N], start=False, stop=True), reads=[KRsb, qr], writes=[sp_])

                def ex(i):
                    sp_ = pS[(base + i) % 3]
                    ptt = pt[(base + i) % 3]
                    S.op("act", lambda e: e.activation(out=ptt[:, 0:N], in_=sp_[:, 0:N], func=AF.Exp, scale=scale), reads=[sp_], writes=[ptt])

                def pv(i):
                    ptt = pt[(base + i) % 3]
                    S.op("pe", lambda e: e.matmul(pO[g][:, 0:N], Vsb[kb][:, kts[i], :], ptt[:, 0:N], start=(i == 0), stop=(i == nk - 1)), reads=[Vsb[kb], ptt], writes=[pO[g]])
                    S.op("pe", lambda e: e.matmul(pD[g][:, 0:N], ones[:], ptt[:, 0:N], start=(i == 0), stop=(i == nk - 1)), reads=[ones, ptt], writes=[pD[g]])

                qk(0)
                for i in range(nk):
                    if i + 1 < nk:
                        qk(i + 1)
                    ex(i)
                    pv(i)
                S.op("dve", lambda e: e.reciprocal(rden[:, 0:N], pD[g][:, 0:N]), reads=[pD[g]], writes=[rden])
                ob = osb[g]
                S.op("dve", lambda e: e.tensor_tensor(ob[:, 0:N], pO[g][:, 0:N], rden[:, 0:N], ALU.mult), reads=[pO[g], rden], writes=[ob])
                if hh < 2:
                    S.dma("pool", OT[hh * 4:hh * 4 + 4, :, tok0:tok0 + 128].rearrange("h p t -> p h t"), ob[:].rearrange("p (h t) -> p h t", t=128), reads=[ob])
                else:
                    S.dma("pool", OT[8 + h, :, tok0:tok0 + N], ob[:, 0:N], reads=[ob])
            S.barrier()
        if cfg.dbg:
            S.dma("sp", dbg["d_OT"][:, :, :], OT[:, :, :])
            S.barrier()

        def out_phase(layer, tiles, OTd, Wout, resid, dst):
            with ExitStack() as ps:
                ph = Ph()
                ph.gtmp = P.sb(ps, "gtmp", [128, D], F32)
                P.mod_vecs(ps, ph, layer, 0, [("Gl", "G", 2, "g_mix_post")])
                if layer == 0:
                    P.mod_vecs(ps, ph, layer, 1, [("Gc", "G", 2, "g_mix_post")])
                Wsb = P.sb(ps, "Wo", [128, KC, D], BF16)
                S.dma("pool", Wsb[:], Wout[:, :].rearrange("(kc p) n -> p kc n", p=128), writes=[Wsb])
                oT = [P.sb(ps, "oT%d" % i, [128, 16, 128], BF16) for i in range(2)]
                xr = [P.sb(ps, "xr%d" % i, [128, D], F32) for i in range(2)]
                of = P.sb(ps, "of", [128, D], F32)
                junk = P.sb(ps, "junk", [128, D], BF16)
                ss = P.sb(ps, "ss", [128, 16], F32)
                pp = [P.pb(ps, "po%d" % i, [128, 512], F32) for i in range(8)]
                for i, t in enumerate(tiles):
                    is_ctx = (layer == 0) and t >= NQT - 2
                    o_ = oT[i % 2]
                    x_ = xr[i % 2]
                    tsl = slice(t * 128, (t + 1) * 128)
                    S.dma("sp", o_[:], OTd[:, :, tsl].rearrange("h p t -> p h t"), writes=[o_])
                    S.dma("sp", x_[:], resid[tsl, :], writes=[x_])
                    for cg in range(4):
                        pb_ = pp[(i % 2) * 4 + cg]
                        _lin(S, pb_, pb_[:, :], o_, Wsb, cg * 512, (cg + 1) * 512)
                        if cg % 2 == 0:
                            S.op("act", lambda e, pb_=pb_, cg=cg: e.copy(of[:, cg * 512:(cg + 1) * 512], pb_[:, :]), reads=[pb_], writes=[of])
                        else:
                            S.op("dve", lambda e, pb_=pb_, cg=cg: e.tensor_copy(of[:, cg * 512:(cg + 1) * 512], pb_[:, :]), reads=[pb_], writes=[of])
                    S.op("act", lambda e: e.activation(out=junk[:], in_=of[:], func=AF.Square, accum_out=ss[:, 0:1]), reads=[of], writes=[junk, ss])
                    P.rstd_from_ss(ss, slice(0, 1), D)
                    G = ph.Gc if is_ctx else ph.Gl
                    S.op("dve", lambda e, G=G: e.scalar_tensor_tensor(of[:], of[:], ss[:, 0:1], G[:], ALU.mult, ALU.mult), reads=[of, ss, G], writes=[of])
                    S.op("dve", lambda e, x_=x_: e.tensor_tensor(x_[:], of[:], x_[:], ALU.add), reads=[of, x_], writes=[x_])
                    S.dma("pool", dst[tsl, :], x_[:], reads=[x_])
                S.barrier()

        def ffn_phase(layer, tiles, src, dst_fn):
            JB = 2
            NB = FC // JB
            with ExitStack() as ps:
                ph = Ph()
                P.front_alloc(ps, ph)
                hx = [P.sb(ps, "hx%d" % i, [128, D], F32) for i in range(2)]
                ph.gtmp = hx[0]
                MV = [("A", "A", 4, "g_ffn_pre"), ("Sh", "S", 3, None), ("G", "G", 5, "g_ffn_post")]
                P.mod_vecs(ps, ph, layer, 0, MV)

                def switch_ctx(attrs):
                    for attr, kind, k, gname in MV:
                        if attr not in attrs:
                            continue
                        t_ = getattr(ph, attr)
                        P.load_bcast("sp", t_, P.modv[layer, 1, k * D:(k + 1) * D])
                        if kind != "S":
                            g_ = ph.gtmp
                            P.load_bcast("sp", g_, P.din[gname][layer, :])
                            if kind == "A":
                                S.op("dve", lambda e, t_=t_, g_=g_: e.scalar_tensor_tensor(t_[:], t_[:], 1.0, g_[:], ALU.add, ALU.mult), reads=[t_, g_], writes=[t_])
                            else:
                                S.op("dve", lambda e, t_=t_, g_=g_: e.tensor_tensor(t_[:], t_[:], g_[:], ALU.mult), reads=[t_, g_], writes=[t_])

                facc = [P.sb(ps, "facc%d" % i, [128, D], F32) for i in range(4)]
                u2T = [P.sb(ps, "u2T%d" % i, [128, KC, 512], BF16) for i in range(2)]
                wg = [P.sb(ps, "wg%d" % i, [128, KC, JB * 128], BF16) for i in range(2)]
                wu = [P.sb(ps, "wu%d" % i, [128, KC, JB * 128], BF16) for i in range(2)]
                wd = [P.sb(ps, "wd%d" % i, [128, JB, D], BF16) for i in range(2)]
                aT = [P.sb(ps, "aT%d" % i, [128, JB, 512], BF16) for i in range(2)]
                sg = [P.sb(ps, "sg%d" % i, [128, 512], F32) for i in range(2)]
                pg = [P.pb(ps, "pg%d" % i, [128, 512], F32) for i in range(2)]
                pu = [P.pb(ps, "pu%d" % i, [128, 512], F32) for i in range(2)]
                pd = [P.pb(ps, "pd%d" % i, [128, 512], F32) for i in range(2)]
                groups = []
                i = 0
                while i < len(tiles):
                    ctxf = (layer == 0) and tiles[i] >= NQT - 2
                    j = i
                    while j < len(tiles) and j - i < 4 and (((layer == 0) and tiles[j] >= NQT - 2) == ctxf):
                        j += 1
                    groups.append((tiles[i:j], ctxf))
                    i = j
                state = {"pro_ctx": False, "epi_ctx": False, "hxi": 0}

                def prologue(gidx):
                    gt, ctxf = groups[gidx]
                    if ctxf and not state["pro_ctx"]:
                        state["pro_ctx"] = True
                        switch_ctx(("A", "Sh"))
                    for ti, t in enumerate(gt):
                        x_ = hx[state["hxi"] % 2]
                        state["hxi"] += 1
                        S.dma("sp", x_[:], src[t * 128:(t + 1) * 128, :], writes=[x_])
                        P.front(ph, x_, ph.A, ph.Sh)
                        S.op("dve", lambda e, ti=ti: e.tensor_copy(u2T[gidx % 2][:, :, ti * 128:(ti + 1) * 128], ph.uT[:]), reads=[ph.uT], writes=[u2T[gidx % 2]])

                blk = 0
                gstep = 0
                dstep = 0
                prologue(0)
                for gidx, (gt, ctxf) in enumerate(groups):
                    G = len(gt) * 128
                    uT_ = u2T[gidx % 2]
                    bpar = {}

                    def gu(bi):
                        nonlocal blk, gstep, dstep
                        b = blk % 2
                        blk += 1
                        bpar[bi] = b
                        c0 = bi * JB * 128
                        S.dma("pool", wg[b][:], w_gate[layer, :, c0:c0 + JB * 128].rearrange("(kc p) n -> p kc n", p=128), writes=[wg[b]])
                        S.dma("pool", wu[b][:], w_up[layer, :, c0:c0 + JB * 128].rearrange("(kc p) n -> p kc n", p=128), writes=[wu[b]])
                        S.dma("pool", wd[b][:], w_down[layer, c0:c0 + JB * 128, :].rearrange("(j p) n -> p j n", p=128), writes=[wd[b]])
                        for jj in range(JB):
                            pgt = pg[gstep % 2]
                            sgt = sg[gstep % 2]
                            put = pu[gstep % 2]
                            gstep += 1
                            for kc in range(KC):
                                S.op("pe", lambda e, kc=kc: e.matmul(pgt[:, 0:G], wg[b][:, kc, jj * 128:(jj + 1) * 128], uT_[:, kc, 0:G], start=(kc == 0), stop=(kc == KC - 1)), reads=[wg[b], uT_], writes=[pgt])
                                if kc % 4 == 3:
                                    yield
                            S.op("act", lambda e: e.activation(out=sgt[:, 0:G], in_=pgt[:, 0:G], func=AF.Silu), reads=[pgt], writes=[sgt])
                            for kc in range(KC):
                                S.op("pe", lambda e, kc=kc: e.matmul(put[:, 0:G], wu[b][:, kc, jj * 128:(jj + 1) * 128], uT_[:, kc, 0:G], start=(kc == 0), stop=(kc == KC - 1)), reads=[wu[b], uT_], writes=[put])
                                if kc % 4 == 3 and kc != KC - 1:
                                    yield
                            S.op("dve", lambda e: e.tensor_tensor(aT[b][:, jj, 0:G], put[:, 0:G], sgt[:, 0:G], ALU.mult), reads=[put, sgt], writes=[aT[b]])
                            yield

                    def down(bi):
                        nonlocal dstep
                        b = bpar[bi]
                        for ti in range(len(gt)):
                            for cg in range(4):
                                pdt = pd[dstep % 2]
                                dstep += 1
                                for jj in range(JB):
                                    S.op("pe", lambda e, jj=jj: e.matmul(pdt[:, :], aT[b][:, jj, ti * 128:(ti + 1) * 128], wd[b][:, jj, cg * 512:(cg + 1) * 512], start=(jj == 0), stop=(jj == JB - 1)), reads=[aT[b], wd[b]], writes=[pdt])
                                fa = facc[ti]
                                if bi == 0:
                                    S.op("act", lambda e: e.copy(fa[:, cg * 512:(cg + 1) * 512], pdt[:, :]), reads=[pdt], writes=[fa])
                                else:
                                    S.op("dve", lambda e: e.tensor_tensor(fa[:, cg * 512:(cg + 1) * 512], pdt[:, :], fa[:, cg * 512:(cg + 1) * 512], ALU.add), reads=[pdt, fa], writes=[fa])
                                yield

                    run_rr([gu(0)])
                    for bi in range(NB):
                        gens = []
                        if bi + 1 < NB:
                            gens.append(gu(bi + 1))
                        gens.append(down(bi))
                        run_rr(gens)
                        if bi == 3 and gidx + 1 < len(groups):
                            prologue(gidx + 1)
                    if ctxf and not state["epi_ctx"]:
                        state["epi_ctx"] = True
                        switch_ctx(("G",))
                    for ti, t in enumerate(gt):
                        fa = facc[ti]
                        x_ = hx[state["hxi"] % 2]
                        state["hxi"] += 1
                        S.dma("sp", x_[:], src[t * 128:(t + 1) * 128, :], writes=[x_])
                        S.op("act", lambda e: e.activation(out=ph.junk[:], in_=fa[:], func=AF.Square, accum_out=ph.ss[:, 4:5]), reads=[fa], writes=[ph.junk, ph.ss])
                        P.rstd_from_ss(ph.ss, slice(4, 5), D)
                        S.op("dve", lambda e: e.scalar_tensor_tensor(fa[:], fa[:], ph.ss[:, 4:5], ph.G[:], ALU.mult, ALU.mult), reads=[fa, ph.ss, ph.G], writes=[fa])
                        S.op("dve", lambda e: e.tensor_tensor(x_[:], fa[:], x_[:], ALU.add), reads=[fa, x_], writes=[x_])
                        S.dma("sp", dst_fn(t), x_[:], reads=[x_])
                S.barrier()


        out_phase(0, list(range(NQT)), OT, ab_w_out, xq, hmid)
        if cfg.dbg:
            S.dma("sp", dbg["d_hmid"][:, :], hmid[:, :])
            S.barrier()
        ffn_phase(0, list(range(NQT)), hmid, lambda t: h1[t * 128:(t + 1) * 128, :])
        if cfg.dbg:
            S.dma("sp", dbg["d_h1"][:, :], h1[:, :])
            S.barrier()

        qkv_phase(1, NQT, h1, ropeq, NQT - 2, cd_w_in, [
            dict(kind="heads", c0=0, n=1024, gain="c_q_norm", rope=True, dst=QCT, own_only=True),
            dict(kind="heads", c0=1024, n=256, gain="c_k_norm", rope=True, dst=KCT),
            dict(kind="v", c0=1280, n=256, dst=VC),
        ], "g_mix_pre")
        qkv_phase(1, NQT, h1, ropeq, NQT - 2, cd_w_in, [
            dict(kind="heads", c0=1536, n=1024, gain="d_q_norm", rope=False, dst=QDT, own_only=True),
            dict(kind="heads", c0=2560, n=1024, gain="d_k_norm", rope=False, dst=KDT),
        ], "g_mix_pre")
        qkv_phase(1, NQT, h1, ropeq, NQT - 2, cd_w_in, [
            dict(kind="v", c0=3584, n=1024, dst=VD),
        ], "g_mix_pre")

        with ExitStack() as ps:
            ones = P.sb(ps, "ones", [128, 128], BF16)
            S.dma("pool", ones[:], ones_in[:, :], writes=[ones])
            KCs = P.sb(ps, "KCs", [128, 2, NQ], BF16)
            VCs = P.sb(ps, "VCs", [128, 2, NQT, 128], BF16)
            KDs = P.sb(ps, "KDs", [128, 8, NQ], BF16)
            VDs = P.sb(ps, "VDs", [128, 8, NQT, 128], BF16)
            S.dma("sp", KCs[:], KCT[:, :, :].rearrange("h p t -> p h t"), writes=[KCs])
            S.dma("sp", KDs[:], KDT[:, :, :].rearrange("h p t -> p h t"), writes=[KDs])
            for g in range(2):
                S.dma("sp", VCs[:, g, :, :], VC[g, :, :].rearrange("(t p) d -> p t d", p=128), writes=[VCs])
            for h in range(8):
                for t0_ in range(0, NQT, 11):
                    t1_ = min(NQT, t0_ + 11)
                    S.dma("sp", VDs[:, h, t0_:t1_, :], VD[h, t0_ * 128:t1_ * 128, :].rearrange("(t p) d -> p t d", p=128), writes=[VDs])
            esink = P.sb(ps, "esink", [128, 8], F32)
            P.load_bcast("sp", esink, c_sink[:])
            S.op("act", lambda e: e.activation(out=esink[:], in_=esink[:], func=AF.Exp), reads=[esink], writes=[esink])
            mC = P.sb(ps, "mC", [128, 3, 3, 128], F32)
            S.dma("sp", mC[:].rearrange("p v s q -> p (v s) q"), maskC[:, :, :, :].rearrange("v s k q -> k (v s) q"), writes=[mC])
            bD = [P.sb(ps, "bD%d" % i, [128, 6, 128], F32) for i in range(3)]
            qsb = [P.sb(ps, "qsb%d" % i, [128, 512], BF16) for i in range(3)]
            pt = [P.sb(ps, "pt%d" % i, [128, 512], BF16) for i in range(3)]
            sb_ = [P.sb(ps, "sbias%d" % i, [128, 512], F32) for i in range(3)]
            rden = [P.sb(ps, "rden%d" % i, [128, 512], F32) for i in range(2)]
            osb = [P.sb(ps, "osb%d" % i, [128, 512], BF16) for i in range(2)]
            pS = [P.pb(ps, "pS%d" % i, [128, 512], F32) for i in range(3)]
            pO = [P.pb(ps, "pO%d" % i, [128, 512], F32) for i in range(2)]
            pD = [P.pb(ps, "pD%d" % i, [128, 512], F32) for i in range(2)]
            scale = 128 ** -0.5
            glist = []
            for i in range(NOWN):
                glist += [("C", i, g) for g in range(2)]
                glist += [("D", i, h) for h in range(8)]

            def ginfo(gi):
                kind, i, hg = glist[gi]
                e_ = OWN0 + i
                if kind == "C":
                    keys = [(NQT - 2, None), (NQT - 1, None), (e_ - 1, 0), (e_, 1), (e_ + 1, 2)]
                    return kind, i, hg, e_, keys, 512
                lo = -3 if i == NOWN - 1 else -2
                hi = 3 if i == 0 else 2
                rel = list(range(lo, hi + 1))
                keys = [(NQT - 2, None), (NQT - 1, None)] + [(e_ + r_, si) for si, r_ in enumerate(rel)]
                return kind, i, hg, e_, keys, 128

            def load_q(gi):
                kind, i, hg, e_, keys, N = ginfo(gi)
                tsl = slice(e_ * 128, (e_ + 1) * 128)
                qs = qsb[gi % 3]
                if kind == "C":
                    S.dma("sp", qs[:].rearrange("p (h t) -> p h t", t=128), QCT[hg * 4:hg * 4 + 4, :, tsl].rearrange("h p t -> p h t"), writes=[qs])
                else:
                    vD = 0 if i == 0 else (1 if i == 1 else (3 if i == NOWN - 2 else (4 if i == NOWN - 1 else 2)))
                    bt = bD[gi % 3]
                    nsl = len(keys) - 2
                    S.dma("sp", qs[:, 0:128], QDT[hg, :, tsl], writes=[qs])
                    S.dma("sp", bt[:, 0:nsl, :], biasD[vD, hg, 0:nsl, :, :].rearrange("s k q -> k s q"), writes=[bt])

            load_q(0)
            load_q(1)
            step = 0
            for gi in range(len(glist)):
                kind, i, hg, e_, keys, N = ginfo(gi)
                if gi + 2 < len(glist):
                    load_q(gi + 2)
                tsl = slice(e_ * 128, (e_ + 1) * 128)
                gg = gi % 2
                qs = qsb[gi % 3]
                bt = bD[gi % 3]
                vC = 0 if i == 0 else (2 if i == NOWN - 1 else 1)
                nk = len(keys)
                base = step
                step += nk

                def qk(j):
                    kt, slot = keys[j]
                    sp_ = pS[(base + j) % 3]
                    ksl = slice(kt * 128, (kt + 1) * 128)
                    if kind == "C":
                        S.op("pe", lambda e: e.matmul(sp_[:, 0:N], KCs[:, hg, ksl], qs[:, 0:N], start=True, stop=True), reads=[KCs, qs], writes=[sp_])
                    else:
                        S.op("pe", lambda e: e.matmul(sp_[:, 0:N], KDs[:, hg, ksl], qs[:, 0:N], start=True, stop=True), reads=[KDs, qs], writes=[sp_])

                def ex(j):
                    kt, slot = keys[j]
                    sp_ = pS[(base + j) % 3]
                    ptt = pt[(base + j) % 3]
                    if slot is None:
                        S.op("act", lambda e: e.activation(out=ptt[:, 0:N], in_=sp_[:, 0:N], func=AF.Exp, scale=scale), reads=[sp_], writes=[ptt])
                        return
                    sbt = sb_[(base + j) % 3]
                    if kind == "C":
                        S.op("dve", lambda e: e.scalar_tensor_tensor(sbt[:].rearrange("p (h t) -> p h t", t=128), sp_[:, :].rearrange("p (h t) -> p h t", t=128), scale,
                                                                      mC[:, vC, slot, :].unsqueeze(1).to_broadcast([128, 4, 128]), ALU.mult, ALU.add), reads=[sp_, mC], writes=[sbt])
                    else:
                        S.op("dve", lambda e: e.scalar_tensor_tensor(sbt[:, 0:128], sp_[:, 0:128], scale, bt[:, slot, :], ALU.mult, ALU.add), reads=[sp_, bt], writes=[sbt])
                    S.op("act", lambda e: e.activation(out=ptt[:, 0:N], in_=sbt[:, 0:N], func=AF.Exp), reads=[sbt], writes=[ptt])

                def pv(j):
                    kt, slot = keys[j]
                    ptt = pt[(base + j) % 3]
                    vv = VCs[:, hg, kt, :] if kind == "C" else VDs[:, hg, kt, :]
                    S.op("pe", lambda e: e.matmul(pO[gg][:, 0:N], vv, ptt[:, 0:N], start=(j == 0), stop=(j == nk - 1)), reads=[VCs if kind == "C" else VDs, ptt], writes=[pO[gg]])
                    S.op("pe", lambda e: e.matmul(pD[gg][:, 0:N], ones[:], ptt[:, 0:N], start=(j == 0), stop=(j == nk - 1)), reads=[ones, ptt], writes=[pD[gg]])

                qk(0)
                for j in range(nk):
                    if j + 1 < nk:
                        qk(j + 1)
                    ex(j)
                    pv(j)
                rd = rden[gg]
                ob = osb[gg]
                if kind == "C":
                    S.op("dve", lambda e: e.tensor_tensor(rd[:].rearrange("p (h t) -> p h t", t=128), pD[gg][:, :].rearrange("p (h t) -> p h t", t=128),
                                                          esink[:, hg * 4:hg * 4 + 4].unsqueeze(2).to_broadcast([128, 4, 128]), ALU.add), reads=[pD[gg], esink], writes=[rd])
                    S.op("dve", lambda e: e.reciprocal(rd[:], rd[:]), reads=[rd], writes=[rd])
                    S.op("dve", lambda e: e.tensor_tensor(ob[:], pO[gg][:, :], rd[:], ALU.mult), reads=[pO[gg], rd], writes=[ob])
                    S.dma("pool", OT1[hg * 4:hg * 4 + 4, :, tsl].rearrange("h p t -> p h t"), ob[:].rearrange("p (h t) -> p h t", t=128), reads=[ob])
                else:
                    S.op("dve", lambda e: e.reciprocal(rd[:, 0:128], pD[gg][:, 0:128]), reads=[pD[gg]], writes=[rd])
                    S.op("dve", lambda e: e.tensor_tensor(ob[:, 0:128], pO[gg][:, 0:128], rd[:, 0:128], ALU.mult), reads=[pO[gg], rd], writes=[ob])
                    S.dma("pool", OT1[8 + hg, :, tsl], ob[:, 0:128], reads=[ob])
            S.barrier()
        if cfg.dbg:
            S.dma("sp", dbg["d_OT1"][:, :, :], OT1[:, :, :])
            S.barrier()


        own = list(range(OWN0, OWN0 + NOWN))
        out_phase(1, own, OT1, cd_w_out, h1, hmid1)
        if cfg.dbg:
            S.dma("sp", dbg["d_hmid1"][:, :], hmid1[:, :])
            S.barrier()
        ffn_phase(1, own, hmid1, lambda t: yout[(t - OWN0) * 128:(t - OWN0 + 1) * 128, :])
        S.barrier()
        P.nins = S.nins
    return P


GRID_W = 64
L_SEQ = 8192


def _rope_tables(tok, dim):
    tok = np.asarray(tok)
    valid = tok >= 0
    t = np.where(valid, tok, 0)
    row = (t // GRID_W).astype(np.float32)
    col = (t % GRID_W).astype(np.float32)
    half = dim // 2
    inv = (np.float32(10000.0) ** (-np.arange(0, half, 2, dtype=np.float32) / np.float32(half))).astype(np.float32)
    ar = row[:, None] * inv[None, :]
    ac = col[:, None] * inv[None, :]
    ang = np.concatenate([ar, ar, ac, ac], axis=-1).astype(np.float32)
    cos = np.cos(ang).astype(np.float32)
    sin = np.sin(ang).astype(np.float32)
    q = dim // 4
    sgn = np.concatenate([-np.ones(q), np.ones(q), -np.ones(q), np.ones(q)]).astype(np.float32)
    sin = sin * sgn[None, :]
    cos[~valid] = 1.0
    sin[~valid] = 0.0
    return cos, sin


def _rope_pack(tok):
    n = len(tok)
    out = np.zeros((4, n, 128), np.float32)
    c, s = _rope_tables(tok, 128)
    out[0], out[1] = c, s
    c, s = _rope_tables(tok, 64)
    out[2, :, :64], out[3, :, :64] = c, s
    return out


def _mask_c(cc, nown=16):
    kk = np.arange(128)[:, None]
    qq = np.arange(128)[None, :]
    m = np.zeros((3, 3, 128, 128), np.float32)
    for v in range(3):
        m[v, 0] = np.where(kk >= qq, 0.0, NEG)
        m[v, 2] = np.where(kk <= qq, 0.0, NEG)
    if cc == 0:
        m[0, 0] = NEG
    if cc == 3:
        m[2, 2] = NEG
    return m


def _bias_d(cc, rpb, nown=16):
    out = np.full((5, 8, 6, 128, 128), NEG, np.float32)
    rows = L_SEQ // GRID_W
    var_tiles = [0, 1, 2, nown - 2, nown - 1]
    for v, i in enumerate(var_tiles):
        lo = -3 if i == nown - 1 else -2
        hi = 3 if i == 0 else 2
        qtok = 2048 * cc + 128 * i + np.arange(128)
        qr = qtok // GRID_W
        qc = qtok % GRID_W
        rs = np.clip(qr - 4, 0, rows - 8)
        cs = np.clip(qc - 8, 0, GRID_W - 16)
        for si, r_ in enumerate(range(lo, hi + 1)):
            ktok = 2048 * cc + 128 * (i + r_) + np.arange(128)
            inb = (ktok >= 0) & (ktok < L_SEQ)
            kr = np.where(inb, ktok, 0) // GRID_W
            kc = np.where(inb, ktok, 0) % GRID_W
            ok = inb[:, None] & (kr[:, None] >= rs[None, :]) & (kr[:, None] < rs[None, :] + 8) & (kc[:, None] >= cs[None, :]) & (kc[:, None] < cs[None, :] + 16)
            di = np.clip(kr[:, None] - qr[None, :] + 7, 0, 14)
            dj = np.clip(kc[:, None] - qc[None, :] + 15, 0, 30)
            g = rpb[:, di, dj]
            out[v, :, si] = np.where(ok[None], g, NEG)
    return out


def _host_inputs(inputs, cfg=None):
    f = lambda a: np.ascontiguousarray(np.asarray(a, dtype=np.float32))
    x = f(inputs["x"])
    ctx = f(inputs["ctx"])
    c = f(inputs["c"])
    c_ctx = f(inputs["c_ctx"])
    shared = {
        "ident": np.eye(128, dtype=np.float32), "ones": np.ones((128, 128), np.float32),
        "w_mod": f(inputs["w_mod"]), "b_mod": f(inputs["b_mod"]),
        "g_mix_pre": f(inputs["g_mix_pre"]), "g_mix_post": f(inputs["g_mix_post"]),
        "g_ffn_pre": f(inputs["g_ffn_pre"]), "g_ffn_post": f(inputs["g_ffn_post"]),
        "w_gate": f(inputs["w_gate"]), "w_up": f(inputs["w_up"]), "w_down": f(inputs["w_down"]),
        "ab_w_in": f(inputs["ab_w_in"][0]), "ab_w_out": f(inputs["ab_w_out"][0]),
        "a_q_norm": f(inputs["a_q_norm"][0]), "a_k_norm": f(inputs["a_k_norm"][0]),
        "b_q_norm": f(inputs["b_q_norm"][0]), "b_kv_norm": f(inputs["b_kv_norm"][0]),
        "cd_w_in": f(inputs["cd_w_in"][0]), "cd_w_out": f(inputs["cd_w_out"][0]),
        "c_q_norm": f(inputs["c_q_norm"][0]), "c_k_norm": f(inputs["c_k_norm"][0]), "c_sink": f(inputs["c_sink"][0]),
        "d_q_norm": f(inputs["d_q_norm"][0]), "d_k_norm": f(inputs["d_k_norm"][0]),
    }
    wuq = f(inputs["b_w_uq"][0]).reshape(512, 8, 192)
    shared["w_uq"] = np.ascontiguousarray(np.concatenate([wuq[:, :, :128].reshape(512, 1024), wuq[:, :, 128:].reshape(512, 512)], axis=1))
    wukv = f(inputs["b_w_ukv"][0]).reshape(512, 8, 256)
    shared["w_ukv"] = np.ascontiguousarray(np.concatenate([wukv[:, :, :128].reshape(512, 1024), wukv[:, :, 128:].reshape(512, 1024)], axis=1))
    rpb = f(inputs["d_rpb"][0])
    maps = []
    ktok = np.concatenate([np.arange(L_SEQ), -np.ones(256, np.int64)])
    ropek = _rope_pack(ktok)
    for core in range(8):
        b, cc = core // 4, core % 4
        m = dict(shared)
        tok = np.arange(2048 * cc - 256, 2048 * cc + 2048 + 256)
        inb = (tok >= 0) & (tok < L_SEQ)
        tokc = np.clip(tok, 0, L_SEQ - 1)
        m["xq"] = np.ascontiguousarray(np.concatenate([x[b][tokc], ctx[b]], axis=0))
        m["xkv"] = np.ascontiguousarray(np.concatenate([x[b], ctx[b]], axis=0))
        m["cT"] = np.ascontiguousarray(np.stack([c[b].reshape(KC, 128).T, c_ctx.reshape(KC, 128).T], axis=-1))
        m["ropeq"] = _rope_pack(np.concatenate([np.where(inb, tok, -1), -np.ones(256, np.int64)]))
        m["ropek"] = ropek
        m["maskC"] = _mask_c(cc)
        m["biasD"] = _bias_d(cc, rpb)
        maps.append(m)
    return maps


_CACHE = {}


def kernel(**inputs):
    cfg = Cfg()
    if "prog" not in _CACHE:
        _CACHE["prog"] = build(cfg)
    P = _CACHE["prog"]
    maps = _host_inputs(inputs)
    res = run_bass_kernel_spmd(P.nc, maps, core_ids=list(range(8)))
    out = np.zeros((2, L_SEQ, D), np.float32)
    for core in range(8):
        b, cc = core // 4, core % 4
        out[b, 2048 * cc:2048 * cc + 2048] = np.asarray(res.results[core]["y"], dtype=np.float32)
    return out
```

```python
import numpy as np
from contextlib import ExitStack
import concourse.bass as bass
import concourse.mybir as mybir
from concourse.bass_utils import run_bass_kernel_spmd

F32 = mybir.dt.float32
BF16 = mybir.dt.bfloat16
AF = mybir.ActivationFunctionType
ALU = mybir.AluOpType
AX = mybir.AxisListType
ND = 12

D = 2048
KC = 16
DFF = 5632
FC = 44
EPS = 1e-6
NEG = -1e30


class Res:
    __slots__ = ("w", "rs", "name")

    def __init__(self, name=""):
        self.w = None
        self.rs = {}
        self.name = name


class T:
    def __init__(self, t, name):
        self.t = t
        self.r = Res(name)

    def __getitem__(self, k):
        return self.t[k]


def _res(x):
    return x.r if isinstance(x, T) else x


class Sch:
    def __init__(self, nc, es):
        self.nc = nc
        self.eng = {"pe": nc.tensor, "act": nc.scalar, "dve": nc.vector, "pool": nc.gpsimd, "sp": nc.sync}
        self.semobj = {}
        self.cnt = {}
        for k in ["pe", "act", "dve", "pool"]:
            self.semobj[k] = es.enter_context(nc.semaphore("s_" + k))
            self.cnt[k] = 0
        self.known = {k: {} for k in self.eng}
        self.dcnt = {}
        self.dnext = {}
        for q in ["sp", "pool"]:
            for i in range(ND):
                self.semobj[(q, i)] = es.enter_context(nc.semaphore("d_%s_%d" % (q, i)))
                self.dcnt[(q, i)] = 0
            self.dnext[q] = 0
        self.nins = 0

    def _wait(self, e, deps):
        need = {}
        kn = self.known[e]
        for d in deps:
            if d is None:
                continue
            k, v = d
            if kn.get(k, 0) >= v:
                continue
            if need.get(k, 0) < v:
                need[k] = v
        for k, v in need.items():
            self.eng[e].wait_ge(self.semobj[k], v)
            kn[k] = v
            self.nins += 1

    @staticmethod
    def _deps(reads, writes):
        deps = []
        for r in reads:
            deps.append(_res(r).w)
        for w in writes:
            w = _res(w)
            deps.append(w.w)
            deps.extend(w.rs.items())
        return deps

    @staticmethod
    def _mark(ev, reads, writes):
        k, v = ev
        for r in reads:
            r = _res(r)
            if r.rs.get(k, 0) < v:
                r.rs[k] = v
        for w in writes:
            w = _res(w)
            w.w = ev
            w.rs = {}

    def op(self, e, fn, reads=(), writes=()):
        deps = self._deps(reads, writes)
        if e == "pe":
            deps = [d for d in deps if d is not None and d[0] != "pe"]
        self._wait(e, deps)
        ins = fn(self.eng[e])
        self.cnt[e] += 1
        ins.then_inc(self.semobj[e], 1)
        self.nins += 1
        ev = (e, self.cnt[e])
        self._mark(ev, reads, writes)
        return ev

    def dma(self, q, out, in_, reads=(), writes=()):
        i = self.dnext[q]
        self.dnext[q] = (i + 1) % ND
        key = (q, i)
        deps = self._deps(reads, writes)
        if self.dcnt[key] > 0:
            deps.append((key, self.dcnt[key]))
        self._wait(q, deps)
        ins = self.eng[q].dma_start(out=out, in_=in_)
        self.dcnt[key] += 16
        ins.then_inc(self.semobj[key], 16)
        self.nins += 1
        ev = (key, self.dcnt[key])
        self._mark(ev, reads, writes)
        return ev

    def barrier(self):
        deps = [(k, v) for k, v in self.cnt.items() if v > 0]
        deps += [(k, v) for k, v in self.dcnt.items() if v > 0]
        for e in ("pe", "act", "dve", "pool", "sp"):
            self._wait(e, deps)


class Cfg:
    def __init__(self, nqt=22, nkt=66, own0=2, nown=16, dbg=False):
        self.NQT = nqt
        self.NKT = nkt
        self.OWN0 = own0
        self.NOWN = nown
        self.dbg = dbg


class Prog:
    def __init__(self, cfg):
        self.cfg = cfg
        self.nc = bass.Bass("TRN2", target_bir_lowering=False)
        self.es = ExitStack()
        self.S = None
        self.din = {}
        self.uid = 0

    def inp(self, name, shape, dt=F32):
        self.din[name] = self.nc.dram_tensor(name, list(shape), dt, kind="ExternalInput").ap()
        return self.din[name]

    def scratch(self, name, shape, dt):
        return self.nc.dram_tensor(name, list(shape), dt, kind="Internal").ap()

    def sb(self, ps, name, shape, dt):
        self.uid += 1
        return T(ps.enter_context(self.nc.sbuf_tensor("%s_%d" % (name, self.uid), list(shape), dt)), name)

    def pb(self, ps, name, shape, dt):
        self.uid += 1
        return T(ps.enter_context(self.nc.psum_tensor("%s_%d" % (name, self.uid), list(shape), dt)), name)

    def rstd_from_ss(self, ss, col, n):
        S = self.S
        a = ss[:, col]
        S.op("act", lambda e: e.activation(out=a, in_=a, func=AF.Sqrt, bias=EPS, scale=1.0 / n), reads=[ss], writes=[ss])
        S.op("dve", lambda e: e.reciprocal(a, a), reads=[ss], writes=[ss])

    def load_bcast(self, q, dst, src_ap):
        self.S.dma(q, dst[:], src_ap.partition_broadcast(128), writes=[dst])

    def front(self, ph, xt, Avec, shvec):
        S = self.S
        ss, junk, ub, uT, pT, ident = ph.ss, ph.junk, ph.ub, ph.uT, ph.pT, ph.ident
        S.op("act", lambda e: e.activation(out=junk[:], in_=xt[:], func=AF.Square, accum_out=ss[:, 0:1]), reads=[xt], writes=[junk, ss])
        self.rstd_from_ss(ss, slice(0, 1), D)
        S.op("dve", lambda e: e.scalar_tensor_tensor(xt[:], xt[:], ss[:, 0:1], Avec[:], ALU.mult, ALU.mult), reads=[xt, ss, Avec], writes=[xt])
        S.op("dve", lambda e: e.tensor_tensor(ub[:], xt[:], shvec[:], ALU.add), reads=[xt, shvec], writes=[ub])
        for half in range(2):
            p = pT[half]
            for j in range(8):
                kc = half * 8 + j
                S.op("pe", lambda e, p=p, j=j, kc=kc: e.transpose(p[:, j * 128:(j + 1) * 128], ub[:, kc * 128:(kc + 1) * 128], ident[:]), reads=[ub, ident], writes=[p])
            eng = "act" if half == 0 else "dve"
            dst = uT[:, half * 8:(half + 1) * 8, :].rearrange("p a b -> p (a b)")
            if eng == "act":
                S.op("act", lambda e, p=p, dst=dst: e.copy(dst, p[:]), reads=[p], writes=[uT])
            else:
                S.op("dve", lambda e, p=p, dst=dst: e.tensor_copy(dst, p[:]), reads=[p], writes=[uT])

    def front_alloc(self, ps, ph, share=None):
        ph.ss = self.sb(ps, "ss", [128, 16], F32)
        ph.junk = self.sb(ps, "junk", [128, D], BF16)
        ph.ub = self.sb(ps, "ub", [128, D], BF16)
        ph.uT = self.sb(ps, "uT", [128, KC, 128], BF16)
        if share is not None:
            ph.pT = share.pT
            ph.ident = share.ident
            return
        ph.pT = [self.pb(ps, "pT%d" % i, [128, 1024], BF16) for i in range(2)]
        ph.ident = self.sb(ps, "ident", [128, 128], BF16)
        self.S.dma("pool", ph.ident[:], self.din["ident"][:, :], writes=[ph.ident])

    def mod_vecs(self, ps, ph, layer, which, names):
        S = self.S
        for attr, kind, k, gname in names:
            t = self.sb(ps, attr, [128, D], F32)
            self.load_bcast("sp", t, self.modv[layer, which, k * D:(k + 1) * D])
            if kind != "S":
                g = ph.gtmp
                self.load_bcast("sp", g, self.din[gname][layer, :])
                if kind == "A":
                    S.op("dve", lambda e, t=t, g=g: e.scalar_tensor_tensor(t[:], t[:], 1.0, g[:], ALU.add, ALU.mult), reads=[t, g], writes=[t])
                else:
                    S.op("dve", lambda e, t=t, g=g: e.tensor_tensor(t[:], t[:], g[:], ALU.mult), reads=[t, g], writes=[t])
            setattr(ph, attr, t)

    def head_norm_rope(self, ph, src_ps, src_ap, H, hd, gain, cs, dstT, rope=True, norm=True):
        S = self.S
        qf = ph.qf
        n = H * hd
        v3 = lambda ap: ap.rearrange("p (h d) -> p h d", d=hd)
        S.op("act", lambda e: e.copy(qf[:, 0:n], src_ap), reads=[src_ps], writes=[qf])
        if norm:
            sq = ph.qsq
            S.op("dve", lambda e: e.tensor_tensor(sq[:, 0:n], qf[:, 0:n], qf[:, 0:n], ALU.mult), reads=[qf], writes=[sq])
            S.op("dve", lambda e: e.tensor_reduce(ph.ss[:, 2:2 + H], v3(sq[:, 0:n]), AX.X, ALU.add), reads=[sq], writes=[ph.ss])
            self.rstd_from_ss(ph.ss, slice(2, 2 + H), hd)
            S.op("dve", lambda e: e.tensor_tensor(v3(qf[:, 0:n]), v3(qf[:, 0:n]), ph.ss[:, 2:2 + H].unsqueeze(2).to_broadcast([128, H, hd]), ALU.mult), reads=[qf, ph.ss], writes=[qf])
            S.op("dve", lambda e: e.tensor_tensor(v3(qf[:, 0:n]), v3(qf[:, 0:n]), gain[:, 0:hd].unsqueeze(1).to_broadcast([128, H, hd]), ALU.mult), reads=[qf, gain], writes=[qf])
        if rope:
            csT, ci_, si_ = cs
            cosv = csT[:, ci_, :]
            sinv = csT[:, si_, :]
            q4 = hd // 4
            rot = ph.qsq
            x5 = lambda ap: ap.rearrange("p (h a b d) -> p h a b d", a=2, b=2, d=q4)
            s4 = lambda ap: ap.rearrange("p (a b d) -> p a b d", a=2, b=2, d=q4)
            for b0 in range(2):
                S.op("dve", lambda e, b0=b0: e.tensor_tensor(x5(rot[:, 0:n])[:, :, :, b0, :], x5(qf[:, 0:n])[:, :, :, 1 - b0, :],
                                                            s4(sinv[:, 0:hd])[:, :, b0, :].unsqueeze(1).to_broadcast([128, H, 2, q4]), ALU.mult), reads=[qf, csT], writes=[rot])
            S.op("dve", lambda e: e.tensor_tensor(v3(qf[:, 0:n]), v3(qf[:, 0:n]), cosv[:, 0:hd].unsqueeze(1).to_broadcast([128, H, hd]), ALU.mult), reads=[qf, csT], writes=[qf])
            S.op("dve", lambda e: e.tensor_tensor(dstT[:, 0:n], qf[:, 0:n], rot[:, 0:n], ALU.add), reads=[qf, rot], writes=[dstT])
        else:
            S.op("dve", lambda e: e.tensor_copy(dstT[:, 0:n], qf[:, 0:n]), reads=[qf], writes=[dstT])

    def transpose_out(self, ph, src, nblk, pTt, dst_sb, dram_ap, rows=128):
        S = self.S
        for j in range(nblk):
            S.op("pe", lambda e, j=j: e.transpose(pTt[:, j * 128:(j + 1) * 128], src[:, j * 128:(j + 1) * 128], ph.ident[:]), reads=[src, ph.ident], writes=[pTt])
        S.op("act", lambda e: e.copy(dst_sb[:, 0:nblk * 128], pTt[:, 0:nblk * 128]), reads=[pTt], writes=[dst_sb])
        S.dma("pool", dram_ap, dst_sb[0:rows, 0:nblk * 128].rearrange("p (h t) -> p h t", t=128), reads=[dst_sb])


def _lin(S, ps_tile, out_ap, uT, W, c0, c1, kcn=KC):
    for kc in range(kcn):
        S.op("pe", lambda e, kc=kc: e.matmul(out_ap, uT[:, kc, :], W[:, kc, c0:c1], start=(kc == 0), stop=(kc == kcn - 1)), reads=[uT, W], writes=[ps_tile])


class Ph:
    pass


def build(cfg):
    P = Prog(cfg)
    nc = P.nc
    NQT, NKT = cfg.NQT, cfg.NKT
    NQ = NQT * 128
    NK = NKT * 128
    NOWN = cfg.NOWN
    OWN0 = cfg.OWN0
    inp = P.inp
    xq = inp("xq", [NQ, D])
    xkv = inp("xkv", [NK, D])
    cT = inp("cT", [128, KC, 2])
    inp("ident", [128, 128])
    ones_in = inp("ones", [128, 128])
    w_mod = inp("w_mod", [2, D, 6 * D])
    b_mod = inp("b_mod", [2, 6 * D])
    for n in ["g_mix_pre", "g_mix_post", "g_ffn_pre", "g_ffn_post"]:
        inp(n, [2, D])
    w_gate = inp("w_gate", [2, D, DFF])
    w_up = inp("w_up", [2, D, DFF])
    w_down = inp("w_down", [2, DFF, D])
    ab_w_in = inp("ab_w_in", [D, 2624])
    ab_w_out = inp("ab_w_out", [D, D])
    a_q_norm = inp("a_q_norm", [128])
    a_k_norm = inp("a_k_norm", [128])
    b_q_norm = inp("b_q_norm", [512])
    b_kv_norm = inp("b_kv_norm", [512])
    w_uq = inp("w_uq", [512, 1536])
    w_ukv = inp("w_ukv", [512, 2048])
    cd_w_in = inp("cd_w_in", [D, 4608])
    cd_w_out = inp("cd_w_out", [D, D])
    c_q_norm = inp("c_q_norm", [128])
    c_k_norm = inp("c_k_norm", [128])
    c_sink = inp("c_sink", [8])
    d_q_norm = inp("d_q_norm", [128])
    d_k_norm = inp("d_k_norm", [128])
    ropeq = inp("ropeq", [4, NQ, 128])
    ropek = inp("ropek", [4, NK, 128])
    maskC = inp("maskC", [3, 3, 128, 128])
    biasD = inp("biasD", [5, 8, 6, 128, 128])
    yout = nc.dram_tensor("y", [NOWN * 128, D], F32, kind="ExternalOutput").ap()
    sc = P.scratch
    P.modv = sc("modv", [2, 2, 6 * D], F32)
    KAT = sc("KAT", [2, 128, NK], BF16)
    VA = sc("VA", [2, NK, 128], BF16)
    KBT = sc("KBT", [8, 128, NK], BF16)
    KRT = sc("KRT", [128, NK], BF16)
    VB = sc("VB", [8, NK, 128], BF16)
    QAT = sc("QAT", [8, 128, NQ], BF16)
    QBT = sc("QBT", [8, 128, NQ], BF16)
    QRT = sc("QRT", [4, 128, NQ], BF16)
    OT = sc("OT", [16, 128, NQ], BF16)
    hmid = sc("hmid", [NQ, D], F32)
    h1 = sc("h1", [NQ, D], F32)
    QCT = sc("QCT", [8, 128, NQ], BF16)
    KCT = sc("KCT", [2, 128, NQ], BF16)
    VC = sc("VC", [2, NQ, 128], BF16)
    QDT = sc("QDT", [8, 128, NQ], BF16)
    KDT = sc("KDT", [8, 128, NQ], BF16)
    VD = sc("VD", [8, NQ, 128], BF16)
    OT1 = sc("OT1", [16, 128, NQ], BF16)
    hmid1 = sc("hmid1", [NQ, D], F32)
    dbg = {}
    if cfg.dbg:
        for n, shp in [("d_modv", [2, 2, 6 * D]), ("d_hmid", [NQ, D]), ("d_h1", [NQ, D]), ("d_hmid1", [NQ, D])]:
            dbg[n] = nc.dram_tensor(n, shp, F32, kind="ExternalOutput").ap()
        for n, shp in [("d_KAT", [2, 128, NK]), ("d_KBT", [8, 128, NK]), ("d_KRT", [128, NK]), ("d_VA", [2, NK, 128]), ("d_VB", [8, NK, 128]),
                       ("d_QAT", [8, 128, NQ]), ("d_QBT", [8, 128, NQ]), ("d_QRT", [4, 128, NQ]), ("d_OT", [16, 128, NQ]), ("d_OT1", [16, 128, NQ])]:
            dbg[n] = nc.dram_tensor(n, shp, BF16, kind="ExternalOutput").ap()

    with P.es as es:
        S = Sch(nc, es)
        P.S = S

        with ExitStack() as ps:
            cTt = P.sb(ps, "cT", [128, KC, 2], F32)
            scb = P.sb(ps, "scb", [128, KC, 2], BF16)
            wm = [P.sb(ps, "wm%d" % i, [128, KC, 512], BF16) for i in range(2)]
            bmt = [P.sb(ps, "bm%d" % i, [2, 512], F32) for i in range(2)]
            mo = [P.sb(ps, "mo%d" % i, [2, 512], F32) for i in range(2)]
            pm = [P.pb(ps, "pm%d" % i, [128, 512], F32) for i in range(2)]
            S.dma("sp", cTt[:], cT[:, :, :], writes=[cTt])
            S.op("act", lambda e: e.activation(out=scb[:], in_=cTt[:], func=AF.Silu), reads=[cTt], writes=[scb])
            it = 0
            for l in range(2):
                for cg in range(24):
                    b = it % 2
                    it += 1
                    S.dma("pool", wm[b][:], w_mod[l, :, cg * 512:(cg + 1) * 512].rearrange("(kc p) n -> p kc n", p=128), writes=[wm[b]])
                    S.dma("sp", bmt[b][:], b_mod[l, cg * 512:(cg + 1) * 512].partition_broadcast(2), writes=[bmt[b]])
                    for kc in range(KC):
                        S.op("pe", lambda e, kc=kc, b=b: e.matmul(pm[b][0:2, :], scb[:, kc, :], wm[b][:, kc, :], start=(kc == 0), stop=(kc == KC - 1)), reads=[scb, wm[b]], writes=[pm[b]])
                    S.op("dve", lambda e, b=b: e.tensor_tensor(mo[b][:], pm[b][0:2, :], bmt[b][:], ALU.add), reads=[pm[b], bmt[b]], writes=[mo[b]])
                    S.dma("sp", P.modv[l, :, cg * 512:(cg + 1) * 512], mo[b][:], reads=[mo[b]])
            S.barrier()
        if cfg.dbg:
            S.dma("sp", dbg["d_modv"][:, :, :], P.modv[:, :, :])
            S.barrier()

        def run_rr(gens):
            gens = list(gens)
            while gens:
                for g_ in list(gens):
                    try:
                        next(g_)
                    except StopIteration:
                        gens.remove(g_)

        def qkv_phase(layer, nt, src, rope_t, ctx_from, Wdram, col_specs, mod_names_gain):
            with ExitStack() as ps:
                ph0 = Ph()
                P.front_alloc(ps, ph0)
                ph1 = Ph()
                P.front_alloc(ps, ph1, share=ph0)
                phs = [ph0, ph1]
                ident = ph0.ident
                xin = [P.sb(ps, "xin%d" % i, [128, D], F32) for i in range(2)]
                ph0.gtmp = xin[1]
                for which, sfx in ((0, "l"), (1, "c")):
                    P.mod_vecs(ps, ph0, layer, which, [("A" + sfx, "A", 1, mod_names_gain), ("S" + sfx, "S", 0, None)])
                ncols = sum(c["n"] for c in col_specs)
                Wsb = P.sb(ps, "Wsb", [128, KC, ncols], BF16)
                off = 0
                for c in col_specs:
                    c["off"] = off
                    S.dma("pool", Wsb[:, :, off:off + c["n"]], Wdram[:, c["c0"]:c["c0"] + c["n"]].rearrange("(kc p) n -> p kc n", p=128), writes=[Wsb])
                    off += c["n"]
                gains = {}
                gl_all = {"a_q_norm": a_q_norm, "a_k_norm": a_k_norm, "b_q_norm": b_q_norm, "b_kv_norm": b_kv_norm,
                          "c_q_norm": c_q_norm, "c_k_norm": c_k_norm, "d_q_norm": d_q_norm, "d_k_norm": d_k_norm}
                for c in col_specs:
                    gn = c.get("gain")
                    if gn is not None and gn not in gains:
                        n = 512 if gn.startswith("b_") else 128
                        gt = P.sb(ps, gn, [128, n], F32)
                        P.load_bcast("sp", gt, gl_all[gn][:])
                        gains[gn] = gt
                cst = [P.sb(ps, "cs%d" % i, [128, 4, 128], F32) for i in range(2)]
                pq = [P.pb(ps, "pq%d" % i, [128, 512], F32) for i in range(4)]
                pT2 = [P.pb(ps, "pT2%d" % i, [128, 1024], BF16) for i in range(1)]
                pq_extra = P.pb(ps, "pqx", [128, 512], F32)
                ctr = {"pq": 0, "pt": 0}

                def next_pq():
                    ctr["pq"] += 1
                    return pq[ctr["pq"] % 4]

                def next_pt():
                    ctr["pt"] += 1
                    return pT2[0]

                extra = {}
                for c in col_specs:
                    if c["kind"] in ("mla_q", "mla_kv"):
                        extra["cT"] = P.sb(ps, "cnT", [128, 4, 128], BF16)
                        wn = "wuq" if c["kind"] == "mla_q" else "wukv"
                        wsrc = w_uq if c["kind"] == "mla_q" else w_ukv
                        ncol = 1536 if c["kind"] == "mla_q" else 2048
                        extra[wn] = P.sb(ps, wn, [128, 4, ncol], BF16)
                        S.dma("pool", extra[wn][:], wsrc[:, :].rearrange("(kc p) n -> p kc n", p=128), writes=[extra[wn]])
                chains_par = []
                for par_ in range(2):
                  chains = []
                  for ci, c in enumerate(col_specs):
                    for s0 in range(0, c["n"], 512):
                        s1 = min(c["n"], s0 + 512)
                        w = s1 - s0
                        b = Ph()
                        b.c, b.s0, b.s1, b.w = c, s0, s1, w
                        b.pp = pq[len(chains) % 4]
                        b.ss = P.sb(ps, "css", [128, 16], F32)
                        if c["kind"] != "v":
                            b.qf = P.sb(ps, "cqf", [128, max(w, 128)], F32)
                            b.qsq = P.sb(ps, "cqsq", [128, max(w, 128)], F32)
                            b.hT = P.sb(ps, "chT", [128, 512], BF16)
                        b.hb = P.sb(ps, "chb", [128, max(w, 128)], BF16)
                        if c["kind"] == "mla_q":
                            b.qf2 = P.sb(ps, "cqf2", [128, 512], F32)
                            b.qsq2 = P.sb(ps, "cqsq2", [128, 512], F32)
                            b.hb2 = P.sb(ps, "chb2", [128, 512], BF16)
                            b.hT2 = P.sb(ps, "chT2", [128, 512], BF16)
                        if c["kind"] == "mla_kv":
                            b.hb2 = P.sb(ps, "chb2", [128, 512], BF16)
                            b.hT2 = P.sb(ps, "chT2", [128, 512], BF16)
                        chains.append(b)
                  chains_par.append(chains)
                mla_lock = {"held": False}

                def g_norm_rope(b, qf, qsq, n, H, hd, gain, cs, dst, rope, norm):
                    v3 = lambda ap: ap.rearrange("p (h d) -> p h d", d=hd)
                    if norm:
                        S.op("dve", lambda e: e.tensor_tensor(qsq[:, 0:n], qf[:, 0:n], qf[:, 0:n], ALU.mult), reads=[qf], writes=[qsq])
                        yield
                        S.op("dve", lambda e: e.tensor_reduce(b.ss[:, 0:H], v3(qsq[:, 0:n]), AX.X, ALU.add), reads=[qsq], writes=[b.ss])
                        yield
                        S.op("act", lambda e: e.activation(out=b.ss[:, 0:H], in_=b.ss[:, 0:H], func=AF.Sqrt, bias=EPS, scale=1.0 / hd), reads=[b.ss], writes=[b.ss])
                        yield
                        S.op("dve", lambda e: e.reciprocal(b.ss[:, 0:H], b.ss[:, 0:H]), reads=[b.ss], writes=[b.ss])
                        yield
                        S.op("dve", lambda e: e.tensor_tensor(v3(qf[:, 0:n]), v3(qf[:, 0:n]), b.ss[:, 0:H].unsqueeze(2).to_broadcast([128, H, hd]), ALU.mult), reads=[qf, b.ss], writes=[qf])
                        yield
                        S.op("dve", lambda e: e.tensor_tensor(v3(qf[:, 0:n]), v3(qf[:, 0:n]), gain[:, 0:hd].unsqueeze(1).to_broadcast([128, H, hd]), ALU.mult), reads=[qf, gain], writes=[qf])
                        yield
                    if rope:
                        csT, ci_, si_ = cs
                        cosv = csT[:, ci_, :]
                        sinv = csT[:, si_, :]
                        q4 = hd // 4
                        x5 = lambda ap: ap.rearrange("p (h a b d) -> p h a b d", a=2, b=2, d=q4)
                        s4 = lambda ap: ap.rearrange("p (a b d) -> p a b d", a=2, b=2, d=q4)
                        for b0 in range(2):
                            S.op("dve", lambda e, b0=b0: e.tensor_tensor(x5(qsq[:, 0:n])[:, :, :, b0, :], x5(qf[:, 0:n])[:, :, :, 1 - b0, :],
                                                                        s4(sinv[:, 0:hd])[:, :, b0, :].unsqueeze(1).to_broadcast([128, H, 2, q4]), ALU.mult), reads=[qf, csT], writes=[qsq])
                            yield
                        S.op("dve", lambda e: e.tensor_tensor(v3(qf[:, 0:n]), v3(qf[:, 0:n]), cosv[:, 0:hd].unsqueeze(1).to_broadcast([128, H, hd]), ALU.mult), reads=[qf, csT], writes=[qf])
                        yield
                        S.op("dve", lambda e: e.tensor_tensor(dst[:, 0:n], qf[:, 0:n], qsq[:, 0:n], ALU.add), reads=[qf, qsq], writes=[dst])
                        yield
                    else:
                        S.op("dve", lambda e: e.tensor_copy(dst[:, 0:n], qf[:, 0:n]), reads=[qf], writes=[dst])
                        yield

                def g_tr_out(srcb, nblk, dstb, dram_ap):
                    pTt = next_pt()
                    for j in range(nblk):
                        S.op("pe", lambda e, j=j: e.transpose(pTt[:, j * 128:(j + 1) * 128], srcb[:, j * 128:(j + 1) * 128], ident[:]), reads=[srcb, ident], writes=[pTt])
                    S.op("act", lambda e: e.copy(dstb[:, 0:nblk * 128], pTt[:, 0:nblk * 128]), reads=[pTt], writes=[dstb])
                    yield
                    S.dma("pool", dram_ap, dstb[:, 0:nblk * 128].rearrange("p (h t) -> p h t", t=128), reads=[dstb])
                    yield

                def g_chain(t, b, uT, cs):
                    c, s0, s1, w = b.c, b.s0, b.s1, b.w
                    kind = c["kind"]
                    if c.get("own_only") and not (OWN0 <= t < OWN0 + NOWN):
                        return
                    tsl = slice(t * 128, (t + 1) * 128)
                    H = max(w // 128, 1)
                    h0 = s0 // 128
                    pp = b.pp
                    _lin(S, pp, pp[:, 0:w], uT, Wsb, c["off"] + s0, c["off"] + s1)
                    yield
                    if kind == "v":
                        S.op("act", lambda e: e.copy(b.hb[:, 0:w], pp[:, 0:w]), reads=[pp], writes=[b.hb])
                        yield
                        S.dma("pool", c["dst"][h0:h0 + H, tsl, :].rearrange("h p d -> p h d"), b.hb[:, 0:w].rearrange("p (h d) -> p h d", d=128), reads=[b.hb])
                        yield
                        return
                    S.op("act", lambda e: e.copy(b.qf[:, 0:w], pp[:, 0:w]), reads=[pp], writes=[b.qf])
                    yield
                    if kind == "heads":
                        yield from g_norm_rope(b, b.qf, b.qsq, w, H, 128, gains[c["gain"]], (cs, 0, 1), b.hb, c["rope"], True)
                        yield from g_tr_out(b.hb, H, b.hT, c["dst"][h0:h0 + H, :, tsl].rearrange("h p t -> p h t"))
                    elif kind == "kr":
                        yield from g_norm_rope(b, b.qf, b.qsq, 64, 1, 64, None, (cs, 2, 3), b.hb, True, False)
                        S.op("dve", lambda e: e.tensor_copy(b.hb[:, 64:128], b.hb[:, 0:64]), reads=[b.hb], writes=[b.hb])
                        yield
                        yield from g_tr_out(b.hb, 1, b.hT, c["dst"][:, tsl].rearrange("p (h t) -> p h t", h=1))
                    else:
                        yield from g_norm_rope(b, b.qf, b.qsq, 512, 1, 512, gains[c["gain"]], None, b.hb, False, True)
                        cnT = extra["cT"]
                        while mla_lock["held"]:
                            yield
                        mla_lock["held"] = True
                        pTt = next_pt()
                        for j in range(4):
                            S.op("pe", lambda e, j=j: e.transpose(pTt[:, j * 128:(j + 1) * 128], b.hb[:, j * 128:(j + 1) * 128], ident[:]), reads=[b.hb, ident], writes=[pTt])
                        S.op("act", lambda e: e.copy(cnT[:].rearrange("p a b -> p (a b)"), pTt[:, 0:512]), reads=[pTt], writes=[cnT])
                        yield
                        W2 = extra["wuq" if kind == "mla_q" else "wukv"]
                        for hh in range(2):
                            pp2 = pq_extra
                            for h in range(4):
                                hd_ = hh * 4 + h
                                for kc in range(4):
                                    S.op("pe", lambda e, h=h, hd_=hd_, kc=kc: e.matmul(pp2[:, h * 128:(h + 1) * 128], W2[:, kc, hd_ * 128:(hd_ + 1) * 128], cnT[:, kc, :],
                                                                                      start=(kc == 0), stop=(kc == 3)), reads=[W2, cnT], writes=[pp2])
                            yield
                            hT_ = b.hT if hh == 0 else b.hT2
                            S.op("act", lambda e: e.copy(hT_[:, 0:512], pp2[:, 0:512]), reads=[pp2], writes=[hT_])
                            yield
                            S.dma("pool", c["dst"][hh * 4:hh * 4 + 4, :, tsl].rearrange("h p t -> p h t"), hT_[:, 0:512].rearrange("p (h t) -> p h t", t=128), reads=[hT_])
                            yield
                        if kind == "mla_q":
                            pp2 = pq_extra
                            _lin(S, pp2, pp2[:, 0:512], cnT, W2, 1024, 1536, kcn=4)
                            yield
                            S.op("act", lambda e: e.copy(b.qf2[:, 0:512], pp2[:, 0:512]), reads=[pp2], writes=[b.qf2])
                            yield
                            yield from g_norm_rope(b, b.qf2, b.qsq2, 512, 8, 64, None, (cs, 2, 3), b.hb2, True, False)
                            yield from g_tr_out(b.hb2, 4, b.hT, c["dst_r"][:, :, tsl].rearrange("h p t -> p h t"))
                        else:
                            for hh in range(2):
                                pp2 = pq_extra
                                _lin(S, pp2, pp2[:, 0:512], cnT, W2, 1024 + hh * 512, 1024 + (hh + 1) * 512, kcn=4)
                                yield
                                hb_ = b.hb if hh == 0 else b.hb2
                                S.op("act", lambda e: e.copy(hb_[:, 0:512], pp2[:, 0:512]), reads=[pp2], writes=[hb_])
                                yield
                                S.dma("pool", c["dst_v"][hh * 4:hh * 4 + 4, tsl, :].rearrange("h p d -> p h d"), hb_[:, 0:512].rearrange("p (h d) -> p h d", d=128), reads=[hb_])
                                yield

                        mla_lock["held"] = False

                def g_front(t):
                    is_ctx = t >= ctx_from
                    xt = xin[t % 2]
                    ph = phs[t % 2]
                    Avec = ph0.Ac if is_ctx else ph0.Al
                    shvec = ph0.Sc if is_ctx else ph0.Sl
                    S.dma("sp", xt[:], src[t * 128:(t + 1) * 128, :], writes=[xt])
                    S.dma("sp", cst[t % 2][:], rope_t[:, t * 128:(t + 1) * 128, :].rearrange("f p d -> p f d"), writes=[cst[t % 2]])
                    yield
                    S.op("act", lambda e: e.activation(out=ph.junk[:], in_=xt[:], func=AF.Square, accum_out=ph.ss[:, 0:1]), reads=[xt], writes=[ph.junk, ph.ss])
                    yield
                    S.op("act", lambda e: e.activation(out=ph.ss[:, 0:1], in_=ph.ss[:, 0:1], func=AF.Sqrt, bias=EPS, scale=1.0 / D), reads=[ph.ss], writes=[ph.ss])
                    yield
                    S.op("dve", lambda e: e.reciprocal(ph.ss[:, 0:1], ph.ss[:, 0:1]), reads=[ph.ss], writes=[ph.ss])
                    yield
                    S.op("dve", lambda e: e.scalar_tensor_tensor(xt[:], xt[:], ph.ss[:, 0:1], Avec[:], ALU.mult, ALU.mult), reads=[xt, ph.ss, Avec], writes=[xt])
                    yield
                    S.op("dve", lambda e: e.tensor_tensor(ph.ub[:], xt[:], shvec[:], ALU.add), reads=[xt, shvec], writes=[ph.ub])
                    yield
                    for half in range(2):
                        p = ph.pT[half]
                        for j in range(8):
                            kc = half * 8 + j
                            S.op("pe", lambda e, j=j, kc=kc: e.transpose(p[:, j * 128:(j + 1) * 128], ph.ub[:, kc * 128:(kc + 1) * 128], ident[:]), reads=[ph.ub, ident], writes=[p])
                        yield
                        dst = ph.uT[:, half * 8:(half + 1) * 8, :].rearrange("p a b -> p (a b)")
                        if half == 0:
                            S.op("act", lambda e: e.copy(dst, p[:]), reads=[p], writes=[ph.uT])
                        else:
                            S.op("dve", lambda e: e.tensor_copy(dst, p[:]), reads=[p], writes=[ph.uT])
                        yield

                nch = len(chains_par[0])
                done = set()
                steps = {}
                pending = []
                for t in range(nt):
                    pending.append(("F", t, -1))
                    for ci in range(nch):
                        pending.append(("C", t, ci))
                active = []

                def can_start(key):
                    kind, t, ci = key
                    if kind == "F":
                        if t >= 1 and ("F", t - 1, -1) not in done:
                            return False
                        if t >= 2:
                            for cj in range(nch):
                                if ("C", t - 2, cj) not in done:
                                    return False
                        return True
                    if ("F", t, -1) not in done:
                        return False
                    if t >= 2 and ("C", t - 2, ci) not in done:
                        return False
                    if t >= 1 and ("C", t - 1, ci) not in done and steps.get(("C", t - 1, ci), 0) < 3:
                        return False
                    return True

                while pending or active:
                    for key in list(pending):
                        if can_start(key):
                            pending.remove(key)
                            kind, t, ci = key
                            gen = g_front(t) if kind == "F" else g_chain(t, chains_par[t % 2][ci], phs[t % 2].uT, cst[t % 2])
                            active.append((key, gen))
                    for key, gen in list(active):
                        try:
                            next(gen)
                            steps[key] = steps.get(key, 0) + 1
                        except StopIteration:
                            active.remove((key, gen))
                            done.add(key)
                S.barrier()


        qkv_phase(0, NKT, xkv, ropek, NKT - 2, ab_w_in, [
            dict(kind="heads", c0=1024, n=256, gain="a_k_norm", rope=True, dst=KAT),
            dict(kind="v", c0=1280, n=256, dst=VA),
            dict(kind="mla_kv", c0=1536 + 512, n=512, gain="b_kv_norm", dst=KBT, dst_v=VB),
            dict(kind="kr", c0=1536 + 1024, n=64, dst=KRT),
        ], "g_mix_pre")
        qkv_phase(0, NQT, xq, ropeq, NQT - 2, ab_w_in, [
            dict(kind="heads", c0=0, n=1024, gain="a_q_norm", rope=True, dst=QAT),
            dict(kind="mla_q", c0=1536, n=512, gain="b_q_norm", dst=QBT, dst_r=QRT),
        ], "g_mix_pre")
        if cfg.dbg:
            for n_, src_ in [("d_KAT", KAT), ("d_KBT", KBT), ("d_VA", VA), ("d_VB", VB), ("d_QAT", QAT), ("d_QBT", QBT), ("d_QRT", QRT)]:
                S.dma("sp", dbg[n_][:, :, :], src_[:, :, :])
            S.dma("sp", dbg["d_KRT"][:, :], KRT[:, :])
            S.barrier()

        with ExitStack() as ps:
            ones = P.sb(ps, "ones", [128, 128], BF16)
            S.dma("pool", ones[:], ones_in[:, :], writes=[ones])
            Ksb = [P.sb(ps, "Ksb%d" % i, [128, NK], BF16) for i in range(2)]
            Vsb = [P.sb(ps, "Vsb%d" % i, [128, NKT, 128], BF16) for i in range(2)]
            KRsb = P.sb(ps, "KRsb", [128, NK], BF16)
            qsb = [P.sb(ps, "qsb%d" % i, [128, 512], BF16) for i in range(2)]
            qrsb = [P.sb(ps, "qrsb%d" % i, [128, 512], BF16) for i in range(2)]
            pt = [P.sb(ps, "pt%d" % i, [128, 512], BF16) for i in range(4)]
            rden = P.sb(ps, "rden", [128, 512], F32)
            osb = [P.sb(ps, "osb%d" % i, [128, 512], BF16) for i in range(2)]
            pS = [P.pb(ps, "pS%d" % i, [128, 512], F32) for i in range(4)]
            pO = [P.pb(ps, "pO%d" % i, [128, 512], F32) for i in range(2)]
            pD = [P.pb(ps, "pD%d" % i, [128, 512], F32) for i in range(2)]
            S.dma("sp", KRsb[:], KRT[:, :], writes=[KRsb])

            def load_kv(hh):
                kb = hh % 2
                if hh < 2:
                    S.dma("sp", Ksb[kb][:], KAT[hh, :, :], writes=[Ksb[kb]])
                    vsrc = VA[hh]
                else:
                    S.dma("sp", Ksb[kb][:], KBT[hh - 2, :, :], writes=[Ksb[kb]])
                    vsrc = VB[hh - 2]
                for t0_ in range(0, NKT, 11):
                    t1_ = min(NKT, t0_ + 11)
                    S.dma("sp", Vsb[kb][:, t0_:t1_, :], vsrc[t0_ * 128:t1_ * 128, :].rearrange("(t p) d -> p t d", p=128), writes=[Vsb[kb]])

            glist = []
            for hh in range(10):
                if hh < 2:
                    glist += [(hh, qt * 128, 512) for qt in range(NQT)]
                else:
                    glist += [(hh, t0 * 128, min(4, NQT - t0) * 128) for t0 in range(0, NQT, 4)]

            def load_q(gi):
                hh, tok0, N = glist[gi]
                g = gi % 2
                if hh < 2:
                    S.dma("sp", qsb[g][:].rearrange("p (h t) -> p h t", t=128), QAT[hh * 4:hh * 4 + 4, :, tok0:tok0 + 128].rearrange("h p t -> p h t"), writes=[qsb[g]])
                else:
                    h = hh - 2
                    S.dma("sp", qsb[g][:, 0:N], QBT[h, :, tok0:tok0 + N], writes=[qsb[g]])
                    S.dma("sp", qrsb[g][:, 0:N], QRT[h // 2, :, tok0:tok0 + N], writes=[qrsb[g]])

            load_kv(0)
            load_q(0)
            step = 0
            for gi, (hh, tok0, N) in enumerate(glist):
                g = gi % 2
                kb = hh % 2
                first_of_head = gi == 0 or glist[gi - 1][0] != hh
                if first_of_head and hh + 1 < 10:
                    load_kv(hh + 1)
                if gi + 1 < len(glist):
                    load_q(gi + 1)
                qs = qsb[g]
                qr = qrsb[g]
                scale = 128 ** -0.5 if hh < 2 else 192 ** -0.5
                if hh >= 2:
                    h = hh - 2
                    half = slice(0, 64) if h % 2 == 0 else slice(64, 128)
                kts = list(range(NKT - 2, NKT)) if tok0 >= (NQT - 2) * 128 else list(range(NKT))
                nk = len(kts)
                base = step
                step += nk

                def qk(i):
                    sp_ = pS[(base + i) % 4]
                    ksl = slice(kts[i] * 128, (kts[i] + 1) * 128)
                    if hh < 2:
                        S.op("pe", lambda e: e.matmul(sp_[:, 0:N], Ksb[kb][:, ksl], qs[:, 0:N], start=True, stop=True), reads=[Ksb[kb], qs], writes=[sp_])
                    else:
                        S.op("pe", lambda e: e.matmul(sp_[:, 0:N], Ksb[kb][:, ksl], qs[:, 0:N], start=True, stop=False), reads=[Ksb[kb], qs], writes=[sp_])
                        S.op("pe", lambda e: e.matmul(sp_[:, 0:N], KRsb[half, ksl], qr[half, 0:N], start=False, stop=True), reads=[KRsb, qr], writes=[sp_])

                def ex(i):
                    sp_ = pS[(base + i) % 4]
                    ptt = pt[(base + i) % 4]
                    S.op("act", lambda e: e.activation(out=ptt[:, 0:N], in_=sp_[:, 0:N], func=AF.Exp, scale=scale), reads=[sp_], writes=[ptt])

                def pv(i):
                    ptt = pt[(base + i) % 4]
                    S.op("pe", lambda e: e.matmul(pO[g][:, 0:N], Vsb[kb][:, kts[i], :], ptt[:, 0:N], start=(i == 0), stop=(i == nk - 1)), reads=[Vsb[kb], ptt], writes=[pO[g]])
                    S.op("pe", lambda e: e.matmul(pD[g][:, 0:N], ones[:], ptt[:, 0:N], start=(i == 0), stop=(i == nk - 1)), reads=[ones, ptt], writes=[pD[g]])

                qk(0)
                if nk > 1:
                    qk(1)
                for i in range(nk):
                    if i + 2 < nk:
                        qk(i + 2)
                    ex(i)
                    pv(i)
                S.op("dve", lambda e: e.reciprocal(rden[:, 0:N], pD[g][:, 0:N]), reads=[pD[g]], writes=[rden])
                ob = osb[g]
                S.op("dve", lambda e: e.tensor_tensor(ob[:, 0:N], pO[g][:, 0:N], rden[:, 0:N], ALU.mult), reads=[pO[g], rden], writes=[ob])
                if hh < 2:
                    S.dma("pool", OT[hh * 4:hh * 4 + 4, :, tok0:tok0 + 128].rearrange("h p t -> p h t"), ob[:].rearrange("p (h t) -> p h t", t=128), reads=[ob])
                else:
                    S.dma("pool", OT[8 + h, :, tok0:tok0 + N], ob[:, 0:N], reads=[ob])
            S.barrier()
        if cfg.dbg:
            S.dma("sp", dbg["d_OT"][:, :, :], OT[:, :, :])
            S.barrier()

        def out_phase(layer, tiles, OTd, Wout, resid, dst):
            with ExitStack() as ps:
                ph = Ph()
                ph.gtmp = P.sb(ps, "gtmp", [128, D], F32)
                P.mod_vecs(ps, ph, layer, 0, [("Gl", "G", 2, "g_mix_post")])
                if layer == 0:
                    P.mod_vecs(ps, ph, layer, 1, [("Gc", "G", 2, "g_mix_post")])
                Wsb = P.sb(ps, "Wo", [128, KC, D], BF16)
                S.dma("pool", Wsb[:], Wout[:, :].rearrange("(kc p) n -> p kc n", p=128), writes=[Wsb])
                oT = [P.sb(ps, "oT%d" % i, [128, 16, 128], BF16) for i in range(2)]
                xr = [P.sb(ps, "xr%d" % i, [128, D], F32) for i in range(2)]
                of = P.sb(ps, "of", [128, D], F32)
                junk = P.sb(ps, "junk", [128, D], BF16)
                ss = P.sb(ps, "ss", [128, 16], F32)
                pp = [P.pb(ps, "po%d" % i, [128, 512], F32) for i in range(8)]
                for i, t in enumerate(tiles):
                    is_ctx = (layer == 0) and t >= NQT - 2
                    o_ = oT[i % 2]
                    x_ = xr[i % 2]
                    tsl = slice(t * 128, (t + 1) * 128)
                    S.dma("sp", o_[:], OTd[:, :, tsl].rearrange("h p t -> p h t"), writes=[o_])
                    S.dma("sp", x_[:], resid[tsl, :], writes=[x_])
                    for cg in range(4):
                        pb_ = pp[(i % 2) * 4 + cg]
                        _lin(S, pb_, pb_[:, :], o_, Wsb, cg * 512, (cg + 1) * 512)
                        if cg % 2 == 0:
                            S.op("act", lambda e, pb_=pb_, cg=cg: e.copy(of[:, cg * 512:(cg + 1) * 512], pb_[:, :]), reads=[pb_], writes=[of])
                        else:
                            S.op("dve", lambda e, pb_=pb_, cg=cg: e.tensor_copy(of[:, cg * 512:(cg + 1) * 512], pb_[:, :]), reads=[pb_], writes=[of])
                    S.op("act", lambda e: e.activation(out=junk[:], in_=of[:], func=AF.Square, accum_out=ss[:, 0:1]), reads=[of], writes=[junk, ss])
                    P.rstd_from_ss(ss, slice(0, 1), D)
                    G = ph.Gc if is_ctx else ph.Gl
                    S.op("dve", lambda e, G=G: e.scalar_tensor_tensor(of[:], of[:], ss[:, 0:1], G[:], ALU.mult, ALU.mult), reads=[of, ss, G], writes=[of])
                    S.op("dve", lambda e, x_=x_: e.tensor_tensor(x_[:], of[:], x_[:], ALU.add), reads=[of, x_], writes=[x_])
                    S.dma("pool", dst[tsl, :], x_[:], reads=[x_])
                S.barrier()

        def ffn_phase(layer, tiles, src, dst_fn):
            JB = 2
            NB = FC // JB
            with ExitStack() as ps:
                ph = Ph()
                P.front_alloc(ps, ph)
                hx = [P.sb(ps, "hx%d" % i, [128, D], F32) for i in range(2)]
                ph.gtmp = hx[0]
                MV = [("A", "A", 4, "g_ffn_pre"), ("Sh", "S", 3, None), ("G", "G", 5, "g_ffn_post")]
                P.mod_vecs(ps, ph, layer, 0, MV)

                def switch_ctx(attrs):
                    for attr, kind, k, gname in MV:
                        if attr not in attrs:
                            continue
                        t_ = getattr(ph, attr)
                        P.load_bcast("sp", t_, P.modv[layer, 1, k * D:(k + 1) * D])
                        if kind != "S":
                            g_ = ph.gtmp
                            P.load_bcast("sp", g_, P.din[gname][layer, :])
                            if kind == "A":
                                S.op("dve", lambda e, t_=t_, g_=g_: e.scalar_tensor_tensor(t_[:], t_[:], 1.0, g_[:], ALU.add, ALU.mult), reads=[t_, g_], writes=[t_])
                            else:
                                S.op("dve", lambda e, t_=t_, g_=g_: e.tensor_tensor(t_[:], t_[:], g_[:], ALU.mult), reads=[t_, g_], writes=[t_])

                facc = [P.sb(ps, "facc%d" % i, [128, D], F32) for i in range(4)]
                u2T = [P.sb(ps, "u2T%d" % i, [128, KC, 512], BF16) for i in range(2)]
                wg = [P.sb(ps, "wg%d" % i, [128, KC, JB * 128], BF16) for i in range(2)]
                wu = [P.sb(ps, "wu%d" % i, [128, KC, JB * 128], BF16) for i in range(2)]
                wd = [P.sb(ps, "wd%d" % i, [128, JB, D], BF16) for i in range(2)]
                aT = [P.sb(ps, "aT%d" % i, [128, JB, 512], BF16) for i in range(2)]
                sg = [P.sb(ps, "sg%d" % i, [128, 512], F32) for i in range(2)]
                pg = [P.pb(ps, "pg%d" % i, [128, 512], F32) for i in range(2)]
                pu = [P.pb(ps, "pu%d" % i, [128, 512], F32) for i in range(2)]
                pd = [P.pb(ps, "pd%d" % i, [128, 512], F32) for i in range(2)]
                groups = []
                i = 0
                while i < len(tiles):
                    ctxf = (layer == 0) and tiles[i] >= NQT - 2
                    j = i
                    while j < len(tiles) and j - i < 4 and (((layer == 0) and tiles[j] >= NQT - 2) == ctxf):
                        j += 1
                    groups.append((tiles[i:j], ctxf))
                    i = j
                state = {"pro_ctx": False, "epi_ctx": False, "hxi": 0}

                def prologue(gidx):
                    gt, ctxf = groups[gidx]
                    if ctxf and not state["pro_ctx"]:
                        state["pro_ctx"] = True
                        switch_ctx(("A", "Sh"))
                    for ti, t in enumerate(gt):
                        x_ = hx[state["hxi"] % 2]
                        state["hxi"] += 1
                        S.dma("sp", x_[:], src[t * 128:(t + 1) * 128, :], writes=[x_])
                        P.front(ph, x_, ph.A, ph.Sh)
                        S.op("dve", lambda e, ti=ti: e.tensor_copy(u2T[gidx % 2][:, :, ti * 128:(ti + 1) * 128], ph.uT[:]), reads=[ph.uT], writes=[u2T[gidx % 2]])

                blk = 0
                gstep = 0
                dstep = 0
                prologue(0)
                for gidx, (gt, ctxf) in enumerate(groups):
                    G = len(gt) * 128
                    uT_ = u2T[gidx % 2]
                    bpar = {}

                    def gu(bi):
                        nonlocal blk, gstep, dstep
                        b = blk % 2
                        blk += 1
                        bpar[bi] = b
                        c0 = bi * JB * 128
                        S.dma("pool", wg[b][:], w_gate[layer, :, c0:c0 + JB * 128].rearrange("(kc p) n -> p kc n", p=128), writes=[wg[b]])
                        S.dma("pool", wu[b][:], w_up[layer, :, c0:c0 + JB * 128].rearrange("(kc p) n -> p kc n", p=128), writes=[wu[b]])
                        S.dma("pool", wd[b][:], w_down[layer, c0:c0 + JB * 128, :].rearrange("(j p) n -> p j n", p=128), writes=[wd[b]])
                        for jj in range(JB):
                            pgt = pg[gstep % 2]
                            sgt = sg[gstep % 2]
                            put = pu[gstep % 2]
                            gstep += 1
                            for kc in range(KC):
                                S.op("pe", lambda e, kc=kc: e.matmul(pgt[:, 0:G], wg[b][:, kc, jj * 128:(jj + 1) * 128], uT_[:, kc, 0:G], start=(kc == 0), stop=(kc == KC - 1)), reads=[wg[b], uT_], writes=[pgt])
                                if kc % 4 == 3:
                                    yield
                            S.op("act", lambda e: e.activation(out=sgt[:, 0:G], in_=pgt[:, 0:G], func=AF.Silu), reads=[pgt], writes=[sgt])
                            for kc in range(KC):
                                S.op("pe", lambda e, kc=kc: e.matmul(put[:, 0:G], wu[b][:, kc, jj * 128:(jj + 1) * 128], uT_[:, kc, 0:G], start=(kc == 0), stop=(kc == KC - 1)), reads=[wu[b], uT_], writes=[put])
                                if kc % 4 == 3 and kc != KC - 1:
                                    yield
                            S.op("dve", lambda e: e.tensor_tensor(aT[b][:, jj, 0:G], put[:, 0:G], sgt[:, 0:G], ALU.mult), reads=[put, sgt], writes=[aT[b]])
                            yield

                    def down(bi):
                        nonlocal dstep
                        b = bpar[bi]
                        for ti in range(len(gt)):
                            for cg in range(4):
                                pdt = pd[dstep % 2]
                                dstep += 1
                                for jj in range(JB):
                                    S.op("pe", lambda e, jj=jj: e.matmul(pdt[:, :], aT[b][:, jj, ti * 128:(ti + 1) * 128], wd[b][:, jj, cg * 512:(cg + 1) * 512], start=(jj == 0), stop=(jj == JB - 1)), reads=[aT[b], wd[b]], writes=[pdt])
                                fa = facc[ti]
                                if bi == 0:
                                    S.op("act", lambda e: e.copy(fa[:, cg * 512:(cg + 1) * 512], pdt[:, :]), reads=[pdt], writes=[fa])
                                else:
                                    S.op("dve", lambda e: e.tensor_tensor(fa[:, cg * 512:(cg + 1) * 512], pdt[:, :], fa[:, cg * 512:(cg + 1) * 512], ALU.add), reads=[pdt, fa], writes=[fa])
                                yield

                    run_rr([gu(0)])
                    for bi in range(NB):
                        gens = []
                        if bi + 1 < NB:
                            gens.append(gu(bi + 1))
                        gens.append(down(bi))
                        run_rr(gens)
                        if bi == 3 and gidx + 1 < len(groups):
                            prologue(gidx + 1)
                    if ctxf and not state["epi_ctx"]:
                        state["epi_ctx"] = True
                        switch_ctx(("G",))
                    for ti, t in enumerate(gt):
                        fa = facc[ti]
                        x_ = hx[state["hxi"] % 2]
                        state["hxi"] += 1
                        S.dma("sp", x_[:], src[t * 128:(t + 1) * 128, :], writes=[x_])
                        S.op("act", lambda e: e.activation(out=ph.junk[:], in_=fa[:], func=AF.Square, accum_out=ph.ss[:, 4:5]), reads=[fa], writes=[ph.junk, ph.ss])
                        P.rstd_from_ss(ph.ss, slice(4, 5), D)
                        S.op("dve", lambda e: e.scalar_tensor_tensor(fa[:], fa[:], ph.ss[:, 4:5], ph.G[:], ALU.mult, ALU.mult), reads=[fa, ph.ss, ph.G], writes=[fa])
                        S.op("dve", lambda e: e.tensor_tensor(x_[:], fa[:], x_[:], ALU.add), reads=[fa, x_], writes=[x_])
                        S.dma("sp", dst_fn(t), x_[:], reads=[x_])
                S.barrier()


        out_phase(0, list(range(NQT)), OT, ab_w_out, xq, hmid)
        if cfg.dbg:
            S.dma("sp", dbg["d_hmid"][:, :], hmid[:, :])
            S.barrier()
        ffn_phase(0, list(range(NQT)), hmid, lambda t: h1[t * 128:(t + 1) * 128, :])
        if cfg.dbg:
            S.dma("sp", dbg["d_h1"][:, :], h1[:, :])
            S.barrier()

        qkv_phase(1, NQT, h1, ropeq, NQT - 2, cd_w_in, [
            dict(kind="heads", c0=0, n=1024, gain="c_q_norm", rope=True, dst=QCT, own_only=True),
            dict(kind="heads", c0=1024, n=256, gain="c_k_norm", rope=True, dst=KCT),
            dict(kind="v", c0=1280, n=256, dst=VC),
        ], "g_mix_pre")
        qkv_phase(1, NQT, h1, ropeq, NQT - 2, cd_w_in, [
            dict(kind="heads", c0=1536, n=1024, gain="d_q_norm", rope=False, dst=QDT, own_only=True),
            dict(kind="heads", c0=2560, n=1024, gain="d_k_norm", rope=False, dst=KDT),
        ], "g_mix_pre")
        qkv_phase(1, NQT, h1, ropeq, NQT - 2, cd_w_in, [
            dict(kind="v", c0=3584, n=1024, dst=VD),
        ], "g_mix_pre")

        with ExitStack() as ps:
            ones = P.sb(ps, "ones", [128, 128], BF16)
            S.dma("pool", ones[:], ones_in[:, :], writes=[ones])
            KCs = P.sb(ps, "KCs", [128, 2, NQ], BF16)
            VCs = P.sb(ps, "VCs", [128, 2, NQT, 128], BF16)
            KDs = P.sb(ps, "KDs", [128, 8, NQ], BF16)
            VDs = P.sb(ps, "VDs", [128, 8, NQT, 128], BF16)
            S.dma("sp", KCs[:], KCT[:, :, :].rearrange("h p t -> p h t"), writes=[KCs])
            S.dma("sp", KDs[:], KDT[:, :, :].rearrange("h p t -> p h t"), writes=[KDs])
            for g in range(2):
                S.dma("sp", VCs[:, g, :, :], VC[g, :, :].rearrange("(t p) d -> p t d", p=128), writes=[VCs])
            for h in range(8):
                for t0_ in range(0, NQT, 11):
                    t1_ = min(NQT, t0_ + 11)
                    S.dma("sp", VDs[:, h, t0_:t1_, :], VD[h, t0_ * 128:t1_ * 128, :].rearrange("(t p) d -> p t d", p=128), writes=[VDs])
            esink = P.sb(ps, "esink", [128, 8], F32)
            P.load_bcast("sp", esink, c_sink[:])
            S.op("act", lambda e: e.activation(out=esink[:], in_=esink[:], func=AF.Exp), reads=[esink], writes=[esink])
            mC = P.sb(ps, "mC", [128, 3, 3, 128], F32)
            S.dma("sp", mC[:].rearrange("p v s q -> p (v s) q"), maskC[:, :, :, :].rearrange("v s k q -> k (v s) q"), writes=[mC])
            bD = [P.sb(ps, "bD%d" % i, [128, 6, 128], F32) for i in range(3)]
            qsb = [P.sb(ps, "qsb%d" % i, [128, 512], BF16) for i in range(3)]
            pt = [P.sb(ps, "pt%d" % i, [128, 512], BF16) for i in range(4)]
            sb_ = [P.sb(ps, "sbias%d" % i, [128, 512], F32) for i in range(4)]
            rden = [P.sb(ps, "rden%d" % i, [128, 512], F32) for i in range(2)]
            osb = [P.sb(ps, "osb%d" % i, [128, 512], BF16) for i in range(2)]
            pS = [P.pb(ps, "pS%d" % i, [128, 512], F32) for i in range(4)]
            pO = [P.pb(ps, "pO%d" % i, [128, 512], F32) for i in range(2)]
            pD = [P.pb(ps, "pD%d" % i, [128, 512], F32) for i in range(2)]
            scale = 128 ** -0.5
            glist = []
            for i in range(NOWN):
                glist += [("C", i, g) for g in range(2)]
                glist += [("D", i, h) for h in range(8)]

            def ginfo(gi):
                kind, i, hg = glist[gi]
                e_ = OWN0 + i
                if kind == "C":
                    keys = [(NQT - 2, None), (NQT - 1, None), (e_ - 1, 0), (e_, 1), (e_ + 1, 2)]
                    return kind, i, hg, e_, keys, 512
                lo = -3 if i == NOWN - 1 else -2
                hi = 3 if i == 0 else 2
                rel = list(range(lo, hi + 1))
                keys = [(NQT - 2, None), (NQT - 1, None)] + [(e_ + r_, si) for si, r_ in enumerate(rel)]
                return kind, i, hg, e_, keys, 128

            def load_q(gi):
                kind, i, hg, e_, keys, N = ginfo(gi)
                tsl = slice(e_ * 128, (e_ + 1) * 128)
                qs = qsb[gi % 3]
                if kind == "C":
                    S.dma("sp", qs[:].rearrange("p (h t) -> p h t", t=128), QCT[hg * 4:hg * 4 + 4, :, tsl].rearrange("h p t -> p h t"), writes=[qs])
                else:
                    vD = 0 if i == 0 else (1 if i == 1 else (3 if i == NOWN - 2 else (4 if i == NOWN - 1 else 2)))
                    bt = bD[gi % 3]
                    nsl = len(keys) - 2
                    S.dma("sp", qs[:, 0:128], QDT[hg, :, tsl], writes=[qs])
                    S.dma("sp", bt[:, 0:nsl, :], biasD[vD, hg, 0:nsl, :, :].rearrange("s k q -> k s q"), writes=[bt])

            load_q(0)
            load_q(1)
            step = 0
            for gi in range(len(glist)):
                kind, i, hg, e_, keys, N = ginfo(gi)
                if gi + 2 < len(glist):
                    load_q(gi + 2)
                tsl = slice(e_ * 128, (e_ + 1) * 128)
                gg = gi % 2
                qs = qsb[gi % 3]
                bt = bD[gi % 3]
                vC = 0 if i == 0 else (2 if i == NOWN - 1 else 1)
                nk = len(keys)
                base = step
                step += nk

                def qk(j):
                    kt, slot = keys[j]
                    sp_ = pS[(base + j) % 4]
                    ksl = slice(kt * 128, (kt + 1) * 128)
                    if kind == "C":
                        S.op("pe", lambda e: e.matmul(sp_[:, 0:N], KCs[:, hg, ksl], qs[:, 0:N], start=True, stop=True), reads=[KCs, qs], writes=[sp_])
                    else:
                        S.op("pe", lambda e: e.matmul(sp_[:, 0:N], KDs[:, hg, ksl], qs[:, 0:N], start=True, stop=True), reads=[KDs, qs], writes=[sp_])

                def ex(j):
                    kt, slot = keys[j]
                    sp_ = pS[(base + j) % 4]
                    ptt = pt[(base + j) % 4]
                    if slot is None:
                        S.op("act", lambda e: e.activation(out=ptt[:, 0:N], in_=sp_[:, 0:N], func=AF.Exp, scale=scale), reads=[sp_], writes=[ptt])
                        return
                    sbt = sb_[(base + j) % 4]
                    if kind == "C":
                        S.op("dve", lambda e: e.scalar_tensor_tensor(sbt[:].rearrange("p (h t) -> p h t", t=128), sp_[:, :].rearrange("p (h t) -> p h t", t=128), scale,
                                                                      mC[:, vC, slot, :].unsqueeze(1).to_broadcast([128, 4, 128]), ALU.mult, ALU.add), reads=[sp_, mC], writes=[sbt])
                    else:
                        S.op("dve", lambda e: e.scalar_tensor_tensor(sbt[:, 0:128], sp_[:, 0:128], scale, bt[:, slot, :], ALU.mult, ALU.add), reads=[sp_, bt], writes=[sbt])
                    S.op("act", lambda e: e.activation(out=ptt[:, 0:N], in_=sbt[:, 0:N], func=AF.Exp), reads=[sbt], writes=[ptt])

                def pv(j):
                    kt, slot = keys[j]
                    ptt = pt[(base + j) % 4]
                    vv = VCs[:, hg, kt, :] if kind == "C" else VDs[:, hg, kt, :]
                    S.op("pe", lambda e: e.matmul(pO[gg][:, 0:N], vv, ptt[:, 0:N], start=(j == 0), stop=(j == nk - 1)), reads=[VCs if kind == "C" else VDs, ptt], writes=[pO[gg]])
                    S.op("pe", lambda e: e.matmul(pD[gg][:, 0:N], ones[:], ptt[:, 0:N], start=(j == 0), stop=(j == nk - 1)), reads=[ones, ptt], writes=[pD[gg]])

                qk(0)
                if nk > 1:
                    qk(1)
                for j in range(nk):
                    if j + 2 < nk:
                        qk(j + 2)
                    ex(j)
                    pv(j)
                rd = rden[gg]
                ob = osb[gg]
                if kind == "C":
                    S.op("dve", lambda e: e.tensor_tensor(rd[:].rearrange("p (h t) -> p h t", t=128), pD[gg][:, :].rearrange("p (h t) -> p h t", t=128),
                                                          esink[:, hg * 4:hg * 4 + 4].unsqueeze(2).to_broadcast([128, 4, 128]), ALU.add), reads=[pD[gg], esink], writes=[rd])
                    S.op("dve", lambda e: e.reciprocal(rd[:], rd[:]), reads=[rd], writes=[rd])
                    S.op("dve", lambda e: e.tensor_tensor(ob[:], pO[gg][:, :], rd[:], ALU.mult), reads=[pO[gg], rd], writes=[ob])
                    S.dma("pool", OT1[hg * 4:hg * 4 + 4, :, tsl].rearrange("h p t -> p h t"), ob[:].rearrange("p (h t) -> p h t", t=128), reads=[ob])
                else:
                    S.op("dve", lambda e: e.reciprocal(rd[:, 0:128], pD[gg][:, 0:128]), reads=[pD[gg]], writes=[rd])
                    S.op("dve", lambda e: e.tensor_tensor(ob[:, 0:128], pO[gg][:, 0:128], rd[:, 0:128], ALU.mult), reads=[pO[gg], rd], writes=[ob])
                    S.dma("pool", OT1[8 + hg, :, tsl], ob[:, 0:128], reads=[ob])
            S.barrier()
        if cfg.dbg:
            S.dma("sp", dbg["d_OT1"][:, :, :], OT1[:, :, :])
            S.barrier()


        own = list(range(OWN0, OWN0 + NOWN))
        out_phase(1, own, OT1, cd_w_out, h1, hmid1)
        if cfg.dbg:
            S.dma("sp", dbg["d_hmid1"][:, :], hmid1[:, :])
            S.barrier()
        ffn_phase(1, own, hmid1, lambda t: yout[(t - OWN0) * 128:(t - OWN0 + 1) * 128, :])
        S.barrier()
        P.nins = S.nins
    return P


GRID_W = 64
L_SEQ = 8192


def _rope_tables(tok, dim):
    tok = np.asarray(tok)
    valid = tok >= 0
    t = np.where(valid, tok, 0)
    row = (t // GRID_W).astype(np.float32)
    col = (t % GRID_W).astype(np.float32)
    half = dim // 2
    inv = (np.float32(10000.0) ** (-np.arange(0, half, 2, dtype=np.float32) / np.float32(half))).astype(np.float32)
    ar = row[:, None] * inv[None, :]
    ac = col[:, None] * inv[None, :]
    ang = np.concatenate([ar, ar, ac, ac], axis=-1).astype(np.float32)
    cos = np.cos(ang).astype(np.float32)
    sin = np.sin(ang).astype(np.float32)
    q = dim // 4
    sgn = np.concatenate([-np.ones(q), np.ones(q), -np.ones(q), np.ones(q)]).astype(np.float32)
    sin = sin * sgn[None, :]
    cos[~valid] = 1.0
    sin[~valid] = 0.0
    return cos, sin


def _rope_pack(tok):
    n = len(tok)
    out = np.zeros((4, n, 128), np.float32)
    c, s = _rope_tables(tok, 128)
    out[0], out[1] = c, s
    c, s = _rope_tables(tok, 64)
    out[2, :, :64], out[3, :, :64] = c, s
    return out


def _mask_c(cc, nown=16):
    kk = np.arange(128)[:, None]
    qq = np.arange(128)[None, :]
    m = np.zeros((3, 3, 128, 128), np.float32)
    for v in range(3):
        m[v, 0] = np.where(kk >= qq, 0.0, NEG)
        m[v, 2] = np.where(kk <= qq, 0.0, NEG)
    if cc == 0:
        m[0, 0] = NEG
    if cc == 3:
        m[2, 2] = NEG
    return m


def _bias_d(cc, rpb, nown=16):
    out = np.full((5, 8, 6, 128, 128), NEG, np.float32)
    rows = L_SEQ // GRID_W
    var_tiles = [0, 1, 2, nown - 2, nown - 1]
    for v, i in enumerate(var_tiles):
        lo = -3 if i == nown - 1 else -2
        hi = 3 if i == 0 else 2
        qtok = 2048 * cc + 128 * i + np.arange(128)
        qr = qtok // GRID_W
        qc = qtok % GRID_W
        rs = np.clip(qr - 4, 0, rows - 8)
        cs = np.clip(qc - 8, 0, GRID_W - 16)
        for si, r_ in enumerate(range(lo, hi + 1)):
            ktok = 2048 * cc + 128 * (i + r_) + np.arange(128)
            inb = (ktok >= 0) & (ktok < L_SEQ)
            kr = np.where(inb, ktok, 0) // GRID_W
            kc = np.where(inb, ktok, 0) % GRID_W
            ok = inb[:, None] & (kr[:, None] >= rs[None, :]) & (kr[:, None] < rs[None, :] + 8) & (kc[:, None] >= cs[None, :]) & (kc[:, None] < cs[None, :] + 16)
            di = np.clip(kr[:, None] - qr[None, :] + 7, 0, 14)
            dj = np.clip(kc[:, None] - qc[None, :] + 15, 0, 30)
            g = rpb[:, di, dj]
            out[v, :, si] = np.where(ok[None], g, NEG)
    return out


def _host_inputs(inputs, cfg=None):
    f = lambda a: np.ascontiguousarray(np.asarray(a, dtype=np.float32))
    x = f(inputs["x"])
    ctx = f(inputs["ctx"])
    c = f(inputs["c"])
    c_ctx = f(inputs["c_ctx"])
    shared = {
        "ident": np.eye(128, dtype=np.float32), "ones": np.ones((128, 128), np.float32),
        "w_mod": f(inputs["w_mod"]), "b_mod": f(inputs["b_mod"]),
        "g_mix_pre": f(inputs["g_mix_pre"]), "g_mix_post": f(inputs["g_mix_post"]),
        "g_ffn_pre": f(inputs["g_ffn_pre"]), "g_ffn_post": f(inputs["g_ffn_post"]),
        "w_gate": f(inputs["w_gate"]), "w_up": f(inputs["w_up"]), "w_down": f(inputs["w_down"]),
        "ab_w_in": f(inputs["ab_w_in"][0]), "ab_w_out": f(inputs["ab_w_out"][0]),
        "a_q_norm": f(inputs["a_q_norm"][0]), "a_k_norm": f(inputs["a_k_norm"][0]),
        "b_q_norm": f(inputs["b_q_norm"][0]), "b_kv_norm": f(inputs["b_kv_norm"][0]),
        "cd_w_in": f(inputs["cd_w_in"][0]), "cd_w_out": f(inputs["cd_w_out"][0]),
        "c_q_norm": f(inputs["c_q_norm"][0]), "c_k_norm": f(inputs["c_k_norm"][0]), "c_sink": f(inputs["c_sink"][0]),
        "d_q_norm": f(inputs["d_q_norm"][0]), "d_k_norm": f(inputs["d_k_norm"][0]),
    }
    wuq = f(inputs["b_w_uq"][0]).reshape(512, 8, 192)
    shared["w_uq"] = np.ascontiguousarray(np.concatenate([wuq[:, :, :128].reshape(512, 1024), wuq[:, :, 128:].reshape(512, 512)], axis=1))
    wukv = f(inputs["b_w_ukv"][0]).reshape(512, 8, 256)
    shared["w_ukv"] = np.ascontiguousarray(np.concatenate([wukv[:, :, :128].reshape(512, 1024), wukv[:, :, 128:].reshape(512, 1024)], axis=1))
    rpb = f(inputs["d_rpb"][0])
    maps = []
    ktok = np.concatenate([np.arange(L_SEQ), -np.ones(256, np.int64)])
    ropek = _rope_pack(ktok)
    for core in range(8):
        b, cc = core // 4, core % 4
        m = dict(shared)
        tok = np.arange(2048 * cc - 256, 2048 * cc + 2048 + 256)
        inb = (tok >= 0) & (tok < L_SEQ)
        tokc = np.clip(tok, 0, L_SEQ - 1)
        m["xq"] = np.ascontiguousarray(np.concatenate([x[b][tokc], ctx[b]], axis=0))
        m["xkv"] = np.ascontiguousarray(np.concatenate([x[b], ctx[b]], axis=0))
        m["cT"] = np.ascontiguousarray(np.stack([c[b].reshape(KC, 128).T, c_ctx.reshape(KC, 128).T], axis=-1))
        m["ropeq"] = _rope_pack(np.concatenate([np.where(inb, tok, -1), -np.ones(256, np.int64)]))
        m["ropek"] = ropek
        m["maskC"] = _mask_c(cc)
        m["biasD"] = _bias_d(cc, rpb)
        maps.append(m)
    return maps


_CACHE = {}


def kernel(**inputs):
    cfg = Cfg()
    if "prog" not in _CACHE:
        _CACHE["prog"] = build(cfg)
    P = _CACHE["prog"]
    maps = _host_inputs(inputs)
    res = run_bass_kernel_spmd(P.nc, maps, core_ids=list(range(8)))
    out = np.zeros((2, L_SEQ, D), np.float32)
    for core in range(8):
        b, cc = core // 4, core % 4
        out[b, 2048 * cc:2048 * cc + 2048] = np.asarray(res.results[core]["y"], dtype=np.float32)
    return out
```

```python
import numpy as np
from contextlib import ExitStack
import concourse.bass as bass
import concourse.mybir as mybir
from concourse.bass_utils import run_bass_kernel_spmd

F32 = mybir.dt.float32
BF16 = mybir.dt.bfloat16
AF = mybir.ActivationFunctionType
ALU = mybir.AluOpType
AX = mybir.AxisListType
ND = 12

D = 2048
KC = 16
DFF = 5632
FC = 44
EPS = 1e-6
NEG = -1e30


class Res:
    __slots__ = ("w", "rs", "name")

    def __init__(self, name=""):
        self.w = None
        self.rs = {}
        self.name = name


class T:
    def __init__(self, t, name):
        self.t = t
        self.r = Res(name)

    def __getitem__(self, k):
        return self.t[k]


def _res(x):
    return x.r if isinstance(x, T) else x


class Sch:
    def __init__(self, nc, es):
        self.nc = nc
        self.eng = {"pe": nc.tensor, "act": nc.scalar, "dve": nc.vector, "pool": nc.gpsimd, "sp": nc.sync}
        self.semobj = {}
        self.cnt = {}
        for k in ["pe", "act", "dve", "pool"]:
            self.semobj[k] = es.enter_context(nc.semaphore("s_" + k))
            self.cnt[k] = 0
        self.known = {k: {} for k in self.eng}
        self.dcnt = {}
        self.dnext = {}
        for q in ["sp", "pool"]:
            for i in range(ND):
                self.semobj[(q, i)] = es.enter_context(nc.semaphore("d_%s_%d" % (q, i)))
                self.dcnt[(q, i)] = 0
            self.dnext[q] = 0
        self.nins = 0

    def _wait(self, e, deps):
        need = {}
        kn = self.known[e]
        for d in deps:
            if d is None:
                continue
            k, v = d
            if kn.get(k, 0) >= v:
                continue
            if need.get(k, 0) < v:
                need[k] = v
        for k, v in need.items():
            self.eng[e].wait_ge(self.semobj[k], v)
            kn[k] = v
            self.nins += 1

    @staticmethod
    def _deps(reads, writes):
        deps = []
        for r in reads:
            deps.append(_res(r).w)
        for w in writes:
            w = _res(w)
            deps.append(w.w)
            deps.extend(w.rs.items())
        return deps

    @staticmethod
    def _mark(ev, reads, writes):
        k, v = ev
        for r in reads:
            r = _res(r)
            if r.rs.get(k, 0) < v:
                r.rs[k] = v
        for w in writes:
            w = _res(w)
            w.w = ev
            w.rs = {}

    def op(self, e, fn, reads=(), writes=()):
        deps = self._deps(reads, writes)
        if e == "pe":
            deps = [d for d in deps if d is not None and d[0] != "pe"]
        self._wait(e, deps)
        ins = fn(self.eng[e])
        self.cnt[e] += 1
        ins.then_inc(self.semobj[e], 1)
        self.nins += 1
        ev = (e, self.cnt[e])
        self._mark(ev, reads, writes)
        return ev

    def dma(self, q, out, in_, reads=(), writes=()):
        i = self.dnext[q]
        self.dnext[q] = (i + 1) % ND
        key = (q, i)
        deps = self._deps(reads, writes)
        if self.dcnt[key] > 0:
            deps.append((key, self.dcnt[key]))
        self._wait(q, deps)
        ins = self.eng[q].dma_start(out=out, in_=in_)
        self.dcnt[key] += 16
        ins.then_inc(self.semobj[key], 16)
        self.nins += 1
        ev = (key, self.dcnt[key])
        self._mark(ev, reads, writes)
        return ev

    def barrier(self):
        deps = [(k, v) for k, v in self.cnt.items() if v > 0]
        deps += [(k, v) for k, v in self.dcnt.items() if v > 0]
        for e in ("pe", "act", "dve", "pool", "sp"):
            self._wait(e, deps)


class Cfg:
    def __init__(self, nqt=22, nkt=66, own0=2, nown=16, dbg=False):
        self.NQT = nqt
        self.NKT = nkt
        self.OWN0 = own0
        self.NOWN = nown
        self.dbg = dbg


class Prog:
    def __init__(self, cfg):
        self.cfg = cfg
        self.nc = bass.Bass("TRN2", target_bir_lowering=False)
        self.es = ExitStack()
        self.S = None
        self.din = {}
        self.uid = 0

    def inp(self, name, shape, dt=F32):
        self.din[name] = self.nc.dram_tensor(name, list(shape), dt, kind="ExternalInput").ap()
        return self.din[name]

    def scratch(self, name, shape, dt):
        return self.nc.dram_tensor(name, list(shape), dt, kind="Internal").ap()

    def sb(self, ps, name, shape, dt):
        self.uid += 1
        return T(ps.enter_context(self.nc.sbuf_tensor("%s_%d" % (name, self.uid), list(shape), dt)), name)

    def pb(self, ps, name, shape, dt):
        self.uid += 1
        return T(ps.enter_context(self.nc.psum_tensor("%s_%d" % (name, self.uid), list(shape), dt)), name)

    def rstd_from_ss(self, ss, col, n):
        S = self.S
        a = ss[:, col]
        S.op("act", lambda e: e.activation(out=a, in_=a, func=AF.Sqrt, bias=EPS, scale=1.0 / n), reads=[ss], writes=[ss])
        S.op("dve", lambda e: e.reciprocal(a, a), reads=[ss], writes=[ss])

    def load_bcast(self, q, dst, src_ap):
        self.S.dma(q, dst[:], src_ap.partition_broadcast(128), writes=[dst])

    def front(self, ph, xt, Avec, shvec):
        S = self.S
        ss, junk, ub, uT, pT, ident = ph.ss, ph.junk, ph.ub, ph.uT, ph.pT, ph.ident
        S.op("act", lambda e: e.activation(out=junk[:], in_=xt[:], func=AF.Square, accum_out=ss[:, 0:1]), reads=[xt], writes=[junk, ss])
        self.rstd_from_ss(ss, slice(0, 1), D)
        S.op("dve", lambda e: e.scalar_tensor_tensor(xt[:], xt[:], ss[:, 0:1], Avec[:], ALU.mult, ALU.mult), reads=[xt, ss, Avec], writes=[xt])
        S.op("dve", lambda e: e.tensor_tensor(ub[:], xt[:], shvec[:], ALU.add), reads=[xt, shvec], writes=[ub])
        for half in range(2):
            p = pT[half]
            for j in range(8):
                kc = half * 8 + j
                S.op("pe", lambda e, p=p, j=j, kc=kc: e.transpose(p[:, j * 128:(j + 1) * 128], ub[:, kc * 128:(kc + 1) * 128], ident[:]), reads=[ub, ident], writes=[p])
            eng = "act" if half == 0 else "dve"
            dst = uT[:, half * 8:(half + 1) * 8, :].rearrange("p a b -> p (a b)")
            if eng == "act":
                S.op("act", lambda e, p=p, dst=dst: e.copy(dst, p[:]), reads=[p], writes=[uT])
            else:
                S.op("dve", lambda e, p=p, dst=dst: e.tensor_copy(dst, p[:]), reads=[p], writes=[uT])

    def front_alloc(self, ps, ph, share=None):
        ph.ss = self.sb(ps, "ss", [128, 16], F32)
        ph.junk = self.sb(ps, "junk", [128, D], BF16)
        ph.ub = self.sb(ps, "ub", [128, D], BF16)
        ph.uT = self.sb(ps, "uT", [128, KC, 128], BF16)
        if share is not None:
            ph.pT = share.pT
            ph.ident = share.ident
            return
        ph.pT = [self.pb(ps, "pT%d" % i, [128, 1024], BF16) for i in range(2)]
        ph.ident = self.sb(ps, "ident", [128, 128], BF16)
        self.S.dma("pool", ph.ident[:], self.din["ident"][:, :], writes=[ph.ident])

    def mod_vecs(self, ps, ph, layer, which, names):
        S = self.S
        for attr, kind, k, gname in names:
            t = self.sb(ps, attr, [128, D], F32)
            self.load_bcast("sp", t, self.modv[layer, which, k * D:(k + 1) * D])
            if kind != "S":
                g = ph.gtmp
                self.load_bcast("sp", g, self.din[gname][layer, :])
                if kind == "A":
                    S.op("dve", lambda e, t=t, g=g: e.scalar_tensor_tensor(t[:], t[:], 1.0, g[:], ALU.add, ALU.mult), reads=[t, g], writes=[t])
                else:
                    S.op("dve", lambda e, t=t, g=g: e.tensor_tensor(t[:], t[:], g[:], ALU.mult), reads=[t, g], writes=[t])
            setattr(ph, attr, t)

    def head_norm_rope(self, ph, src_ps, src_ap, H, hd, gain, cs, dstT, rope=True, norm=True):
        S = self.S
        qf = ph.qf
        n = H * hd
        v3 = lambda ap: ap.rearrange("p (h d) -> p h d", d=hd)
        S.op("act", lambda e: e.copy(qf[:, 0:n], src_ap), reads=[src_ps], writes=[qf])
        if norm:
            sq = ph.qsq
            S.op("dve", lambda e: e.tensor_tensor(sq[:, 0:n], qf[:, 0:n], qf[:, 0:n], ALU.mult), reads=[qf], writes=[sq])
            S.op("dve", lambda e: e.tensor_reduce(ph.ss[:, 2:2 + H], v3(sq[:, 0:n]), AX.X, ALU.add), reads=[sq], writes=[ph.ss])
            self.rstd_from_ss(ph.ss, slice(2, 2 + H), hd)
            S.op("dve", lambda e: e.tensor_tensor(v3(qf[:, 0:n]), v3(qf[:, 0:n]), ph.ss[:, 2:2 + H].unsqueeze(2).to_broadcast([128, H, hd]), ALU.mult), reads=[qf, ph.ss], writes=[qf])
            S.op("dve", lambda e: e.tensor_tensor(v3(qf[:, 0:n]), v3(qf[:, 0:n]), gain[:, 0:hd].unsqueeze(1).to_broadcast([128, H, hd]), ALU.mult), reads=[qf, gain], writes=[qf])
        if rope:
            csT, ci_, si_ = cs
            cosv = csT[:, ci_, :]
            sinv = csT[:, si_, :]
            q4 = hd // 4
            rot = ph.qsq
            x5 = lambda ap: ap.rearrange("p (h a b d) -> p h a b d", a=2, b=2, d=q4)
            s4 = lambda ap: ap.rearrange("p (a b d) -> p a b d", a=2, b=2, d=q4)
            for b0 in range(2):
                S.op("dve", lambda e, b0=b0: e.tensor_tensor(x5(rot[:, 0:n])[:, :, :, b0, :], x5(qf[:, 0:n])[:, :, :, 1 - b0, :],
                                                            s4(sinv[:, 0:hd])[:, :, b0, :].unsqueeze(1).to_broadcast([128, H, 2, q4]), ALU.mult), reads=[qf, csT], writes=[rot])
            S.op("dve", lambda e: e.tensor_tensor(v3(qf[:, 0:n]), v3(qf[:, 0:n]), cosv[:, 0:hd].unsqueeze(1).to_broadcast([128, H, hd]), ALU.mult), reads=[qf, csT], writes=[qf])
            S.op("dve", lambda e: e.tensor_tensor(dstT[:, 0:n], qf[:, 0:n], rot[:, 0:n], ALU.add), reads=[qf, rot], writes=[dstT])
        else:
            S.op("dve", lambda e: e.tensor_copy(dstT[:, 0:n], qf[:, 0:n]), reads=[qf], writes=[dstT])

    def transpose_out(self, ph, src, nblk, pTt, dst_sb, dram_ap, rows=128):
        S = self.S
        for j in range(nblk):
            S.op("pe", lambda e, j=j: e.transpose(pTt[:, j * 128:(j + 1) * 128], src[:, j * 128:(j + 1) * 128], ph.ident[:]), reads=[src, ph.ident], writes=[pTt])
        S.op("act", lambda e: e.copy(dst_sb[:, 0:nblk * 128], pTt[:, 0:nblk * 128]), reads=[pTt], writes=[dst_sb])
        S.dma("pool", dram_ap, dst_sb[0:rows, 0:nblk * 128].rearrange("p (h t) -> p h t", t=128), reads=[dst_sb])


def _lin(S, ps_tile, out_ap, uT, W, c0, c1, kcn=KC):
    for kc in range(kcn):
        S.op("pe", lambda e, kc=kc: e.matmul(out_ap, uT[:, kc, :], W[:, kc, c0:c1], start=(kc == 0), stop=(kc == kcn - 1)), reads=[uT, W], writes=[ps_tile])


class Ph:
    pass


def build(cfg):
    P = Prog(cfg)
    nc = P.nc
    NQT, NKT = cfg.NQT, cfg.NKT
    NQ = NQT * 128
    NK = NKT * 128
    NOWN = cfg.NOWN
    OWN0 = cfg.OWN0
    inp = P.inp
    xq = inp("xq", [NQ, D])
    xkv = inp("xkv", [NK, D])
    cT = inp("cT", [128, KC, 2])
    inp("ident", [128, 128])
    ones_in = inp("ones", [128, 128])
    w_mod = inp("w_mod", [2, D, 6 * D])
    b_mod = inp("b_mod", [2, 6 * D])
    for n in ["g_mix_pre", "g_mix_post", "g_ffn_pre", "g_ffn_post"]:
        inp(n, [2, D])
    w_gate = inp("w_gate", [2, D, DFF])
    w_up = inp("w_up", [2, D, DFF])
    w_down = inp("w_down", [2, DFF, D])
    ab_w_in = inp("ab_w_in", [D, 2624])
    ab_w_out = inp("ab_w_out", [D, D])
    a_q_norm = inp("a_q_norm", [128])
    a_k_norm = inp("a_k_norm", [128])
    b_q_norm = inp("b_q_norm", [512])
    b_kv_norm = inp("b_kv_norm", [512])
    w_uq = inp("w_uq", [512, 1536])
    w_ukv = inp("w_ukv", [512, 2048])
    cd_w_in = inp("cd_w_in", [D, 4608])
    cd_w_out = inp("cd_w_out", [D, D])
    c_q_norm = inp("c_q_norm", [128])
    c_k_norm = inp("c_k_norm", [128])
    c_sink = inp("c_sink", [8])
    d_q_norm = inp("d_q_norm", [128])
    d_k_norm = inp("d_k_norm", [128])
    ropeq = inp("ropeq", [4, NQ, 128])
    ropek = inp("ropek", [4, NK, 128])
    maskC = inp("maskC", [3, 3, 128, 128])
    biasD = inp("biasD", [5, 8, 6, 128, 128])
    yout = nc.dram_tensor("y", [NOWN * 128, D], F32, kind="ExternalOutput").ap()
    sc = P.scratch
    P.modv = sc("modv", [2, 2, 6 * D], F32)
    KAT = sc("KAT", [2, 128, NK], BF16)
    VA = sc("VA", [2, NK, 128], BF16)
    KBT = sc("KBT", [8, 128, NK], BF16)
    KRT = sc("KRT", [128, NK], BF16)
    VB = sc("VB", [8, NK, 128], BF16)
    QAT = sc("QAT", [8, 128, NQ], BF16)
    QBT = sc("QBT", [8, 128, NQ], BF16)
    QRT = sc("QRT", [4, 128, NQ], BF16)
    OT = sc("OT", [16, 128, NQ], BF16)
    hmid = sc("hmid", [NQ, D], F32)
    h1 = sc("h1", [NQ, D], F32)
    QCT = sc("QCT", [8, 128, NQ], BF16)
    KCT = sc("KCT", [2, 128, NQ], BF16)
    VC = sc("VC", [2, NQ, 128], BF16)
    QDT = sc("QDT", [8, 128, NQ], BF16)
    KDT = sc("KDT", [8, 128, NQ], BF16)
    VD = sc("VD", [8, NQ, 128], BF16)
    OT1 = sc("OT1", [16, 128, NQ], BF16)
    hmid1 = sc("hmid1", [NQ, D], F32)
    dbg = {}
    if cfg.dbg:
        for n, shp in [("d_modv", [2, 2, 6 * D]), ("d_hmid", [NQ, D]), ("d_h1", [NQ, D]), ("d_hmid1", [NQ, D])]:
            dbg[n] = nc.dram_tensor(n, shp, F32, kind="ExternalOutput").ap()
        for n, shp in [("d_KAT", [2, 128, NK]), ("d_KBT", [8, 128, NK]), ("d_KRT", [128, NK]), ("d_VA", [2, NK, 128]), ("d_VB", [8, NK, 128]),
                       ("d_QAT", [8, 128, NQ]), ("d_QBT", [8, 128, NQ]), ("d_QRT", [4, 128, NQ]), ("d_OT", [16, 128, NQ]), ("d_OT1", [16, 128, NQ])]:
            dbg[n] = nc.dram_tensor(n, shp, BF16, kind="ExternalOutput").ap()

    with P.es as es:
        S = Sch(nc, es)
        P.S = S

        with ExitStack() as ps:
            cTt = P.sb(ps, "cT", [128, KC, 2], F32)
            scb = P.sb(ps, "scb", [128, KC, 2], BF16)
            wm = [P.sb(ps, "wm%d" % i, [128, KC, 512], BF16) for i in range(2)]
            bmt = [P.sb(ps, "bm%d" % i, [2, 512], F32) for i in range(2)]
            mo = [P.sb(ps, "mo%d" % i, [2, 512], F32) for i in range(2)]
            pm = [P.pb(ps, "pm%d" % i, [128, 512], F32) for i in range(2)]
            S.dma("sp", cTt[:], cT[:, :, :], writes=[cTt])
            S.op("act", lambda e: e.activation(out=scb[:], in_=cTt[:], func=AF.Silu), reads=[cTt], writes=[scb])
            it = 0
            for l in range(2):
                for cg in range(24):
                    b = it % 2
                    it += 1
                    S.dma("pool", wm[b][:], w_mod[l, :, cg * 512:(cg + 1) * 512].rearrange("(kc p) n -> p kc n", p=128), writes=[wm[b]])
                    S.dma("sp", bmt[b][:], b_mod[l, cg * 512:(cg + 1) * 512].partition_broadcast(2), writes=[bmt[b]])
                    for kc in range(KC):
                        S.op("pe", lambda e, kc=kc, b=b: e.matmul(pm[b][0:2, :], scb[:, kc, :], wm[b][:, kc, :], start=(kc == 0), stop=(kc == KC - 1)), reads=[scb, wm[b]], writes=[pm[b]])
                    S.op("dve", lambda e, b=b: e.tensor_tensor(mo[b][:], pm[b][0:2, :], bmt[b][:], ALU.add), reads=[pm[b], bmt[b]], writes=[mo[b]])
                    S.dma("sp", P.modv[l, :, cg * 512:(cg + 1) * 512], mo[b][:], reads=[mo[b]])
            S.barrier()
        if cfg.dbg:
            S.dma("sp", dbg["d_modv"][:, :, :], P.modv[:, :, :])
            S.barrier()

        def run_rr(gens):
            gens = list(gens)
            while gens:
                for g_ in list(gens):
                    try:
                        next(g_)
                    except StopIteration:
                        gens.remove(g_)

        def run_rr2(main, bg):
            main = list(main)
            while main:
                for g_ in list(main):
                    try:
                        next(g_)
                    except StopIteration:
                        main.remove(g_)
                for g_ in list(bg):
                    try:
                        next(g_)
                    except StopIteration:
                        bg.remove(g_)

        def qkv_phase(layer, nt, src, rope_t, ctx_from, Wdram, col_specs, mod_names_gain):
            with ExitStack() as ps:
                ph0 = Ph()
                P.front_alloc(ps, ph0)
                ph1 = Ph()
                P.front_alloc(ps, ph1, share=ph0)
                phs = [ph0, ph1]
                ident = ph0.ident
                xin = [P.sb(ps, "xin%d" % i, [128, D], F32) for i in range(2)]
                ph0.gtmp = xin[1]
                for which, sfx in ((0, "l"), (1, "c")):
                    P.mod_vecs(ps, ph0, layer, which, [("A" + sfx, "A", 1, mod_names_gain), ("S" + sfx, "S", 0, None)])
                ncols = sum(c["n"] for c in col_specs)
                Wsb = P.sb(ps, "Wsb", [128, KC, ncols], BF16)
                off = 0
                for c in col_specs:
                    c["off"] = off
                    S.dma("pool", Wsb[:, :, off:off + c["n"]], Wdram[:, c["c0"]:c["c0"] + c["n"]].rearrange("(kc p) n -> p kc n", p=128), writes=[Wsb])
                    off += c["n"]
                gains = {}
                gl_all = {"a_q_norm": a_q_norm, "a_k_norm": a_k_norm, "b_q_norm": b_q_norm, "b_kv_norm": b_kv_norm,
                          "c_q_norm": c_q_norm, "c_k_norm": c_k_norm, "d_q_norm": d_q_norm, "d_k_norm": d_k_norm}
                for c in col_specs:
                    gn = c.get("gain")
                    if gn is not None and gn not in gains:
                        n = 512 if gn.startswith("b_") else 128
                        gt = P.sb(ps, gn, [128, n], F32)
                        P.load_bcast("sp", gt, gl_all[gn][:])
                        gains[gn] = gt
                cst = [P.sb(ps, "cs%d" % i, [128, 4, 128], F32) for i in range(2)]
                pq = [P.pb(ps, "pq%d" % i, [128, 512], F32) for i in range(4)]
                pT2 = [P.pb(ps, "pT2%d" % i, [128, 1024], BF16) for i in range(1)]
                pq_extra = P.pb(ps, "pqx", [128, 512], F32)
                ctr = {"pq": 0, "pt": 0}

                def next_pq():
                    ctr["pq"] += 1
                    return pq[ctr["pq"] % 4]

                def next_pt():
                    ctr["pt"] += 1
                    return pT2[0]

                extra = {}
                for c in col_specs:
                    if c["kind"] in ("mla_q", "mla_kv"):
                        extra["cT"] = P.sb(ps, "cnT", [128, 4, 128], BF16)
                        wn = "wuq" if c["kind"] == "mla_q" else "wukv"
                        wsrc = w_uq if c["kind"] == "mla_q" else w_ukv
                        ncol = 1536 if c["kind"] == "mla_q" else 2048
                        extra[wn] = P.sb(ps, wn, [128, 4, ncol], BF16)
                        S.dma("pool", extra[wn][:], wsrc[:, :].rearrange("(kc p) n -> p kc n", p=128), writes=[extra[wn]])
                chains_par = []
                for par_ in range(2):
                  chains = []
                  for ci, c in enumerate(col_specs):
                    for s0 in range(0, c["n"], 512):
                        s1 = min(c["n"], s0 + 512)
                        w = s1 - s0
                        b = Ph()
                        b.c, b.s0, b.s1, b.w = c, s0, s1, w
                        b.pp = pq[len(chains) % 4]
                        b.ss = P.sb(ps, "css", [128, 16], F32)
                        if c["kind"] != "v":
                            b.qf = P.sb(ps, "cqf", [128, max(w, 128)], F32)
                            b.qsq = P.sb(ps, "cqsq", [128, max(w, 128)], F32)
                            b.hT = P.sb(ps, "chT", [128, 512], BF16)
                        b.hb = P.sb(ps, "chb", [128, max(w, 128)], BF16)
                        if c["kind"] == "mla_q":
                            b.qf2 = P.sb(ps, "cqf2", [128, 512], F32)
                            b.qsq2 = P.sb(ps, "cqsq2", [128, 512], F32)
                            b.hb2 = P.sb(ps, "chb2", [128, 512], BF16)
                            b.hT2 = P.sb(ps, "chT2", [128, 512], BF16)
                        if c["kind"] == "mla_kv":
                            b.hb2 = P.sb(ps, "chb2", [128, 512], BF16)
                            b.hT2 = P.sb(ps, "chT2", [128, 512], BF16)
                        chains.append(b)
                  chains_par.append(chains)
                mla_lock = {"held": False}

                def g_norm_rope(b, qf, qsq, n, H, hd, gain, cs, dst, rope, norm):
                    v3 = lambda ap: ap.rearrange("p (h d) -> p h d", d=hd)
                    if norm:
                        S.op("dve", lambda e: e.tensor_tensor(qsq[:, 0:n], qf[:, 0:n], qf[:, 0:n], ALU.mult), reads=[qf], writes=[qsq])
                        yield
                        S.op("dve", lambda e: e.tensor_reduce(b.ss[:, 0:H], v3(qsq[:, 0:n]), AX.X, ALU.add), reads=[qsq], writes=[b.ss])
                        yield
                        S.op("act", lambda e: e.activation(out=b.ss[:, 0:H], in_=b.ss[:, 0:H], func=AF.Sqrt, bias=EPS, scale=1.0 / hd), reads=[b.ss], writes=[b.ss])
                        yield
                        S.op("dve", lambda e: e.reciprocal(b.ss[:, 0:H], b.ss[:, 0:H]), reads=[b.ss], writes=[b.ss])
                        yield
                        S.op("dve", lambda e: e.tensor_tensor(v3(qf[:, 0:n]), v3(qf[:, 0:n]), b.ss[:, 0:H].unsqueeze(2).to_broadcast([128, H, hd]), ALU.mult), reads=[qf, b.ss], writes=[qf])
                        yield
                        S.op("dve", lambda e: e.tensor_tensor(v3(qf[:, 0:n]), v3(qf[:, 0:n]), gain[:, 0:hd].unsqueeze(1).to_broadcast([128, H, hd]), ALU.mult), reads=[qf, gain], writes=[qf])
                        yield
                    if rope:
                        csT, ci_, si_ = cs
                        cosv = csT[:, ci_, :]
                        sinv = csT[:, si_, :]
                        q4 = hd // 4
                        x5 = lambda ap: ap.rearrange("p (h a b d) -> p h a b d", a=2, b=2, d=q4)
                        s4 = lambda ap: ap.rearrange("p (a b d) -> p a b d", a=2, b=2, d=q4)
                        for b0 in range(2):
                            S.op("dve", lambda e, b0=b0: e.tensor_tensor(x5(qsq[:, 0:n])[:, :, :, b0, :], x5(qf[:, 0:n])[:, :, :, 1 - b0, :],
                                                                        s4(sinv[:, 0:hd])[:, :, b0, :].unsqueeze(1).to_broadcast([128, H, 2, q4]), ALU.mult), reads=[qf, csT], writes=[qsq])
                            yield
                        S.op("dve", lambda e: e.tensor_tensor(v3(qf[:, 0:n]), v3(qf[:, 0:n]), cosv[:, 0:hd].unsqueeze(1).to_broadcast([128, H, hd]), ALU.mult), reads=[qf, csT], writes=[qf])
                        yield
                        S.op("dve", lambda e: e.tensor_tensor(dst[:, 0:n], qf[:, 0:n], qsq[:, 0:n], ALU.add), reads=[qf, qsq], writes=[dst])
                        yield
                    else:
                        S.op("dve", lambda e: e.tensor_copy(dst[:, 0:n], qf[:, 0:n]), reads=[qf], writes=[dst])
                        yield

                def g_tr_out(srcb, nblk, dstb, dram_ap):
                    pTt = next_pt()
                    for j in range(nblk):
                        S.op("pe", lambda e, j=j: e.transpose(pTt[:, j * 128:(j + 1) * 128], srcb[:, j * 128:(j + 1) * 128], ident[:]), reads=[srcb, ident], writes=[pTt])
                    S.op("act", lambda e: e.copy(dstb[:, 0:nblk * 128], pTt[:, 0:nblk * 128]), reads=[pTt], writes=[dstb])
                    yield
                    S.dma("pool", dram_ap, dstb[:, 0:nblk * 128].rearrange("p (h t) -> p h t", t=128), reads=[dstb])
                    yield

                def g_chain(t, b, uT, cs):
                    c, s0, s1, w = b.c, b.s0, b.s1, b.w
                    kind = c["kind"]
                    if c.get("own_only") and not (OWN0 <= t < OWN0 + NOWN):
                        return
                    tsl = slice(t * 128, (t + 1) * 128)
                    H = max(w // 128, 1)
                    h0 = s0 // 128
                    pp = b.pp
                    _lin(S, pp, pp[:, 0:w], uT, Wsb, c["off"] + s0, c["off"] + s1)
                    yield
                    if kind == "v":
                        S.op("act", lambda e: e.copy(b.hb[:, 0:w], pp[:, 0:w]), reads=[pp], writes=[b.hb])
                        yield
                        S.dma("pool", c["dst"][h0:h0 + H, tsl, :].rearrange("h p d -> p h d"), b.hb[:, 0:w].rearrange("p (h d) -> p h d", d=128), reads=[b.hb])
                        yield
                        return
                    S.op("act", lambda e: e.copy(b.qf[:, 0:w], pp[:, 0:w]), reads=[pp], writes=[b.qf])
                    yield
                    if kind == "heads":
                        yield from g_norm_rope(b, b.qf, b.qsq, w, H, 128, gains[c["gain"]], (cs, 0, 1), b.hb, c["rope"], True)
                        yield from g_tr_out(b.hb, H, b.hT, c["dst"][h0:h0 + H, :, tsl].rearrange("h p t -> p h t"))
                    elif kind == "kr":
                        yield from g_norm_rope(b, b.qf, b.qsq, 64, 1, 64, None, (cs, 2, 3), b.hb, True, False)
                        S.op("dve", lambda e: e.tensor_copy(b.hb[:, 64:128], b.hb[:, 0:64]), reads=[b.hb], writes=[b.hb])
                        yield
                        yield from g_tr_out(b.hb, 1, b.hT, c["dst"][:, tsl].rearrange("p (h t) -> p h t", h=1))
                    else:
                        yield from g_norm_rope(b, b.qf, b.qsq, 512, 1, 512, gains[c["gain"]], None, b.hb, False, True)
                        cnT = extra["cT"]
                        while mla_lock["held"]:
                            yield
                        mla_lock["held"] = True
                        pTt = next_pt()
                        for j in range(4):
                            S.op("pe", lambda e, j=j: e.transpose(pTt[:, j * 128:(j + 1) * 128], b.hb[:, j * 128:(j + 1) * 128], ident[:]), reads=[b.hb, ident], writes=[pTt])
                        S.op("act", lambda e: e.copy(cnT[:].rearrange("p a b -> p (a b)"), pTt[:, 0:512]), reads=[pTt], writes=[cnT])
                        yield
                        W2 = extra["wuq" if kind == "mla_q" else "wukv"]
                        for hh in range(2):
                            pp2 = pq_extra
                            for h in range(4):
                                hd_ = hh * 4 + h
                                for kc in range(4):
                                    S.op("pe", lambda e, h=h, hd_=hd_, kc=kc: e.matmul(pp2[:, h * 128:(h + 1) * 128], W2[:, kc, hd_ * 128:(hd_ + 1) * 128], cnT[:, kc, :],
                                                                                      start=(kc == 0), stop=(kc == 3)), reads=[W2, cnT], writes=[pp2])
                            yield
                            hT_ = b.hT if hh == 0 else b.hT2
                            S.op("act", lambda e: e.copy(hT_[:, 0:512], pp2[:, 0:512]), reads=[pp2], writes=[hT_])
                            yield
                            S.dma("pool", c["dst"][hh * 4:hh * 4 + 4, :, tsl].rearrange("h p t -> p h t"), hT_[:, 0:512].rearrange("p (h t) -> p h t", t=128), reads=[hT_])
                            yield
                        if kind == "mla_q":
                            pp2 = pq_extra
                            _lin(S, pp2, pp2[:, 0:512], cnT, W2, 1024, 1536, kcn=4)
                            yield
                            S.op("act", lambda e: e.copy(b.qf2[:, 0:512], pp2[:, 0:512]), reads=[pp2], writes=[b.qf2])
                            yield
                            yield from g_norm_rope(b, b.qf2, b.qsq2, 512, 8, 64, None, (cs, 2, 3), b.hb2, True, False)
                            yield from g_tr_out(b.hb2, 4, b.hT, c["dst_r"][:, :, tsl].rearrange("h p t -> p h t"))
                        else:
                            for hh in range(2):
                                pp2 = pq_extra
                                _lin(S, pp2, pp2[:, 0:512], cnT, W2, 1024 + hh * 512, 1024 + (hh + 1) * 512, kcn=4)
                                yield
                                hb_ = b.hb if hh == 0 else b.hb2
                                S.op("act", lambda e: e.copy(hb_[:, 0:512], pp2[:, 0:512]), reads=[pp2], writes=[hb_])
                                yield
                                S.dma("pool", c["dst_v"][hh * 4:hh * 4 + 4, tsl, :].rearrange("h p d -> p h d"), hb_[:, 0:512].rearrange("p (h d) -> p h d", d=128), reads=[hb_])
                                yield

                        mla_lock["held"] = False

                def g_front(t):
                    is_ctx = t >= ctx_from
                    xt = xin[t % 2]
                    ph = phs[t % 2]
                    Avec = ph0.Ac if is_ctx else ph0.Al
                    shvec = ph0.Sc if is_ctx else ph0.Sl
                    S.dma("sp", xt[:], src[t * 128:(t + 1) * 128, :], writes=[xt])
                    S.dma("sp", cst[t % 2][:], rope_t[:, t * 128:(t + 1) * 128, :].rearrange("f p d -> p f d"), writes=[cst[t % 2]])
                    yield
                    S.op("act", lambda e: e.activation(out=ph.junk[:], in_=xt[:], func=AF.Square, accum_out=ph.ss[:, 0:1]), reads=[xt], writes=[ph.junk, ph.ss])
                    yield
                    S.op("act", lambda e: e.activation(out=ph.ss[:, 0:1], in_=ph.ss[:, 0:1], func=AF.Sqrt, bias=EPS, scale=1.0 / D), reads=[ph.ss], writes=[ph.ss])
                    yield
                    S.op("dve", lambda e: e.reciprocal(ph.ss[:, 0:1], ph.ss[:, 0:1]), reads=[ph.ss], writes=[ph.ss])
                    yield
                    S.op("dve", lambda e: e.scalar_tensor_tensor(xt[:], xt[:], ph.ss[:, 0:1], Avec[:], ALU.mult, ALU.mult), reads=[xt, ph.ss, Avec], writes=[xt])
                    yield
                    S.op("dve", lambda e: e.tensor_tensor(ph.ub[:], xt[:], shvec[:], ALU.add), reads=[xt, shvec], writes=[ph.ub])
                    yield
                    for half in range(2):
                        p = ph.pT[half]
                        for j in range(8):
                            kc = half * 8 + j
                            S.op("pe", lambda e, j=j, kc=kc: e.transpose(p[:, j * 128:(j + 1) * 128], ph.ub[:, kc * 128:(kc + 1) * 128], ident[:]), reads=[ph.ub, ident], writes=[p])
                        yield
                        dst = ph.uT[:, half * 8:(half + 1) * 8, :].rearrange("p a b -> p (a b)")
                        if half == 0:
                            S.op("act", lambda e: e.copy(dst, p[:]), reads=[p], writes=[ph.uT])
                        else:
                            S.op("dve", lambda e: e.tensor_copy(dst, p[:]), reads=[p], writes=[ph.uT])
                        yield

                nch = len(chains_par[0])
                done = set()
                steps = {}
                pending = []
                for t in range(nt):
                    pending.append(("F", t, -1))
                    for ci in range(nch):
                        pending.append(("C", t, ci))
                active = []

                def can_start(key):
                    kind, t, ci = key
                    if kind == "F":
                        if t >= 1 and ("F", t - 1, -1) not in done:
                            return False
                        if t >= 2:
                            for cj in range(nch):
                                if ("C", t - 2, cj) not in done:
                                    return False
                        return True
                    if ("F", t, -1) not in done:
                        return False
                    if t >= 2 and ("C", t - 2, ci) not in done:
                        return False
                    if t >= 1 and ("C", t - 1, ci) not in done and steps.get(("C", t - 1, ci), 0) < 3:
                        return False
                    return True

                while pending or active:
                    for key in list(pending):
                        if can_start(key):
                            pending.remove(key)
                            kind, t, ci = key
                            gen = g_front(t) if kind == "F" else g_chain(t, chains_par[t % 2][ci], phs[t % 2].uT, cst[t % 2])
                            active.append((key, gen))
                    for key, gen in list(active):
                        try:
                            next(gen)
                            steps[key] = steps.get(key, 0) + 1
                        except StopIteration:
                            active.remove((key, gen))
                            done.add(key)
                S.barrier()


        qkv_phase(0, NKT, xkv, ropek, NKT - 2, ab_w_in, [
            dict(kind="heads", c0=1024, n=256, gain="a_k_norm", rope=True, dst=KAT),
            dict(kind="v", c0=1280, n=256, dst=VA),
            dict(kind="mla_kv", c0=1536 + 512, n=512, gain="b_kv_norm", dst=KBT, dst_v=VB),
            dict(kind="kr", c0=1536 + 1024, n=64, dst=KRT),
        ], "g_mix_pre")
        qkv_phase(0, NQT, xq, ropeq, NQT - 2, ab_w_in, [
            dict(kind="heads", c0=0, n=1024, gain="a_q_norm", rope=True, dst=QAT),
            dict(kind="mla_q", c0=1536, n=512, gain="b_q_norm", dst=QBT, dst_r=QRT),
        ], "g_mix_pre")
        if cfg.dbg:
            for n_, src_ in [("d_KAT", KAT), ("d_KBT", KBT), ("d_VA", VA), ("d_VB", VB), ("d_QAT", QAT), ("d_QBT", QBT), ("d_QRT", QRT)]:
                S.dma("sp", dbg[n_][:, :, :], src_[:, :, :])
            S.dma("sp", dbg["d_KRT"][:, :], KRT[:, :])
            S.barrier()

        with ExitStack() as ps:
            ones = P.sb(ps, "ones", [128, 128], BF16)
            S.dma("pool", ones[:], ones_in[:, :], writes=[ones])
            Ksb = [P.sb(ps, "Ksb%d" % i, [128, NK], BF16) for i in range(2)]
            Vsb = [P.sb(ps, "Vsb%d" % i, [128, NKT, 128], BF16) for i in range(2)]
            KRsb = P.sb(ps, "KRsb", [128, NK], BF16)
            qsb = [P.sb(ps, "qsb%d" % i, [128, 512], BF16) for i in range(2)]
            qrsb = [P.sb(ps, "qrsb%d" % i, [128, 512], BF16) for i in range(2)]
            pt = [P.sb(ps, "pt%d" % i, [128, 512], BF16) for i in range(4)]
            rden = P.sb(ps, "rden", [128, 512], F32)
            osb = [P.sb(ps, "osb%d" % i, [128, 512], BF16) for i in range(2)]
            pS = [P.pb(ps, "pS%d" % i, [128, 512], F32) for i in range(4)]
            pO = [P.pb(ps, "pO%d" % i, [128, 512], F32) for i in range(2)]
            pD = [P.pb(ps, "pD%d" % i, [128, 512], F32) for i in range(2)]
            S.dma("sp", KRsb[:], KRT[:, :], writes=[KRsb])

            def load_kv(hh):
                kb = hh % 2
                if hh < 2:
                    S.dma("sp", Ksb[kb][:], KAT[hh, :, :], writes=[Ksb[kb]])
                    vsrc = VA[hh]
                else:
                    S.dma("sp", Ksb[kb][:], KBT[hh - 2, :, :], writes=[Ksb[kb]])
                    vsrc = VB[hh - 2]
                for t0_ in range(0, NKT, 11):
                    t1_ = min(NKT, t0_ + 11)
                    S.dma("sp", Vsb[kb][:, t0_:t1_, :], vsrc[t0_ * 128:t1_ * 128, :].rearrange("(t p) d -> p t d", p=128), writes=[Vsb[kb]])

            glist = []
            for hh in range(10):
                if hh < 2:
                    glist += [(hh, qt * 128, 512) for qt in range(NQT)]
                else:
                    glist += [(hh, t0 * 128, min(4, NQT - t0) * 128) for t0 in range(0, NQT, 4)]

            def load_q(gi):
                hh, tok0, N = glist[gi]
                g = gi % 2
                if hh < 2:
                    S.dma("sp", qsb[g][:].rearrange("p (h t) -> p h t", t=128), QAT[hh * 4:hh * 4 + 4, :, tok0:tok0 + 128].rearrange("h p t -> p h t"), writes=[qsb[g]])
                else:
                    h = hh - 2
                    S.dma("sp", qsb[g][:, 0:N], QBT[h, :, tok0:tok0 + N], writes=[qsb[g]])
                    S.dma("sp", qrsb[g][:, 0:N], QRT[h // 2, :, tok0:tok0 + N], writes=[qrsb[g]])

            load_kv(0)
            load_q(0)
            step = 0
            for gi, (hh, tok0, N) in enumerate(glist):
                g = gi % 2
                kb = hh % 2
                first_of_head = gi == 0 or glist[gi - 1][0] != hh
                if first_of_head and hh + 1 < 10:
                    load_kv(hh + 1)
                if gi + 1 < len(glist):
                    load_q(gi + 1)
                qs = qsb[g]
                qr = qrsb[g]
                scale = 128 ** -0.5 if hh < 2 else 192 ** -0.5
                if hh >= 2:
                    h = hh - 2
                    half = slice(0, 64) if h % 2 == 0 else slice(64, 128)
                kts = list(range(NKT - 2, NKT)) if tok0 >= (NQT - 2) * 128 else list(range(NKT))
                nk = len(kts)
                base = step
                step += nk

                def qk(i):
                    sp_ = pS[(base + i) % 4]
                    ksl = slice(kts[i] * 128, (kts[i] + 1) * 128)
                    if hh < 2:
                        S.op("pe", lambda e: e.matmul(sp_[:, 0:N], Ksb[kb][:, ksl], qs[:, 0:N], start=True, stop=True), reads=[Ksb[kb], qs], writes=[sp_])
                    else:
                        S.op("pe", lambda e: e.matmul(sp_[:, 0:N], Ksb[kb][:, ksl], qs[:, 0:N], start=True, stop=False), reads=[Ksb[kb], qs], writes=[sp_])
                        S.op("pe", lambda e: e.matmul(sp_[:, 0:N], KRsb[half, ksl], qr[half, 0:N], start=False, stop=True), reads=[KRsb, qr], writes=[sp_])

                def ex(i):
                    sp_ = pS[(base + i) % 4]
                    ptt = pt[(base + i) % 4]
                    S.op("act", lambda e: e.activation(out=ptt[:, 0:N], in_=sp_[:, 0:N], func=AF.Exp, scale=scale), reads=[sp_], writes=[ptt])

                def pv(i):
                    ptt = pt[(base + i) % 4]
                    S.op("pe", lambda e: e.matmul(pO[g][:, 0:N], Vsb[kb][:, kts[i], :], ptt[:, 0:N], start=(i == 0), stop=(i == nk - 1)), reads=[Vsb[kb], ptt], writes=[pO[g]])
                    S.op("pe", lambda e: e.matmul(pD[g][:, 0:N], ones[:], ptt[:, 0:N], start=(i == 0), stop=(i == nk - 1)), reads=[ones, ptt], writes=[pD[g]])

                qk(0)
                if nk > 1:
                    qk(1)
                for i in range(nk):
                    if i + 2 < nk:
                        qk(i + 2)
                    ex(i)
                    pv(i)
                S.op("dve", lambda e: e.reciprocal(rden[:, 0:N], pD[g][:, 0:N]), reads=[pD[g]], writes=[rden])
                ob = osb[g]
                S.op("dve", lambda e: e.tensor_tensor(ob[:, 0:N], pO[g][:, 0:N], rden[:, 0:N], ALU.mult), reads=[pO[g], rden], writes=[ob])
                if hh < 2:
                    S.dma("pool", OT[hh * 4:hh * 4 + 4, :, tok0:tok0 + 128].rearrange("h p t -> p h t"), ob[:].rearrange("p (h t) -> p h t", t=128), reads=[ob])
                else:
                    S.dma("pool", OT[8 + h, :, tok0:tok0 + N], ob[:, 0:N], reads=[ob])
            S.barrier()
        if cfg.dbg:
            S.dma("sp", dbg["d_OT"][:, :, :], OT[:, :, :])
            S.barrier()

        def out_phase(layer, tiles, OTd, Wout, resid, dst):
            with ExitStack() as ps:
                ph = Ph()
                ph.gtmp = P.sb(ps, "gtmp", [128, D], F32)
                P.mod_vecs(ps, ph, layer, 0, [("Gl", "G", 2, "g_mix_post")])
                if layer == 0:
                    P.mod_vecs(ps, ph, layer, 1, [("Gc", "G", 2, "g_mix_post")])
                Wsb = P.sb(ps, "Wo", [128, KC, D], BF16)
                S.dma("pool", Wsb[:], Wout[:, :].rearrange("(kc p) n -> p kc n", p=128), writes=[Wsb])
                oT = [P.sb(ps, "oT%d" % i, [128, 16, 128], BF16) for i in range(2)]
                xr = [P.sb(ps, "xr%d" % i, [128, D], F32) for i in range(2)]
                of = P.sb(ps, "of", [128, D], F32)
                junk = P.sb(ps, "junk", [128, D], BF16)
                ss = P.sb(ps, "ss", [128, 16], F32)
                pp = [P.pb(ps, "po%d" % i, [128, 512], F32) for i in range(8)]
                for i, t in enumerate(tiles):
                    is_ctx = (layer == 0) and t >= NQT - 2
                    o_ = oT[i % 2]
                    x_ = xr[i % 2]
                    tsl = slice(t * 128, (t + 1) * 128)
                    S.dma("sp", o_[:], OTd[:, :, tsl].rearrange("h p t -> p h t"), writes=[o_])
                    S.dma("sp", x_[:], resid[tsl, :], writes=[x_])
                    for cg in range(4):
                        pb_ = pp[(i % 2) * 4 + cg]
                        _lin(S, pb_, pb_[:, :], o_, Wsb, cg * 512, (cg + 1) * 512)
                        if cg % 2 == 0:
                            S.op("act", lambda e, pb_=pb_, cg=cg: e.copy(of[:, cg * 512:(cg + 1) * 512], pb_[:, :]), reads=[pb_], writes=[of])
                        else:
                            S.op("dve", lambda e, pb_=pb_, cg=cg: e.tensor_copy(of[:, cg * 512:(cg + 1) * 512], pb_[:, :]), reads=[pb_], writes=[of])
                    S.op("act", lambda e: e.activation(out=junk[:], in_=of[:], func=AF.Square, accum_out=ss[:, 0:1]), reads=[of], writes=[junk, ss])
                    P.rstd_from_ss(ss, slice(0, 1), D)
                    G = ph.Gc if is_ctx else ph.Gl
                    S.op("dve", lambda e, G=G: e.scalar_tensor_tensor(of[:], of[:], ss[:, 0:1], G[:], ALU.mult, ALU.mult), reads=[of, ss, G], writes=[of])
                    S.op("dve", lambda e, x_=x_: e.tensor_tensor(x_[:], of[:], x_[:], ALU.add), reads=[of, x_], writes=[x_])
                    S.dma("pool", dst[tsl, :], x_[:], reads=[x_])
                S.barrier()

        def ffn_phase(layer, tiles, src, dst_fn):
            JB = 2
            NB = FC // JB
            with ExitStack() as ps:
                ph = Ph()
                P.front_alloc(ps, ph)
                hx = [P.sb(ps, "hx%d" % i, [128, D], F32) for i in range(2)]
                hx2 = [P.sb(ps, "hxp%d" % i, [128, D], F32) for i in range(2)]
                ph.gtmp = hx[0]
                MV = [("A", "A", 4, "g_ffn_pre"), ("Sh", "S", 3, None), ("G", "G", 5, "g_ffn_post")]
                P.mod_vecs(ps, ph, layer, 0, MV)

                def switch_ctx(attrs):
                    for attr, kind, k, gname in MV:
                        if attr not in attrs:
                            continue
                        t_ = getattr(ph, attr)
                        P.load_bcast("sp", t_, P.modv[layer, 1, k * D:(k + 1) * D])
                        if kind != "S":
                            g_ = ph.gtmp
                            P.load_bcast("sp", g_, P.din[gname][layer, :])
                            if kind == "A":
                                S.op("dve", lambda e, t_=t_, g_=g_: e.scalar_tensor_tensor(t_[:], t_[:], 1.0, g_[:], ALU.add, ALU.mult), reads=[t_, g_], writes=[t_])
                            else:
                                S.op("dve", lambda e, t_=t_, g_=g_: e.tensor_tensor(t_[:], t_[:], g_[:], ALU.mult), reads=[t_, g_], writes=[t_])

                ph.junk2 = ph.uT
                ph.ss2 = P.sb(ps, "ss2", [128, 16], F32)
                facc = [P.sb(ps, "facc%d" % i, [128, D], F32) for i in range(4)]
                u2T = [P.sb(ps, "u2T%d" % i, [128, KC, 512], BF16) for i in range(2)]
                wg = [P.sb(ps, "wg%d" % i, [128, KC, JB * 128], BF16) for i in range(2)]
                wu = [P.sb(ps, "wu%d" % i, [128, KC, JB * 128], BF16) for i in range(2)]
                wd = [P.sb(ps, "wd%d" % i, [128, JB, D], BF16) for i in range(2)]
                aT = [P.sb(ps, "aT%d" % i, [128, JB, 512], BF16) for i in range(2)]
                sg = [P.sb(ps, "sg%d" % i, [128, 512], F32) for i in range(2)]
                pg = [P.pb(ps, "pg%d" % i, [128, 512], F32) for i in range(2)]
                pu = [P.pb(ps, "pu%d" % i, [128, 512], F32) for i in range(2)]
                pd = [P.pb(ps, "pd%d" % i, [128, 512], F32) for i in range(2)]
                groups = []
                i = 0
                while i < len(tiles):
                    ctxf = (layer == 0) and tiles[i] >= NQT - 2
                    j = i
                    while j < len(tiles) and j - i < 4 and (((layer == 0) and tiles[j] >= NQT - 2) == ctxf):
                        j += 1
                    groups.append((tiles[i:j], ctxf))
                    i = j
                state = {"pro_ctx": False, "epi_ctx": False, "hxi": 0}

                def prologue(gidx):
                    gt, ctxf = groups[gidx]
                    if ctxf and not state["pro_ctx"]:
                        state["pro_ctx"] = True
                        switch_ctx(("A", "Sh"))
                    for ti, t in enumerate(gt):
                        x_ = hx2[ti % 2]
                        S.dma("sp", x_[:], src[t * 128:(t + 1) * 128, :], writes=[x_])
                        yield
                        S.op("act", lambda e: e.activation(out=ph.junk[:], in_=x_[:], func=AF.Square, accum_out=ph.ss[:, 0:1]), reads=[x_], writes=[ph.junk, ph.ss])
                        yield
                        S.op("act", lambda e: e.activation(out=ph.ss[:, 0:1], in_=ph.ss[:, 0:1], func=AF.Sqrt, bias=EPS, scale=1.0 / D), reads=[ph.ss], writes=[ph.ss])
                        yield
                        S.op("dve", lambda e: e.reciprocal(ph.ss[:, 0:1], ph.ss[:, 0:1]), reads=[ph.ss], writes=[ph.ss])
                        yield
                        S.op("dve", lambda e: e.scalar_tensor_tensor(x_[:], x_[:], ph.ss[:, 0:1], ph.A[:], ALU.mult, ALU.mult), reads=[x_, ph.ss, ph.A], writes=[x_])
                        yield
                        S.op("dve", lambda e: e.tensor_tensor(ph.ub[:], x_[:], ph.Sh[:], ALU.add), reads=[x_, ph.Sh], writes=[ph.ub])
                        for _ in range(5):
                            yield
                        for half in range(2):
                            p = ph.pT[half]
                            for j in range(8):
                                kc = half * 8 + j
                                S.op("pe", lambda e, j=j, kc=kc: e.transpose(p[:, j * 128:(j + 1) * 128], ph.ub[:, kc * 128:(kc + 1) * 128], ph.ident[:]), reads=[ph.ub, ph.ident], writes=[p])
                            yield
                            yield
                            dst = u2T[gidx % 2][:, half * 8:(half + 1) * 8, ti * 128:(ti + 1) * 128]
                            if half == 0:
                                S.op("act", lambda e: e.copy(dst, p[:].rearrange("p (a b) -> p a b", b=128)), reads=[p], writes=[u2T[gidx % 2]])
                            else:
                                S.op("dve", lambda e: e.tensor_copy(dst, p[:].rearrange("p (a b) -> p a b", b=128)), reads=[p], writes=[u2T[gidx % 2]])
                            yield

                blk = 0
                gstep = 0
                dstep = 0
                wfifo = []
                bg = []

                def wload(bi):
                    nonlocal blk
                    b = blk % 2
                    blk += 1
                    wfifo.append(b)
                    c0 = bi * JB * 128
                    S.dma("pool", wg[b][:], w_gate[layer, :, c0:c0 + JB * 128].rearrange("(kc p) n -> p kc n", p=128), writes=[wg[b]])
                    S.dma("pool", wu[b][:], w_up[layer, :, c0:c0 + JB * 128].rearrange("(kc p) n -> p kc n", p=128), writes=[wu[b]])
                    S.dma("pool", wd[b][:], w_down[layer, c0:c0 + JB * 128, :].rearrange("(j p) n -> p j n", p=128), writes=[wd[b]])

                run_rr([prologue(0)])
                for gidx, (gt, ctxf) in enumerate(groups):
                    G = len(gt) * 128
                    uT_ = u2T[gidx % 2]
                    bpar = {}

                    def gu(bi):
                        nonlocal gstep, dstep
                        if not wfifo:
                            wload(bi)
                        b = wfifo.pop(0)
                        bpar[bi] = b
                        for jj in range(JB):
                            pgt = pg[gstep % 2]
                            sgt = sg[gstep % 2]
                            put = pu[gstep % 2]
                            gstep += 1
                            for kc in range(KC):
                                S.op("pe", lambda e, kc=kc: e.matmul(pgt[:, 0:G], wg[b][:, kc, jj * 128:(jj + 1) * 128], uT_[:, kc, 0:G], start=(kc == 0), stop=(kc == KC - 1)), reads=[wg[b], uT_], writes=[pgt])
                                if kc % 4 == 3:
                                    yield
                            S.op("act", lambda e: e.activation(out=sgt[:, 0:G], in_=pgt[:, 0:G], func=AF.Silu), reads=[pgt], writes=[sgt])
                            for kc in range(KC):
                                S.op("pe", lambda e, kc=kc: e.matmul(put[:, 0:G], wu[b][:, kc, jj * 128:(jj + 1) * 128], uT_[:, kc, 0:G], start=(kc == 0), stop=(kc == KC - 1)), reads=[wu[b], uT_], writes=[put])
                                if kc % 4 == 3 and kc != KC - 1:
                                    yield
                            S.op("dve", lambda e: e.tensor_tensor(aT[b][:, jj, 0:G], put[:, 0:G], sgt[:, 0:G], ALU.mult), reads=[put, sgt], writes=[aT[b]])
                            yield

                    def down(bi):
                        nonlocal dstep
                        b = bpar[bi]
                        for ti in range(len(gt)):
                            for cg in range(4):
                                pdt = pd[dstep % 2]
                                dstep += 1
                                for jj in range(JB):
                                    S.op("pe", lambda e, jj=jj: e.matmul(pdt[:, :], aT[b][:, jj, ti * 128:(ti + 1) * 128], wd[b][:, jj, cg * 512:(cg + 1) * 512], start=(jj == 0), stop=(jj == JB - 1)), reads=[aT[b], wd[b]], writes=[pdt])
                                fa = facc[ti]
                                if bi == 0:
                                    S.op("act", lambda e: e.copy(fa[:, cg * 512:(cg + 1) * 512], pdt[:, :]), reads=[pdt], writes=[fa])
                                else:
                                    S.op("dve", lambda e: e.tensor_tensor(fa[:, cg * 512:(cg + 1) * 512], pdt[:, :], fa[:, cg * 512:(cg + 1) * 512], ALU.add), reads=[pdt, fa], writes=[fa])
                                yield

                    run_rr2([gu(0)], bg)
                    for bi in range(NB):
                        gens = []
                        if bi + 1 < NB:
                            gens.append(gu(bi + 1))
                        elif gidx + 1 < len(groups):
                            wload(0)
                        gens.append(down(bi))
                        run_rr2(gens, bg)
                        if bi == 3 and gidx + 1 < len(groups):
                            bg.append(prologue(gidx + 1))
                    run_rr(bg)
                    del bg[:]
                    if ctxf and not state["epi_ctx"]:
                        state["epi_ctx"] = True
                        switch_ctx(("G",))
                    for ti, t in enumerate(gt[:2]):
                        S.dma("sp", hx[ti % 2][:], src[t * 128:(t + 1) * 128, :], writes=[hx[ti % 2]])
                    for ti, t in enumerate(gt):
                        fa = facc[ti]
                        x_ = hx[ti % 2]
                        S.op("act", lambda e: e.activation(out=ph.junk2[:].rearrange("p a b -> p (a b)"), in_=fa[:], func=AF.Square, accum_out=ph.ss2[:, 0:1]), reads=[fa], writes=[ph.junk2, ph.ss2])
                        P.rstd_from_ss(ph.ss2, slice(0, 1), D)
                        S.op("dve", lambda e: e.scalar_tensor_tensor(fa[:], fa[:], ph.ss2[:, 0:1], ph.G[:], ALU.mult, ALU.mult), reads=[fa, ph.ss2, ph.G], writes=[fa])
                        S.op("dve", lambda e: e.tensor_tensor(x_[:], fa[:], x_[:], ALU.add), reads=[fa, x_], writes=[x_])
                        S.dma("sp", dst_fn(t), x_[:], reads=[x_])
                        if ti + 2 < len(gt):
                            t2 = gt[ti + 2]
                            S.dma("sp", x_[:], src[t2 * 128:(t2 + 1) * 128, :], writes=[x_])
                S.barrier()


        out_phase(0, list(range(NQT)), OT, ab_w_out, xq, hmid)
        if cfg.dbg:
            S.dma("sp", dbg["d_hmid"][:, :], hmid[:, :])
            S.barrier()
        ffn_phase(0, list(range(NQT)), hmid, lambda t: h1[t * 128:(t + 1) * 128, :])
        if cfg.dbg:
            S.dma("sp", dbg["d_h1"][:, :], h1[:, :])
            S.barrier()

        qkv_phase(1, NQT, h1, ropeq, NQT - 2, cd_w_in, [
            dict(kind="heads", c0=0, n=1024, gain="c_q_norm", rope=True, dst=QCT, own_only=True),
            dict(kind="heads", c0=1024, n=256, gain="c_k_norm", rope=True, dst=KCT),
            dict(kind="v", c0=1280, n=256, dst=VC),
        ], "g_mix_pre")
        qkv_phase(1, NQT, h1, ropeq, NQT - 2, cd_w_in, [
            dict(kind="heads", c0=1536, n=1024, gain="d_q_norm", rope=False, dst=QDT, own_only=True),
            dict(kind="heads", c0=2560, n=1024, gain="d_k_norm", rope=False, dst=KDT),
        ], "g_mix_pre")
        qkv_phase(1, NQT, h1, ropeq, NQT - 2, cd_w_in, [
            dict(kind="v", c0=3584, n=1024, dst=VD),
        ], "g_mix_pre")

        with ExitStack() as ps:
            ones = P.sb(ps, "ones", [128, 128], BF16)
            S.dma("pool", ones[:], ones_in[:, :], writes=[ones])
            KCs = P.sb(ps, "KCs", [128, 2, NQ], BF16)
            VCs = P.sb(ps, "VCs", [128, 2, NQT, 128], BF16)
            KDs = P.sb(ps, "KDs", [128, 8, NQ], BF16)
            VDs = P.sb(ps, "VDs", [128, 8, NQT, 128], BF16)
            S.dma("sp", KCs[:], KCT[:, :, :].rearrange("h p t -> p h t"), writes=[KCs])
            S.dma("sp", KDs[:], KDT[:, :, :].rearrange("h p t -> p h t"), writes=[KDs])
            for g in range(2):
                S.dma("sp", VCs[:, g, :, :], VC[g, :, :].rearrange("(t p) d -> p t d", p=128), writes=[VCs])
            for h in range(8):
                for t0_ in range(0, NQT, 11):
                    t1_ = min(NQT, t0_ + 11)
                    S.dma("sp", VDs[:, h, t0_:t1_, :], VD[h, t0_ * 128:t1_ * 128, :].rearrange("(t p) d -> p t d", p=128), writes=[VDs])
            esink = P.sb(ps, "esink", [128, 8], F32)
            P.load_bcast("sp", esink, c_sink[:])
            S.op("act", lambda e: e.activation(out=esink[:], in_=esink[:], func=AF.Exp), reads=[esink], writes=[esink])
            mC = P.sb(ps, "mC", [128, 3, 3, 128], F32)
            S.dma("sp", mC[:].rearrange("p v s q -> p (v s) q"), maskC[:, :, :, :].rearrange("v s k q -> k (v s) q"), writes=[mC])
            bD = [P.sb(ps, "bD%d" % i, [128, 6, 128], F32) for i in range(3)]
            qsb = [P.sb(ps, "qsb%d" % i, [128, 512], BF16) for i in range(3)]
            pt = [P.sb(ps, "pt%d" % i, [128, 512], BF16) for i in range(4)]
            sb_ = [P.sb(ps, "sbias%d" % i, [128, 512], F32) for i in range(4)]
            rden = [P.sb(ps, "rden%d" % i, [128, 512], F32) for i in range(2)]
            osb = [P.sb(ps, "osb%d" % i, [128, 512], BF16) for i in range(2)]
            pS = [P.pb(ps, "pS%d" % i, [128, 512], F32) for i in range(4)]
            pO = [P.pb(ps, "pO%d" % i, [128, 512], F32) for i in range(2)]
            pD = [P.pb(ps, "pD%d" % i, [128, 512], F32) for i in range(2)]
            scale = 128 ** -0.5
            glist = []
            for i in range(NOWN):
                glist += [("C", i, g) for g in range(2)]
                glist += [("D", i, h) for h in range(8)]

            def ginfo(gi):
                kind, i, hg = glist[gi]
                e_ = OWN0 + i
                if kind == "C":
                    keys = [(NQT - 2, None), (NQT - 1, None), (e_ - 1, 0), (e_, 1), (e_ + 1, 2)]
                    return kind, i, hg, e_, keys, 512
                lo = -3 if i == NOWN - 1 else -2
                hi = 3 if i == 0 else 2
                rel = list(range(lo, hi + 1))
                keys = [(NQT - 2, None), (NQT - 1, None)] + [(e_ + r_, si) for si, r_ in enumerate(rel)]
                return kind, i, hg, e_, keys, 128

            def load_q(gi):
                kind, i, hg, e_, keys, N = ginfo(gi)
                tsl = slice(e_ * 128, (e_ + 1) * 128)
                qs = qsb[gi % 3]
                if kind == "C":
                    S.dma("sp", qs[:].rearrange("p (h t) -> p h t", t=128), QCT[hg * 4:hg * 4 + 4, :, tsl].rearrange("h p t -> p h t"), writes=[qs])
                else:
                    vD = 0 if i == 0 else (1 if i == 1 else (3 if i == NOWN - 2 else (4 if i == NOWN - 1 else 2)))
                    bt = bD[gi % 3]
                    nsl = len(keys) - 2
                    S.dma("sp", qs[:, 0:128], QDT[hg, :, tsl], writes=[qs])
                    S.dma("sp", bt[:, 0:nsl, :], biasD[vD, hg, 0:nsl, :, :].rearrange("s k q -> k s q"), writes=[bt])

            load_q(0)
            load_q(1)
            step = 0
            for gi in range(len(glist)):
                kind, i, hg, e_, keys, N = ginfo(gi)
                if gi + 2 < len(glist):
                    load_q(gi + 2)
                tsl = slice(e_ * 128, (e_ + 1) * 128)
                gg = gi % 2
                qs = qsb[gi % 3]
                bt = bD[gi % 3]
                vC = 0 if i == 0 else (2 if i == NOWN - 1 else 1)
                nk = len(keys)
                base = step
                step += nk

                def qk(j):
                    kt, slot = keys[j]
                    sp_ = pS[(base + j) % 4]
                    ksl = slice(kt * 128, (kt + 1) * 128)
                    if kind == "C":
                        S.op("pe", lambda e: e.matmul(sp_[:, 0:N], KCs[:, hg, ksl], qs[:, 0:N], start=True, stop=True), reads=[KCs, qs], writes=[sp_])
                    else:
                        S.op("pe", lambda e: e.matmul(sp_[:, 0:N], KDs[:, hg, ksl], qs[:, 0:N], start=True, stop=True), reads=[KDs, qs], writes=[sp_])

                def ex(j):
                    kt, slot = keys[j]
                    sp_ = pS[(base + j) % 4]
                    ptt = pt[(base + j) % 4]
                    if slot is None:
                        S.op("act", lambda e: e.activation(out=ptt[:, 0:N], in_=sp_[:, 0:N], func=AF.Exp, scale=scale), reads=[sp_], writes=[ptt])
                        return
                    sbt = sb_[(base + j) % 4]
                    if kind == "C":
                        S.op("dve", lambda e: e.scalar_tensor_tensor(sbt[:].rearrange("p (h t) -> p h t", t=128), sp_[:, :].rearrange("p (h t) -> p h t", t=128), scale,
                                                                      mC[:, vC, slot, :].unsqueeze(1).to_broadcast([128, 4, 128]), ALU.mult, ALU.add), reads=[sp_, mC], writes=[sbt])
                    else:
                        S.op("dve", lambda e: e.scalar_tensor_tensor(sbt[:, 0:128], sp_[:, 0:128], scale, bt[:, slot, :], ALU.mult, ALU.add), reads=[sp_, bt], writes=[sbt])
                    S.op("act", lambda e: e.activation(out=ptt[:, 0:N], in_=sbt[:, 0:N], func=AF.Exp), reads=[sbt], writes=[ptt])

                def pv(j):
                    kt, slot = keys[j]
                    ptt = pt[(base + j) % 4]
                    vv = VCs[:, hg, kt, :] if kind == "C" else VDs[:, hg, kt, :]
                    S.op("pe", lambda e: e.matmul(pO[gg][:, 0:N], vv, ptt[:, 0:N], start=(j == 0), stop=(j == nk - 1)), reads=[VCs if kind == "C" else VDs, ptt], writes=[pO[gg]])
                    S.op("pe", lambda e: e.matmul(pD[gg][:, 0:N], ones[:], ptt[:, 0:N], start=(j == 0), stop=(j == nk - 1)), reads=[ones, ptt], writes=[pD[gg]])

                qk(0)
                if nk > 1:
                    qk(1)
                for j in range(nk):
                    if j + 2 < nk:
                        qk(j + 2)
                    ex(j)
                    pv(j)
                rd = rden[gg]
                ob = osb[gg]
                if kind == "C":
                    S.op("dve", lambda e: e.tensor_tensor(rd[:].rearrange("p (h t) -> p h t", t=128), pD[gg][:, :].rearrange("p (h t) -> p h t", t=128),
                                                          esink[:, hg * 4:hg * 4 + 4].unsqueeze(2).to_broadcast([128, 4, 128]), ALU.add), reads=[pD[gg], esink], writes=[rd])
                    S.op("dve", lambda e: e.reciprocal(rd[:], rd[:]), reads=[rd], writes=[rd])
                    S.op("dve", lambda e: e.tensor_tensor(ob[:], pO[gg][:, :], rd[:], ALU.mult), reads=[pO[gg], rd], writes=[ob])
                    S.dma("pool", OT1[hg * 4:hg * 4 + 4, :, tsl].rearrange("h p t -> p h t"), ob[:].rearrange("p (h t) -> p h t", t=128), reads=[ob])
                else:
                    S.op("dve", lambda e: e.reciprocal(rd[:, 0:128], pD[gg][:, 0:128]), reads=[pD[gg]], writes=[rd])
                    S.op("dve", lambda e: e.tensor_tensor(ob[:, 0:128], pO[gg][:, 0:128], rd[:, 0:128], ALU.mult), reads=[pO[gg], rd], writes=[ob])
                    S.dma("pool", OT1[8 + hg, :, tsl], ob[:, 0:128], reads=[ob])
            S.barrier()
        if cfg.dbg:
            S.dma("sp", dbg["d_OT1"][:, :, :], OT1[:, :, :])
            S.barrier()


        own = list(range(OWN0, OWN0 + NOWN))
        out_phase(1, own, OT1, cd_w_out, h1, hmid1)
        if cfg.dbg:
            S.dma("sp", dbg["d_hmid1"][:, :], hmid1[:, :])
            S.barrier()
        ffn_phase(1, own, hmid1, lambda t: yout[(t - OWN0) * 128:(t - OWN0 + 1) * 128, :])
        S.barrier()
        P.nins = S.nins
    return P


GRID_W = 64
L_SEQ = 8192


def _rope_tables(tok, dim):
    tok = np.asarray(tok)
    valid = tok >= 0
    t = np.where(valid, tok, 0)
    row = (t // GRID_W).astype(np.float32)
    col = (t % GRID_W).astype(np.float32)
    half = dim // 2
    inv = (np.float32(10000.0) ** (-np.arange(0, half, 2, dtype=np.float32) / np.float32(half))).astype(np.float32)
    ar = row[:, None] * inv[None, :]
    ac = col[:, None] * inv[None, :]
    ang = np.concatenate([ar, ar, ac, ac], axis=-1).astype(np.float32)
    cos = np.cos(ang).astype(np.float32)
    sin = np.sin(ang).astype(np.float32)
    q = dim // 4
    sgn = np.concatenate([-np.ones(q), np.ones(q), -np.ones(q), np.ones(q)]).astype(np.float32)
    sin = sin * sgn[None, :]
    cos[~valid] = 1.0
    sin[~valid] = 0.0
    return cos, sin


def _rope_pack(tok):
    n = len(tok)
    out = np.zeros((4, n, 128), np.float32)
    c, s = _rope_tables(tok, 128)
    out[0], out[1] = c, s
    c, s = _rope_tables(tok, 64)
    out[2, :, :64], out[3, :, :64] = c, s
    return out


def _mask_c(cc, nown=16):
    kk = np.arange(128)[:, None]
    qq = np.arange(128)[None, :]
    m = np.zeros((3, 3, 128, 128), np.float32)
    for v in range(3):
        m[v, 0] = np.where(kk >= qq, 0.0, NEG)
        m[v, 2] = np.where(kk <= qq, 0.0, NEG)
    if cc == 0:
        m[0, 0] = NEG
    if cc == 3:
        m[2, 2] = NEG
    return m


def _bias_d(cc, rpb, nown=16):
    out = np.full((5, 8, 6, 128, 128), NEG, np.float32)
    rows = L_SEQ // GRID_W
    var_tiles = [0, 1, 2, nown - 2, nown - 1]
    for v, i in enumerate(var_tiles):
        lo = -3 if i == nown - 1 else -2
        hi = 3 if i == 0 else 2
        qtok = 2048 * cc + 128 * i + np.arange(128)
        qr = qtok // GRID_W
        qc = qtok % GRID_W
        rs = np.clip(qr - 4, 0, rows - 8)
        cs = np.clip(qc - 8, 0, GRID_W - 16)
        for si, r_ in enumerate(range(lo, hi + 1)):
            ktok = 2048 * cc + 128 * (i + r_) + np.arange(128)
            inb = (ktok >= 0) & (ktok < L_SEQ)
            kr = np.where(inb, ktok, 0) // GRID_W
            kc = np.where(inb, ktok, 0) % GRID_W
            ok = inb[:, None] & (kr[:, None] >= rs[None, :]) & (kr[:, None] < rs[None, :] + 8) & (kc[:, None] >= cs[None, :]) & (kc[:, None] < cs[None, :] + 16)
            di = np.clip(kr[:, None] - qr[None, :] + 7, 0, 14)
            dj = np.clip(kc[:, None] - qc[None, :] + 15, 0, 30)
            g = rpb[:, di, dj]
            out[v, :, si] = np.where(ok[None], g, NEG)
    return out


def _host_inputs(inputs, cfg=None):
    f = lambda a: np.ascontiguousarray(np.asarray(a, dtype=np.float32))
    x = f(inputs["x"])
    ctx = f(inputs["ctx"])
    c = f(inputs["c"])
    c_ctx = f(inputs["c_ctx"])
    shared = {
        "ident": np.eye(128, dtype=np.float32), "ones": np.ones((128, 128), np.float32),
        "w_mod": f(inputs["w_mod"]), "b_mod": f(inputs["b_mod"]),
        "g_mix_pre": f(inputs["g_mix_pre"]), "g_mix_post": f(inputs["g_mix_post"]),
        "g_ffn_pre": f(inputs["g_ffn_pre"]), "g_ffn_post": f(inputs["g_ffn_post"]),
        "w_gate": f(inputs["w_gate"]), "w_up": f(inputs["w_up"]), "w_down": f(inputs["w_down"]),
        "ab_w_in": f(inputs["ab_w_in"][0]), "ab_w_out": f(inputs["ab_w_out"][0]),
        "a_q_norm": f(inputs["a_q_norm"][0]), "a_k_norm": f(inputs["a_k_norm"][0]),
        "b_q_norm": f(inputs["b_q_norm"][0]), "b_kv_norm": f(inputs["b_kv_norm"][0]),
        "cd_w_in": f(inputs["cd_w_in"][0]), "cd_w_out": f(inputs["cd_w_out"][0]),
        "c_q_norm": f(inputs["c_q_norm"][0]), "c_k_norm": f(inputs["c_k_norm"][0]), "c_sink": f(inputs["c_sink"][0]),
        "d_q_norm": f(inputs["d_q_norm"][0]), "d_k_norm": f(inputs["d_k_norm"][0]),
    }
    wuq = f(inputs["b_w_uq"][0]).reshape(512, 8, 192)
    shared["w_uq"] = np.ascontiguousarray(np.concatenate([wuq[:, :, :128].reshape(512, 1024), wuq[:, :, 128:].reshape(512, 512)], axis=1))
    wukv = f(inputs["b_w_ukv"][0]).reshape(512, 8, 256)
    shared["w_ukv"] = np.ascontiguousarray(np.concatenate([wukv[:, :, :128].reshape(512, 1024), wukv[:, :, 128:].reshape(512, 1024)], axis=1))
    rpb = f(inputs["d_rpb"][0])
    maps = []
    ktok = np.concatenate([np.arange(L_SEQ), -np.ones(256, np.int64)])
    ropek = _rope_pack(ktok)
    for core in range(8):
        b, cc = core // 4, core % 4
        m = dict(shared)
        tok = np.arange(2048 * cc - 256, 2048 * cc + 2048 + 256)
        inb = (tok >= 0) & (tok < L_SEQ)
        tokc = np.clip(tok, 0, L_SEQ - 1)
        m["xq"] = np.ascontiguousarray(np.concatenate([x[b][tokc], ctx[b]], axis=0))
        m["xkv"] = np.ascontiguousarray(np.concatenate([x[b], ctx[b]], axis=0))
        m["cT"] = np.ascontiguousarray(np.stack([c[b].reshape(KC, 128).T, c_ctx.reshape(KC, 128).T], axis=-1))
        m["ropeq"] = _rope_pack(np.concatenate([np.where(inb, tok, -1), -np.ones(256, np.int64)]))
        m["ropek"] = ropek
        m["maskC"] = _mask_c(cc)
        m["biasD"] = _bias_d(cc, rpb)
        maps.append(m)
    return maps


_CACHE = {}


def kernel(**inputs):
    cfg = Cfg()
    if "prog" not in _CACHE:
        _CACHE["prog"] = build(cfg)
    P = _CACHE["prog"]
    maps = _host_inputs(inputs)
    res = run_bass_kernel_spmd(P.nc, maps, core_ids=list(range(8)))
    out = np.zeros((2, L_SEQ, D), np.float32)
    for core in range(8):
        b, cc = core // 4, core % 4
        out[b, 2048 * cc:2048 * cc + 2048] = np.asarray(res.results[core]["y"], dtype=np.float32)
    return out
```
